# Optimizing a Trainium2 kernel written in Bass

```python
import jax
import jax.numpy as jnp
from jax import lax
import numpy as np

D_MODEL = 1024
BATCH = 4
SEQ = 4096
DEPTH = 2

GRID_W = 64
CTX_LEN = 256
EPS = 1e-6
N_MOD = 9
D_FF = 2816

CONV_A_W = 256
CONV_A_K = 3
SGU_W = 256
SGU_GROUPS = 4
SGU_GROUP_W = SGU_W // SGU_GROUPS
CHUNK = 128
DA_HEADS = 4
DA_HEAD_DIM = 64
DA_W = DA_HEADS * 2 * DA_HEAD_DIM
ROPE_BASE = 10000.0
ROPE_NF = DA_HEAD_DIM // 4
Q_BLOCK = 128
CONF_W = 256
CONF_K = 31
N_BRANCH = 4

A_OFF = 0
B_OFF = A_OFF + 3 * CONV_A_W
Q_OFF = B_OFF + 2 * SGU_W
K_OFF = Q_OFF + DA_W
V_OFF = K_OFF + DA_W
D_OFF = V_OFF + DA_W
G_OFF = D_OFF + 2 * CONF_W
IN_COLS = G_OFF + N_BRANCH * D_MODEL

kernel_name = "hybrid_gated_mixers_dit_block"


def rmsnorm(x, g):
    xf = x.astype(jnp.float32)
    y = xf * lax.rsqrt(jnp.mean(xf * xf, axis=-1, keepdims=True) + EPS)
    return (y * g.astype(jnp.float32)).astype(x.dtype)


def layernorm_plain(x):
    xf = x.astype(jnp.float32)
    mu = jnp.mean(xf, axis=-1, keepdims=True)
    var = jnp.mean(jnp.square(xf - mu), axis=-1, keepdims=True)
    return ((xf - mu) * lax.rsqrt(var + EPS)).astype(x.dtype)


def layernorm_affine(x, g, b):
    return (layernorm_plain(x) * g + b).astype(x.dtype)


def modulated_norm(x, g, shift, scale):
    return rmsnorm(x, g) * (1 + scale) + shift


def swiglu(h, w1, w3, w2):
    return (jax.nn.silu(h @ w1) * (h @ w3)) @ w2


def dwconv_centred(x, w):
    k = w.shape[0]
    return lax.conv_general_dilated(
        x, w[:, None, :].astype(x.dtype), window_strides=(1,),
        padding=[(k // 2, k // 2)], dimension_numbers=("NWC", "WIO", "NWC"),
        feature_group_count=x.shape[-1])


def axial_rope_tables(n_tokens):
    n_rows = n_tokens // GRID_W
    row = jnp.repeat(jnp.arange(n_rows, dtype=jnp.float32), GRID_W)
    col = jnp.tile(jnp.arange(GRID_W, dtype=jnp.float32), n_rows)
    inv = ROPE_BASE ** (-jnp.arange(ROPE_NF, dtype=jnp.float32) / ROPE_NF)
    ang = jnp.stack([row[:, None] * inv, col[:, None] * inv], axis=1)
    ang = ang[:, None, None]
    return jnp.cos(ang), jnp.sin(ang)


def apply_rope(x, cos, sin):
    xs = x.reshape(x.shape[:-1] + (2, 2, ROPE_NF))
    x1, x2 = xs[..., 0, :], xs[..., 1, :]
    cos, sin = cos.astype(x.dtype), sin.astype(x.dtype)
    out = jnp.stack([x1 * cos - x2 * sin, x2 * cos + x1 * sin], axis=-2)
    return out.reshape(x.shape)


def heads_qk(p):
    return p.reshape(p.shape[:2] + (DA_HEADS, 2, DA_HEAD_DIM))


def heads_v(p):
    return p.reshape(p.shape[:2] + (DA_HEADS, 2 * DA_HEAD_DIM))


def diff_attend(q, k, v, lam):
    s = jnp.einsum("bqhmd,bkhmd->bmhqk", q, k).astype(jnp.float32) * (DA_HEAD_DIM ** -0.5)
    p = jax.nn.softmax(s, axis=-1)
    a = p[:, 0] - lam * p[:, 1]
    return jnp.einsum("bhqk,bkhe->bqhe", a.astype(v.dtype), v)


def diff_attn_out(o, subln_g, lam_init, w_c_out):
    o = rmsnorm(o, subln_g) * (1 - lam_init)
    return o.reshape(o.shape[:2] + (DA_W,)) @ w_c_out


def short_conv_branch(p, conv_w, w_out):
    bg, cg, xin = jnp.split(p[..., A_OFF:B_OFF], 3, axis=-1)
    return (bg * dwconv_centred(cg * xin, conv_w)) @ w_out


def sgu_branch(p, w_s, b_s, w_out):
    z = jax.nn.gelu(p[..., B_OFF:Q_OFF])
    u, v = jnp.split(z, 2, axis=-1)
    v = layernorm_plain(v)
    bsz, t = v.shape[:2]
    vc = v.reshape(bsz, t // CHUNK, CHUNK, SGU_GROUPS, SGU_GROUP_W)
    s = jnp.einsum("gpq,bnqgc->bnpgc", w_s, vc) + b_s.T[:, :, None]
    return (u * s.reshape(bsz, t, SGU_W)) @ w_out


def conformer_conv_branch(p, dw, db, ln_g, ln_b, w_out):
    z = p[..., D_OFF:G_OFF]
    h = z[..., :CONF_W] * jax.nn.sigmoid(z[..., CONF_W:])
    h = dwconv_centred(h, dw) + db
    h = jax.nn.silu(layernorm_affine(h, ln_g, ln_b))
    return h @ w_out


def mix_stream(p, yc, conv_a_w, w_a_out, sgu_w, sgu_b, w_b_out,
               conf_dw, conf_db, conf_ln_g, conf_ln_b, w_d_out, w_o):
    ya = short_conv_branch(p, conv_a_w, w_a_out)
    yb = sgu_branch(p, sgu_w, sgu_b, w_b_out)
    yd = conformer_conv_branch(p, conf_dw, conf_db, conf_ln_g, conf_ln_b, w_d_out)
    g = jax.nn.sigmoid(p[..., G_OFF:]).reshape(p.shape[:2] + (N_BRANCH, D_MODEL))
    merged = g[:, :, 0] * ya + g[:, :, 1] * yb + g[:, :, 2] * yc + g[:, :, 3] * yd
    return merged @ w_o


def setup_inputs(seed: int = 0) -> dict:
    key = jax.random.key(seed)
    ks = iter(jax.random.split(key, 32))

    def nrm(shape, scale):
        return scale * jax.random.normal(next(ks), shape, jnp.float32)

    L, D = DEPTH, D_MODEL
    return {
        "x": nrm((BATCH, SEQ, D), 1.0),
        "c": nrm((BATCH, D), 1.0),
        "ctx": nrm((BATCH, CTX_LEN, D), 1.0),
        "c_ctx": nrm((D,), 1.0),
        "w_ada": nrm((L, D, N_MOD * D), 0.5 * D ** -0.5),
        "b_ada": nrm((L, N_MOD * D), 0.02),
        "norm_g": 1.0 + nrm((L, 3, D), 0.02),
        "ffn1_w1": nrm((L, D, D_FF), D ** -0.5),
        "ffn1_w3": nrm((L, D, D_FF), D ** -0.5),
        "ffn1_w2": nrm((L, D_FF, D), D_FF ** -0.5),
        "ffn2_w1": nrm((L, D, D_FF), D ** -0.5),
        "ffn2_w3": nrm((L, D, D_FF), D ** -0.5),
        "ffn2_w2": nrm((L, D_FF, D), D_FF ** -0.5),
        "w_in": nrm((L, D, IN_COLS), D ** -0.5),
        "conv_a_w": nrm((L, CONV_A_K, CONV_A_W), CONV_A_K ** -0.5),
        "w_a_out": nrm((L, CONV_A_W, D), CONV_A_W ** -0.5),
        "sgu_w": nrm((L, SGU_GROUPS, CHUNK, CHUNK), CHUNK ** -0.5),
        "sgu_b": nrm((L, SGU_GROUPS, CHUNK), 0.02),
        "w_b_out": nrm((L, SGU_W, D), SGU_W ** -0.5),
        "lam_p": nrm((L, 4, DA_HEAD_DIM), 0.1),
        "subln_g": 1.0 + nrm((L, 2 * DA_HEAD_DIM), 0.02),
        "w_c_out": nrm((L, DA_W, D), DA_W ** -0.5),
        "conf_dw": nrm((L, CONF_K, CONF_W), CONF_K ** -0.5),
        "conf_db": nrm((L, CONF_W), 0.02),
        "conf_ln_g": 1.0 + nrm((L, CONF_W), 0.02),
        "conf_ln_b": nrm((L, CONF_W), 0.02),
        "w_d_out": nrm((L, CONF_W, D), CONF_W ** -0.5),
        "w_o": nrm((L, D, D), D ** -0.5),
        "final_g": 1.0 + nrm((D,), 0.02),
    }


def reference(x, c, ctx, c_ctx, w_ada, b_ada, norm_g, ffn1_w1, ffn1_w3, ffn1_w2,
              ffn2_w1, ffn2_w3, ffn2_w2, w_in, conv_a_w, w_a_out, sgu_w, sgu_b,
              w_b_out, lam_p, subln_g, w_c_out, conf_dw, conf_db, conf_ln_g,
              conf_ln_b, w_d_out, w_o, final_g):
    bsz, n_tok = x.shape[:2]
    n_blk = n_tok // Q_BLOCK
    cos, sin = axial_rope_tables(n_tok)
    sc = jax.nn.silu(c)[:, None, :]
    scc = jax.nn.silu(c_ctx)[None, None, :]
    for l in range(DEPTH):
        last = l == DEPTH - 1
        mx = jnp.split(sc @ w_ada[l] + b_ada[l], N_MOD, axis=-1)
        mc = jnp.split(scc @ w_ada[l] + b_ada[l], N_MOD, axis=-1)
        lam_init = 0.8 - 0.6 * float(np.exp(-0.3 * l))
        lp = lam_p[l].astype(jnp.float32)
        lam = jnp.exp(jnp.sum(lp[0] * lp[1])) - jnp.exp(jnp.sum(lp[2] * lp[3])) + lam_init
        ffn1 = (ffn1_w1[l], ffn1_w3[l], ffn1_w2[l])
        ffn2 = (ffn2_w1[l], ffn2_w3[l], ffn2_w2[l])
        lw = (conv_a_w[l], w_a_out[l], sgu_w[l], sgu_b[l], w_b_out[l],
              conf_dw[l], conf_db[l], conf_ln_g[l], conf_ln_b[l], w_d_out[l], w_o[l])

        x = x + 0.5 * mx[2] * swiglu(modulated_norm(x, norm_g[l, 0], mx[0], mx[1]), *ffn1)
        ctx = ctx + 0.5 * mc[2] * swiglu(modulated_norm(ctx, norm_g[l, 0], mc[0], mc[1]), *ffn1)

        hx = modulated_norm(x, norm_g[l, 1], mx[3], mx[4])
        hc = modulated_norm(ctx, norm_g[l, 1], mc[3], mc[4])
        px = hx @ w_in[l]
        if last:
            pc_kv = hc @ w_in[l][:, K_OFF:D_OFF]
            kc, vc = heads_qk(pc_kv[..., :DA_W]), heads_v(pc_kv[..., DA_W:])
        else:
            pc = hc @ w_in[l]
            qc = heads_qk(pc[..., Q_OFF:K_OFF])
            kc = heads_qk(pc[..., K_OFF:V_OFF])
            vc = heads_v(pc[..., V_OFF:D_OFF])
        qx = apply_rope(heads_qk(px[..., Q_OFF:K_OFF]), cos, sin)
        kx = apply_rope(heads_qk(px[..., K_OFF:V_OFF]), cos, sin)
        vx = heads_v(px[..., V_OFF:D_OFF])
        k_all = jnp.concatenate([kx, kc], axis=1)
        v_all = jnp.concatenate([vx, vc], axis=1)
        qb = jnp.moveaxis(qx.reshape((bsz, n_blk, Q_BLOCK) + qx.shape[2:]), 1, 0)
        ob = lax.map(lambda qi: diff_attend(qi, k_all, v_all, lam), qb)
        ox = jnp.moveaxis(ob, 0, 1).reshape((bsz, n_tok) + ob.shape[3:])
        yc_x = diff_attn_out(ox, subln_g[l], lam_init, w_c_out[l])
        x = x + mx[5] * mix_stream(px, yc_x, *lw)

        x = x + 0.5 * mx[8] * swiglu(modulated_norm(x, norm_g[l, 2], mx[6], mx[7]), *ffn2)

        if not last:
            oc = diff_attend(qc, kc, vc, lam)
            yc_c = diff_attn_out(oc, subln_g[l], lam_init, w_c_out[l])
            ctx = ctx + mc[5] * mix_stream(pc, yc_c, *lw)
            ctx = ctx + 0.5 * mc[8] * swiglu(modulated_norm(ctx, norm_g[l, 2], mc[6], mc[7]), *ffn2)

    return rmsnorm(x, final_g)
```

```python
import numpy as np
import concourse.bass as bass
import concourse.mybir as mybir
from concourse.bass_utils import run_bass_kernel_spmd
from contextlib import ExitStack

F32 = mybir.dt.float32
BF16 = mybir.dt.bfloat16
AF = mybir.ActivationFunctionType
ALU = mybir.AluOpType

D = 1024
DC = 8
L = 2
NT = 2048
NCX = 256
DFF = 2816
FC = 22
IN_COLS = 7424
A_OFF, B_OFF, Q_OFF, K_OFF, V_OFF, D_OFF, G_OFF = 0, 768, 1280, 1792, 2304, 2816, 3328
EPS = 1e-6
KW = 2048
NKT = 34

SAME_ENGINE_SYNC = True
DEBUG_STOP = 10 ** 9
DEBUG_TAPS = False


class Res:
    __slots__ = ("name", "w", "r", "dsem", "dcount")

    def __init__(self, name):
        self.name = name
        self.w = {}
        self.r = {}
        self.dsem = None
        self.dcount = 0


class Prog:
    ENGS = ("pe", "act", "dve", "pool", "sp")

    def __init__(self, nc, stack):
        self.nc = nc
        self.stack = stack
        self.streams = {e: [] for e in self.ENGS}
        self.seen = {e: {} for e in self.ENGS}
        self.esem = {}
        for e in self.ENGS:
            self.esem[e] = stack.enter_context(nc.semaphore("es_" + e))
        self.nres = 0
        self.dma_res = []
        self.bar_seen = {}

    def res(self, name=None):
        self.nres += 1
        return Res(name or f"r{self.nres}")

    def emit(self, eng, fn, reads=(), writes=(), dma=None, inc=16, multi=None):
        deps = {}
        for r in reads:
            for k, v in r.w.items():
                if deps.get(k, -1) < v:
                    deps[k] = v
        for r in writes:
            for d in (r.w, r.r):
                for k, v in d.items():
                    if deps.get(k, -1) < v:
                        deps[k] = v
        seen = self.seen[eng]
        waits = []
        for k, v in deps.items():
            if k == eng and dma is None:
                if eng == "pe" or not SAME_ENGINE_SYNC:
                    continue
            if seen.get(k, -1) >= v:
                continue
            seen[k] = v
            waits.append((k, v))
        fns = multi if multi is not None else [fn]
        for i, f in enumerate(fns):
            idx = len(self.streams[eng])
            rec = {"fn": f, "waits": waits if i == 0 else [], "signal": False, "dma": dma, "inc": inc}
            self.streams[eng].append(rec)
            if dma is not None:
                if dma.dsem is None:
                    dma.dsem = self.stack.enter_context(self.nc.semaphore("ds_" + dma.name))
                    self.dma_res.append(dma)
                dma.dcount += inc
        if dma is not None:
            key, val = ("d", dma), dma.dcount
        else:
            key, val = eng, idx
        for r in reads:
            r.r[key] = max(r.r.get(key, -1), val)
        for r in writes:
            r.w = {key: val}
            r.r = {}

    def barrier(self):
        b = Res("bar")
        for e in self.ENGS:
            st_ = self.streams[e]
            for i in range(len(st_) - 1, -1, -1):
                if st_[i]["dma"] is None:
                    b.w[e] = i
                    break
        for r in self.dma_res:
            if r.dcount > self.bar_seen.get(r, 0):
                b.w[("d", r)] = r.dcount
                self.bar_seen[r] = r.dcount
        for e in self.ENGS:
            self.emit(e, lambda h: h.nop(), reads=[b])

    def fence(self, group):
        deps = {}
        for r in group:
            for d in (r.w, r.r):
                for k, v in d.items():
                    if deps.get(k, -1) < v:
                        deps[k] = v
        for r in group:
            for k, v in deps.items():
                if r.w.get(k, -1) < v:
                    r.w[k] = v

    def finalize(self):
        nc = self.nc
        for e in self.ENGS:
            for rec in self.streams[e]:
                for k, v in rec["waits"]:
                    if isinstance(k, str):
                        assert self.streams[k][v]["dma"] is None
                        self.streams[k][v]["signal"] = True
        cnt = {}
        for e in self.ENGS:
            c = 0
            arr = []
            for rec in self.streams[e]:
                if rec["signal"]:
                    c += 1
                arr.append(c)
            cnt[e] = arr
        handles = {"pe": "tensor", "act": "scalar", "dve": "vector", "pool": "gpsimd", "sp": "sync"}
        with nc.Block() as block:
            for e in self.ENGS:
                stream = self.streams[e]
                if not stream:
                    continue

                def body(h, e=e, stream=stream):
                    for rec in stream:
                        for k, v in rec["waits"]:
                            if isinstance(k, str):
                                h.wait_ge(self.esem[k], cnt[k][v])
                            else:
                                h.wait_ge(k[1].dsem, v)
                        ins = rec["fn"](h)
                        if rec["dma"] is not None:
                            ins.then_inc(rec["dma"].dsem, rec["inc"])
                        elif rec["signal"]:
                            ins.then_inc(self.esem[e], 1)

                getattr(block, handles[e])(body)


PV = {}
_o = 0
for _n, _w in [("b_ada", L * 72), ("norm_g", L * 3 * 8), ("final_g", 8), ("conv_a_w", L * 3 * 2),
               ("conf_dw", L * 31 * 2), ("conf_db", L * 2), ("conf_ln_g", L * 2), ("conf_ln_b", L * 2),
               ("subln_g", L), ("lam_p", L * 256), ("sgu_b", L * 2 * 128), ("mleft", 1), ("mright", 1)]:
    PV[_n] = (_o, _w)
    _o += _w
NPV = _o


def build_program():
    nc = bass.Bass("TRN2", target_bir_lowering=False)

    def din(name, shape, dt=F32):
        return nc.dram_tensor(name, list(shape), dt, kind="ExternalInput").ap()

    xT_d = din("xT", [D, NT])
    cT_d = din("cT", [D, NCX])
    cvec_d = din("cvec", [D, 2])
    pvec_d = din("pvec", [128, NPV])
    rot_d = din("rot", [128, 128])
    cos_d = din("cosT", [128, NT])
    sin_d = din("sinT", [128, NT])
    w_ada_d = din("w_ada", [L, D, 9 * D])
    ffw = {}
    for nm in ("ffn1_w1", "ffn1_w3", "ffn2_w1", "ffn2_w3"):
        ffw[nm] = din(nm, [L, D, DFF])
    for nm in ("ffn1_w2", "ffn2_w2"):
        ffw[nm] = din(nm, [L, DFF, D])
    w_in_d = din("w_in", [L, D, IN_COLS])
    w_a_d = din("w_a_out", [L, 256, D])
    w_b_d = din("w_b_out", [L, 256, D])
    w_c_d = din("w_c_out", [L, 512, D])
    w_d_d = din("w_d_out", [L, 256, D])
    w_o_d = din("w_o", [L, D, D])
    sguw_d = din("sgu_wT", [L, 4, 128, 128])
    yT_d = nc.dram_tensor("yT", [D, NT], F32, kind="ExternalOutput").ap()
    dbg_d = nc.dram_tensor("dbg", [128, 40, 512], BF16, kind="ExternalOutput").ap() if DEBUG_TAPS else None

    def dscr(name, shape):
        return nc.dram_tensor(name, list(shape), BF16, kind="Internal").ap()

    K_loc = [dscr(f"K_loc{l}", [128, 4 * KW]) for l in range(L)]
    K_all = [dscr(f"K_all{l}", [256, 4 * KW]) for l in range(L)]
    V_loc = [dscr(f"V_loc{l}", [NT, 512]) for l in range(L)]
    V_all = [dscr(f"V_all{l}", [2 * NT, 512]) for l in range(L)]
    H_loc = [dscr(f"H_loc{l}", [128, 128]) for l in range(L)]
    H_all = [dscr(f"H_all{l}", [256, 128]) for l in range(L)]
    Kc_loc = [dscr(f"Kc_loc{l}", [128, 4 * NCX]) for l in range(L)]
    Vc_loc = [dscr(f"Vc_loc{l}", [NCX, 512]) for l in range(L)]
    CV_loc = [dscr(f"CV_loc{l}", [128, 4 * NT]) for l in range(L)]
    CVc_loc = [dscr(f"CVc_loc{l}", [128, 4 * NCX]) for l in range(L)]

    with ExitStack() as st:
        P = Prog(nc, st)

        def sb(name, shape, dt):
            return st.enter_context(nc.sbuf_tensor(name, list(shape), dt))

        xT = sb("xT_s", [128, DC, NT], F32)
        cT = sb("cT_s", [128, DC, NCX], F32)
        NSLOT = 4
        slots = [sb(f"wslot{i}", [128, 4096], BF16) for i in range(NSLOT)]
        hT = sb("hT", [128, DC, 1280], BF16)
        BIGN = 24192
        big = sb("big", [128, BIGN], BF16)
        NTMP = 7
        tmp_all = sb("tmp_all", [128, NTMP * 512], F32)
        tmps = [tmp_all[:, i * 512:(i + 1) * 512] for i in range(NTMP)]
        pvec = sb("pvec_s", [128, NPV], F32)
        cvec = sb("cvec_s", [128, DC, 2], F32)
        scb = sb("scb", [128, DC, 2], BF16)
        modT = sb("modT", [128, L, 72, 2], F32)
        gsT = sb("gsT", [128, L, 3, DC, 2], F32)
        ghT = sb("ghT", [128, L, 2, DC, 2], F32)
        fgT = sb("fgT", [128, DC], F32)
        lamT = sb("lamT", [128, L, 4], F32)
        lscr = sb("lscr", [128, 136], F32)
        ones_bf = sb("ones_bf", [128, 128], BF16)
        rot_bf = sb("rot_bf", [128, 128], BF16)
        sguw = sb("sguw", [128, L, 4, 128], BF16)
        bias_rep = sb("bias_rep", [128, 2, 512], F32)
        sqs = sb("sqs", [128, 512], BF16)
        mv8 = sb("mv8", [128, 8], F32)
        ps_all = st.enter_context(nc.psum_tensor("ps_all", [128, 4096], F32))
        ps = [ps_all[:, i * 512:(i + 1) * 512] for i in range(8)]

        def bigv(off, n):
            return big[:, off:off + n]
        gT = bigv(0, 11 * 1280).rearrange("p (c t) -> p c t", c=11)
        o = 0
        kT = bigv(o, NKT * 128); o += NKT * 128
        vT = bigv(o, NKT * 128).rearrange("p (k e) -> p k e", k=NKT); o += NKT * 128
        owin = o
        win = bigv(o, 4 * 544).rearrange("p (c t) -> p c t", c=4); o += 4 * 544
        oq = o
        q0T = bigv(o, 2048).rearrange("p (h t) -> p h t", h=4); o += 2048
        q1T = bigv(o, 2048).rearrange("p (h t) -> p h t", h=4); o += 2048
        pT = [bigv(o + i * 512, 512) for i in range(4)]
        pT2 = [bigv(o + i * 1024, 1024) for i in range(2)]; o += 2048
        ocT = bigv(owin, 2048).rearrange("p (h t) -> p h t", h=4)
        aT = bigv(o, 1024).rearrange("p (c t) -> p c t", c=2); o += 1024
        bT = bigv(o, 1024).rearrange("p (c t) -> p c t", c=2); o += 1024
        dT = bigv(o, 1024).rearrange("p (c t) -> p c t", c=2); o += 1024
        kst = bigv(oq, 2048).rearrange("p (h t) -> p h t", h=4)
        vst = bigv(oq + 2048, 2048).rearrange("p (k e) -> p k e", k=4)
        cvst = bigv(oq + 4096, 2048).rearrange("p (c t) -> p c t", c=4)
        sgs = pT
        o = max(o, 11 * 1280)
        sq8 = bigv(o, 4096).rearrange("p (c t) -> p c t", c=8)
        assert o + 4096 <= BIGN, o
        merged = sq8
        ropec = sb("ropec", [128, 512], F32)
        ropes = sb("ropes", [128, 512], F32)

        r_x = [P.res(f"x{t}") for t in range(5)]
        r_slot = [P.res(f"slot{i}") for i in range(NSLOT)]
        r_hTc = [[P.res(f"hT{i}_{c}") for c in range(DC)] for i in range(3)]
        r_hT_all = [r for lst in r_hTc for r in lst]
        r_gT = [P.res(f"gT{i}") for i in range(11)]
        r_tmp = [P.res(f"tmp{i}") for i in range(NTMP)]
        r_ps = [P.res(f"ps{i}") for i in range(8)]
        r_small = P.res("small")
        r_mod1 = P.res("mod1")
        r_pvec = P.res("pvec")
        r_cvec = P.res("cvec")
        r_const = P.res("const")
        r_kTp = [P.res(f"kT{i}") for i in range(3)]
        r_vTp = [P.res(f"vT{i}") for i in range(3)]
        r_kT, r_vT = r_kTp[2], r_vTp[2]
        r_win = P.res("win")
        r_q, r_oc, r_a, r_b, r_d = P.res("q"), P.res("oc"), P.res("a"), P.res("b"), P.res("d")
        r_pT = [P.res(f"pT{i}") for i in range(4)]
        r_kst, r_vst, r_cvst = P.res("kst"), P.res("vst"), P.res("cvst")
        r_sgs = r_pT
        r_sq8 = P.res("sq8")
        r_vnm = [P.res(f"vnm{i}") for i in range(4)]
        r_csq = [P.res("csq0"), P.res("csq1")]
        r_rope = P.res("rope")
        r_sqs = P.res("sqs")
        r_vn = P.res("vn")
        r_vt = P.res("vt")
        r_Kloc = [P.res(f"Kloc{l}") for l in range(L)]
        r_Kall = [P.res(f"Kall{l}") for l in range(L)]
        r_Vloc = [P.res(f"Vloc{l}") for l in range(L)]
        r_Vall = [P.res(f"Vall{l}") for l in range(L)]
        r_Hloc = [P.res(f"Hloc{l}") for l in range(L)]
        r_Hall = [P.res(f"Hall{l}") for l in range(L)]
        r_Kc = [P.res(f"Kc{l}") for l in range(L)]
        r_Vc = [P.res(f"Vc{l}") for l in range(L)]
        r_CV = [P.res(f"CV{l}") for l in range(L)]
        r_CVc = [P.res(f"CVc{l}") for l in range(L)]
        r_out = P.res("out")

        r_dbg = P.res("dbg")

        def tap(slot, ap, res, width=512):
            if not DEBUG_TAPS:
                return
            P.emit("sp", lambda h: h.dma_start(out=dbg_d[:, slot, 0:width], in_=ap), reads=[res], writes=[r_dbg], dma=r_dbg)

        state = {"slot": 0, "psa": 0, "psb": 0, "tmp": 0}

        def next_slot():
            i = state["slot"]
            state["slot"] = (i + 1) % NSLOT
            return i

        def psA():
            i = state["psa"]
            state["psa"] = (i + 1) % 4
            return i

        def psB():
            i = state["psb"]
            state["psb"] = (i + 1) % 4
            return 4 + i

        def pairA():
            i = state.get("pair", 0)
            state["pair"] = 2 - i
            return i

        def ntmp2():
            i = state["tmp"]
            if i + 1 >= NTMP:
                i = 0
            state["tmp"] = (i + 2) % NTMP
            return i

        def ps_any():
            i = state.get("psr", 0)
            state["psr"] = (i + 1) % 8
            return i

        def ntmp():
            i = state["tmp"]
            state["tmp"] = (i + 1) % NTMP
            return i

        def mm(pi, out, lhsT, rhs, start, stop, reads):
            P.emit("pe", lambda h: h.matmul(out, lhsT=lhsT, rhs=rhs, start=start, stop=stop),
                   reads=reads, writes=[r_ps[pi]])

        def act(out, in_, func, reads, writes, bias=0.0, scale=1.0):
            P.emit("act", lambda h: h.activation(out=out, in_=in_, func=func, bias=bias, scale=scale),
                   reads=reads, writes=writes)

        def tt(out, in0, in1, op, reads, writes, eng="dve"):
            P.emit(eng, lambda h: h.tensor_tensor(out=out, in0=in0, in1=in1, op=op), reads=reads, writes=writes)

        def ts(out, in0, s1, s2, op0, op1, reads, writes, eng="dve"):
            if s2 is None:
                P.emit(eng, lambda h: h.tensor_scalar(out=out, in0=in0, scalar1=s1, scalar2=None, op0=op0),
                       reads=reads, writes=writes)
            else:
                P.emit(eng, lambda h: h.tensor_scalar(out=out, in0=in0, scalar1=s1, scalar2=s2, op0=op0, op1=op1),
                       reads=reads, writes=writes)

        def stt(out, in0, scalar, in1, op0, op1, reads, writes, eng="dve"):
            P.emit(eng, lambda h: h.scalar_tensor_tensor(out=out, in0=in0, scalar=scalar, in1=in1, op0=op0, op1=op1),
                   reads=reads, writes=writes)

        def cp(out, in_, reads, writes, eng="dve"):
            if eng == "act":
                P.emit(eng, lambda h: h.copy(out=out, in_=in_), reads=reads, writes=writes)
            else:
                P.emit(eng, lambda h: h.tensor_copy(out=out, in_=in_), reads=reads, writes=writes)

        def recip(out, in_, reads, writes):
            P.emit("dve", lambda h: h.reciprocal(out=out, in_=in_), reads=reads, writes=writes)

        def memset(ap, val, writes, eng="pool"):
            P.emit(eng, lambda h: h.memset(ap, val), writes=writes)

        def load_slot(src_ap, view_fn):
            i = next_slot()
            dst = view_fn(slots[i])
            P.emit("pool", lambda h: h.dma_start(out=dst, in_=src_ap), writes=[r_slot[i]], dma=r_slot[i])
            return i, dst

        def pv(name, idx=0, n=1):
            o_, w_ = PV[name]
            return pvec[:, o_ + idx:o_ + idx + n]

        TILES = [(0, 0, 512), (1, 512, 512), (2, 1024, 512), (3, 1536, 512), (4, 0, NCX)]

        def xview(t):
            ti, t0, n = TILES[t]
            if ti < 4:
                return xT[:, :, t0:t0 + n]
            return cT[:, :, 0:n]

        P.emit("sp", lambda h: h.dma_start(out=pvec[:], in_=pvec_d), writes=[r_pvec], dma=r_pvec)
        P.emit("sp", lambda h: h.dma_start(out=cvec[:], in_=cvec_d.rearrange("(c p) j -> p c j", p=128)),
               writes=[r_cvec], dma=r_cvec)
        for t in range(4):
            t0 = TILES[t][1]
            P.emit("sp", lambda h, t0=t0: h.dma_start(out=xT[:, :, t0:t0 + 512],
                                                    in_=xT_d.rearrange("(c p) t -> p c t", p=128)[:, :, t0:t0 + 512]),
                   writes=[r_x[t]], dma=r_x[t])
        P.emit("sp", lambda h: h.dma_start(out=cT[:], in_=cT_d.rearrange("(c p) t -> p c t", p=128)),
               writes=[r_x[4]], dma=r_x[4])
        P.emit("pool", lambda h: h.dma_start(out=rot_bf[:], in_=rot_d), writes=[r_const], dma=r_const)
        P.emit("pool", lambda h: h.dma_start(out=sguw[:], in_=sguw_d.rearrange("l g q p -> q l g p")),
               writes=[r_const], dma=r_const)
        memset(ones_bf[:], 1.0, [r_const], eng="dve")
        memset(mv8[:], 1.0, [r_vt], eng="dve")
        act(scb[:], cvec[:], AF.Silu, [r_cvec], [r_small])

        def ada_block(l, j):
            rm = r_small if l == 0 else r_mod1
            pi = ps_any()
            si, wv = load_slot(w_ada_d[l].rearrange("(c p) f -> p c f", p=128)[:, :, j * 512:(j + 1) * 512],
                               lambda s: s[:, 0:4096].rearrange("p (c f) -> p c f", c=DC))
            for m in range(4):
                for c in range(DC):
                    mm(pi, ps[pi][:, 2 * m:2 * m + 2], wv[:, c, m * 128:(m + 1) * 128], scb[:, c, :],
                       c == 0, c == DC - 1, [r_slot[si], r_small])
            bo = PV["b_ada"][0] + l * 72 + j * 4
            for col in range(2):
                tt(modT[:, l, j * 4:(j + 1) * 4, col], ps[pi][:, 0:8].rearrange("p (j c) -> p j c", c=2)[:, :, col],
                   pvec[:, bo:bo + 4], ALU.add, [r_ps[pi], r_pvec], [rm])

        def ada_finish_k(l, k):
            rm = r_small if l == 0 else r_mod1
            for col in range(2):
                go = PV["norm_g"][0] + (l * 3 + k) * 8
                stt(gsT[:, l, k, :, col], modT[:, l, (3 * k + 1) * 8:(3 * k + 2) * 8, col], 1.0,
                    pvec[:, go:go + 8], ALU.add, ALU.mult, [rm, r_pvec], [rm])
            if k != 1:
                which = 0 if k == 0 else 1
                for col in range(2):
                    ts(ghT[:, l, which, :, col], modT[:, l, (3 * k + 2) * 8:(3 * k + 3) * 8, col], 0.5, None,
                       ALU.mult, None, [rm], [rm])
            else:
                lo = PV["lam_p"][0] + l * 256
                ls = lscr[:, l * 68:(l + 1) * 68] if False else lscr
                tt(lscr[:, 0:64], pvec[:, lo:lo + 64], pvec[:, lo + 64:lo + 128], ALU.mult, [r_pvec], [rm])
                tt(lscr[:, 64:128], pvec[:, lo + 128:lo + 192], pvec[:, lo + 192:lo + 256], ALU.mult, [r_pvec], [rm])
                P.emit("dve", lambda h: h.reduce_sum(out=lscr[:, 128:130], in_=lscr[:, 0:128].rearrange("p (a b) -> p a b", a=2),
                                                     axis=mybir.AxisListType.X), reads=[rm], writes=[rm])
                act(lscr[:, 130:132], lscr[:, 128:130], AF.Exp, [rm], [rm])
                lam_init = 0.8 - 0.6 * float(np.exp(-0.3 * l))
                stt(lamT[:, l, 0:1], lscr[:, 131:132], -lam_init, lscr[:, 130:131], ALU.add, ALU.subtract,
                    [rm], [rm])
                so = PV["subln_g"][0] + l
                ts(lamT[:, l, 1:2], pvec[:, so:so + 1], 1.0 - lam_init, None, ALU.mult, None, [r_pvec], [rm])
            if l == 1 and k == 2:
                P.fence([r_small, r_mod1])

        ada_todo = [(0, j) for j in range(6, 18)] + [(1, j) for j in range(18)]

        def ada_some(n_):
            for _ in range(n_):
                if ada_todo:
                    l_, j_ = ada_todo.pop(0)
                    ada_block(l_, j_)
                    if j_ % 6 == 5:
                        ada_finish_k(l_, j_ // 6)

        for j in range(6):
            ada_block(0, j)
        ada_finish_k(0, 0)
        cp(fgT[:], pv("final_g", 0, 8), [r_pvec], [r_small])

        def modnorm(t, gs_ap, shift_ap, hoff=0, rh=None, mixer=False):
            ti, t0, n = TILES[t]
            rh = rh or r_hTc[0]
            xv = xview(t)
            sqv = hT[:, :, 768:1280] if mixer else sq8
            sqr = r_hTc[2] if mixer else [r_sq8] * DC
            for c in range(DC):
                if c < 5:
                    act(sqv[:, c, 0:n], xv[:, c, :], AF.Square, [r_x[t]], [sqr[c]])
                else:
                    tt(sqv[:, c, 0:n], xv[:, c, :], xv[:, c, :], ALU.mult, [r_x[t]], [sqr[c]])
            pi = ps_any()
            for c in range(DC):
                mm(pi, ps[pi][:, 0:n], ones_bf[:], sqv[:, c, 0:n], c == 0, c == DC - 1, [sqr[c], r_const])
            ri = ntmp()
            act(tmps[ri][:, 0:n], ps[pi][:, 0:n], AF.Sqrt, [r_ps[pi]], [r_tmp[ri]], bias=EPS, scale=1.0 / D)
            recip(tmps[ri][:, 0:n], tmps[ri][:, 0:n], [r_tmp[ri]], [r_tmp[ri]])
            wis = [ntmp(), ntmp(), ntmp()]
            for c in range(DC):
                wi = wis[c % 3]
                stt(tmps[wi][:, 0:n], xv[:, c, :], gs_ap[:, c:c + 1], tmps[ri][:, 0:n], ALU.mult, ALU.mult,
                    [r_x[t], r_small, r_tmp[ri]], [r_tmp[wi]])
                act(hT[:, c, hoff:hoff + n], tmps[wi][:, 0:n], AF.Identity, [r_tmp[wi], r_small], [rh[c]],
                    bias=shift_ap[:, c:c + 1], scale=1.0)

        def ffn(tlist, l, which):
            k = 0 if which == 0 else 2
            w1 = ffw[f"ffn{which + 1}_w1"][l].rearrange("(c p) f -> p c f", p=128)
            w3 = ffw[f"ffn{which + 1}_w3"][l].rearrange("(c p) f -> p c f", p=128)
            w2 = ffw[f"ffn{which + 1}_w2"][l].rearrange("(c p) d -> p c d", p=128)
            subs = []
            ho = 0
            for si_, t in enumerate(tlist):
                ti, t0, n = TILES[t]
                col = 0 if ti < 4 else 1
                subs.append((t, ho, n, col, r_hTc[si_]))
                ho += n

            def norm_sub(sub):
                t, ho, n, col, rh = sub
                modnorm(t, gsT[:, l, k, :, col], modT[:, l, (3 * k) * 8:(3 * k + 1) * 8, col], hoff=ho, rh=rh)

            for fh in range(2):
                fbase = fh * 11
                for fb in range(3):
                    nf = 4 if fb < 2 else 3
                    wcols = nf * 128
                    c0 = (fbase + fb * 4) * 128
                    s1, v1 = load_slot(w1[:, :, c0:c0 + wcols],
                                       lambda s, wcols=wcols: s[:, 0:DC * wcols].rearrange("p (c f) -> p c f", c=DC))
                    s3, v3 = load_slot(w3[:, :, c0:c0 + wcols],
                                       lambda s, wcols=wcols: s[:, 0:DC * wcols].rearrange("p (c f) -> p c f", c=DC))
                    first_blk = fh == 0 and fb == 0
                    order = ([(m, sb_) for sb_ in subs for m in range(nf)] if first_blk
                             else [(m, sb_) for m in range(nf) for sb_ in subs])
                    if first_blk:
                        norm_sub(subs[0])
                        if len(subs) > 1:
                            norm_sub(subs[1])
                    for m, sb_ in order:
                        fc = fb * 4 + m
                        if first_blk and len(subs) > 2 and sb_ is subs[1] and m == 0:
                            norm_sub(subs[2])
                        for (t, ho, n, col, rh) in (sb_,):
                            p1, p3 = ps_any(), ps_any()
                            for c in range(DC):
                                mm(p1, ps[p1][:, 0:n], v1[:, c, m * 128:(m + 1) * 128], hT[:, c, ho:ho + n], c == 0,
                                   c == DC - 1, [r_slot[s1], rh[c]])
                            for c in range(DC):
                                mm(p3, ps[p3][:, 0:n], v3[:, c, m * 128:(m + 1) * 128], hT[:, c, ho:ho + n], c == 0,
                                   c == DC - 1, [r_slot[s3], rh[c]])
                            wi = ntmp()
                            act(tmps[wi][:, 0:n], ps[p1][:, 0:n], AF.Silu, [r_ps[p1]], [r_tmp[wi]])
                            tt(gT[:, fc, ho:ho + n], tmps[wi][:, 0:n], ps[p3][:, 0:n], ALU.mult, [r_tmp[wi], r_ps[p3]],
                               [r_gT[fc]])
                    if l == 0:
                        ada_some(1)
                for dc in range(DC):
                    s2, v2 = load_slot(w2[:, fbase:fbase + 11, dc * 128:(dc + 1) * 128],
                                       lambda s: s[:, 0:11 * 128].rearrange("p (c d) -> p c d", c=11))
                    for (t, ho, n, col, rh) in subs:
                        xv = xview(t)
                        pi = ps_any()
                        for fc in range(11):
                            mm(pi, ps[pi][:, 0:n], v2[:, fc, :], gT[:, fc, ho:ho + n], fc == 0, fc == 10,
                               [r_slot[s2], r_gT[fc]])
                        stt(xv[:, dc, :], ps[pi][:, 0:n], ghT[:, l, which, dc:dc + 1, col], xv[:, dc, :], ALU.mult, ALU.add,
                            [r_ps[pi], r_small, r_x[t]], [r_x[t]])
                    if l == 0 and dc % 2 == 1:
                        ada_some(1)

        w_in_v = [w_in_d[l].rearrange("(c p) f -> p c f", p=128) for l in range(L)]

        def v512(s):
            return s[:, 0:4096].rearrange("p (c f) -> p c f", c=DC)

        def v256(s):
            return s[:, 0:2048].rearrange("p (c f) -> p c f", c=DC)

        def proj(si, wv, m, n, reads_extra=()):
            pi = ps_any()
            for c in range(DC):
                mm(pi, ps[pi][:, 0:n], wv[:, c, m * 128:(m + 1) * 128], hT[:, c, 0:n], c == 0, c == DC - 1,
                   [r_slot[si], r_hTc[0][c]])
            return pi

        def rope_to(dst_fn, pi, n, dst_res):
            a = ntmp()
            cp(sqs[:, 0:n], ps[pi][:, 0:n], [r_ps[pi]], [r_sqs])
            p2 = ps_any()
            mm(p2, ps[p2][:, 0:n], rot_bf[:], sqs[:, 0:n], True, True, [r_sqs, r_const])
            tt(tmps[a][:, 0:n], ps[pi][:, 0:n], ropec[:, 0:n], ALU.mult, [r_ps[pi], r_rope], [r_tmp[a]])
            b = ntmp()
            tt(tmps[b][:, 0:n], ps[p2][:, 0:n], ropes[:, 0:n], ALU.mult, [r_ps[p2], r_rope], [r_tmp[b]])
            return a, b

        def mixer_a(t, l):
            ti, t0, n = TILES[t]
            latent = ti < 4
            col = 0 if latent else 1
            modnorm(t, gsT[:, l, 1, :, col], modT[:, l, 24:32, col], mixer=True)
            if latent:
                P.emit("sp", None, writes=[r_rope], dma=r_rope, multi=[
                    lambda h: h.dma_start(out=ropec[:], in_=cos_d[:, t0:t0 + 512]),
                    lambda h: h.dma_start(out=ropes[:], in_=sin_d[:, t0:t0 + 512])])
            si, wv = load_slot(w_in_v[l][:, :, K_OFF:K_OFF + 512], v512)
            for hd in range(4):
                pi = proj(si, wv, hd, n)
                if latent:
                    a, b = rope_to(None, pi, n, r_kst)
                    tt(kst[:, hd, 0:n], tmps[a][:, 0:n], tmps[b][:, 0:n], ALU.add, [r_tmp[a], r_tmp[b]], [r_kst])
                else:
                    cp(kst[:, hd, 0:n], ps[pi][:, 0:n], [r_ps[pi]], [r_kst])
            if latent:
                dstK = K_loc[l].rearrange("p (h w) -> p h w", h=4)[:, :, t0:t0 + n]
                P.emit("sp", lambda h: h.dma_start(out=dstK, in_=kst[:, :, 0:n]), reads=[r_kst], writes=[r_Kloc[l]], dma=r_Kloc[l])
            else:
                dstK = Kc_loc[l].rearrange("p (h w) -> p h w", h=4)
                P.emit("sp", lambda h: h.dma_start(out=dstK, in_=kst[:, :, 0:n]), reads=[r_kst], writes=[r_Kc[l]], dma=r_Kc[l])
            si, wv = load_slot(w_in_v[l][:, :, V_OFF:V_OFF + 512], v512)
            for tc in range(n // 128):
                pi = ps_any()
                for c in range(DC):
                    mm(pi, ps[pi][:, :], hT[:, c, tc * 128:(tc + 1) * 128], wv[:, c, :], c == 0, c == DC - 1,
                       [r_slot[si], r_hTc[0][c]])
                cp(vst[:, tc, :], ps[pi][:, :], [r_ps[pi]], [r_vst], eng="act" if tc % 2 else "dve")
            if latent:
                dstV = V_loc[l][t0:t0 + n, :].rearrange("(k p) e -> p k e", p=128)
                P.emit("sp", lambda h: h.dma_start(out=dstV, in_=vst[:, 0:n // 128, :]), reads=[r_vst], writes=[r_Vloc[l]], dma=r_Vloc[l])
            else:
                dstV = Vc_loc[l].rearrange("(k p) e -> p k e", p=128)
                P.emit("sp", lambda h: h.dma_start(out=dstV, in_=vst[:, 0:n // 128, :]), reads=[r_vst], writes=[r_Vc[l]], dma=r_Vc[l])
            if (not latent) and l == L - 1:
                return
            si, wv = load_slot(w_in_v[l][:, :, 256:768], v512)
            for k2 in range(2):
                pc = proj(si, wv, k2, n)
                px = proj(si, wv, 2 + k2, n)
                a = ntmp()
                act(tmps[a][:, 0:n], ps[pc][:, 0:n], AF.Identity, [r_ps[pc]], [r_tmp[a]])
                tt(cvst[:, k2, 0:n], tmps[a][:, 0:n], ps[px][:, 0:n], ALU.mult, [r_tmp[a], r_ps[px]], [r_cvst])
            si, wv = load_slot(w_in_v[l][:, :, D_OFF:D_OFF + 512], v512)
            for k2 in range(2):
                pz = proj(si, wv, k2, n)
                pg = proj(si, wv, 2 + k2, n)
                a = ntmp()
                act(tmps[a][:, 0:n], ps[pg][:, 0:n], AF.Sigmoid, [r_ps[pg]], [r_tmp[a]])
                tt(cvst[:, 2 + k2, 0:n], tmps[a][:, 0:n], ps[pz][:, 0:n], ALU.mult, [r_tmp[a], r_ps[pz]], [r_cvst])
            if latent:
                dst = CV_loc[l].rearrange("p (c w) -> p c w", c=4)[:, :, t0:t0 + n]
                P.emit("sp", lambda h: h.dma_start(out=dst, in_=cvst[:, :, 0:n]), reads=[r_cvst], writes=[r_CV[l]], dma=r_CV[l])
                hv = H_loc[l].rearrange("p (c w) -> p c w", c=4)
                if ti == 0:
                    P.emit("sp", lambda h: h.dma_start(out=hv[:, :, 0:16], in_=cvst[:, :, 0:16]),
                           reads=[r_cvst], writes=[r_Hloc[l]], dma=r_Hloc[l])
                if ti == 3:
                    P.emit("sp", lambda h: h.dma_start(out=hv[:, :, 16:32], in_=cvst[:, :, 496:512]),
                           reads=[r_cvst], writes=[r_Hloc[l]], dma=r_Hloc[l])
            else:
                dst = CVc_loc[l].rearrange("p (c w) -> p c w", c=4)
                P.emit("sp", lambda h: h.dma_start(out=dst, in_=cvst[:, :, 0:n]), reads=[r_cvst], writes=[r_CVc[l]], dma=r_CVc[l])

        def exchange(l):
            grp = [[0, 1], [2, 3], [4, 5], [6, 7]]
            P.emit("pool", lambda h: h.collective_compute("AllGather", ALU.bypass, replica_groups=grp,
                                                          ins=[K_loc[l]], outs=[K_all[l]]),
                   reads=[r_Kloc[l]], writes=[r_Kall[l]], dma=r_Kall[l], inc=1)
            P.emit("pool", lambda h: h.collective_compute("AllGather", ALU.bypass, replica_groups=grp,
                                                          ins=[H_loc[l]], outs=[H_all[l]]),
                   reads=[r_Hloc[l]], writes=[r_Hall[l]], dma=r_Hall[l], inc=1)
            P.emit("pool", lambda h: h.collective_compute("AllGather", ALU.bypass, replica_groups=grp,
                                                          ins=[V_loc[l]], outs=[V_all[l]]),
                   reads=[r_Vloc[l]], writes=[r_Vall[l]], dma=r_Vall[l], inc=1)

        def attention(t, l, hooks=None):
            hooks = dict(hooks or {})
            ti, t0, n = TILES[t]
            latent = ti < 4
            kts = list(range(NKT)) if latent else [32, 33]
            pending = []
            active = []
            for hd in range(4):
                kall = K_all[l].rearrange("q (h w) -> q h w", h=4)
                if latent:
                    for part in range(2):
                        P.emit("sp", lambda h, hd=hd, part=part: h.dma_start(
                            out=kT[:, part * 2048:(part + 1) * 2048], in_=kall[part * 128:(part + 1) * 128, hd, 0:2048]),
                            reads=[r_Kall[l]], writes=[r_kTp[part]], dma=r_kTp[part])
                        P.emit("sp", lambda h, hd=hd, part=part: h.dma_start(
                            out=vT[:, part * 16:(part + 1) * 16, :],
                            in_=V_all[l][part * 2048:(part + 1) * 2048, hd * 128:(hd + 1) * 128].rearrange("(k p) e -> p k e", p=128)),
                            reads=[r_Vall[l]], writes=[r_vTp[part]], dma=r_vTp[part])
                P.emit("sp", lambda h, hd=hd: h.dma_start(out=kT[:, 4096:4352],
                                                        in_=Kc_loc[l].rearrange("p (h w) -> p h w", h=4)[:, hd, :]),
                       reads=[r_Kc[l]], writes=[r_kTp[2]], dma=r_kTp[2])
                P.emit("sp", lambda h, hd=hd: h.dma_start(
                    out=vT[:, 32:34, :], in_=Vc_loc[l][:, hd * 128:(hd + 1) * 128].rearrange("(k p) e -> p k e", p=128)),
                    reads=[r_Vc[l]], writes=[r_vTp[2]], dma=r_vTp[2])

                def pv_step(ki):
                    kt = kts[ki]
                    first, last = ki == 0, ki == len(kts) - 1
                    for m in range(2):
                        pj = (ki % 2) * 2 + m
                        mm(4 + 2 * m, ps[4 + 2 * m][:, 0:n], vT[:, kt, :], pT[pj][:, 0:n], first, last, [r_vTp[min(kt // 16, 2)], r_pT[pj]])
                        mm(5 + 2 * m, ps[5 + 2 * m][:, 0:n], ones_bf[:], pT[pj][:, 0:n], first, last, [r_const, r_pT[pj]])

                for ki, kt in enumerate(kts):
                    pb = pairA()
                    for m in range(2):
                        qsrc = q0T if m == 0 else q1T
                        mm(pb + m, ps[pb + m][:, 0:n], kT[:, kt * 128:(kt + 1) * 128], qsrc[:, hd, 0:n], True, True,
                           [r_kTp[min(kt // 16, 2)], r_q])
                    pj0 = (ki % 2) * 2
                    if n == 512:
                        P.emit("act", lambda h, pb=pb, pj0=pj0: h.activation(
                            out=pT2[pj0 // 2], in_=ps_all[:, pb * 512:(pb + 2) * 512], func=AF.Exp, bias=0.0, scale=0.125),
                            reads=[r_ps[pb], r_ps[pb + 1]], writes=[r_pT[pj0], r_pT[pj0 + 1]])
                    else:
                        for m in range(2):
                            act(pT[pj0 + m][:, 0:n], ps[pb + m][:, 0:n], AF.Exp, [r_ps[pb + m]], [r_pT[pj0 + m]], scale=0.125)
                    if ki > 0:
                        pv_step(ki - 1)
                    if ki == min(6, len(kts) - 1) and pending:
                        pending.pop(0)()
                    fn_ = hooks.pop((hd, ki), None)
                    if fn_ is not None:
                        active.append(fn_())
                    for g_ in list(active):
                        try:
                            next(g_)
                        except StopIteration:
                            active.remove(g_)
                pv_step(len(kts) - 1)
                ev = [ntmp2(), ntmp2()]
                if n == 512:
                    P.emit("dve", lambda h, ev=ev: h.tensor_copy(out=tmp_all[:, ev[0] * 512:(ev[0] + 2) * 512],
                                                                in_=ps_all[:, 4 * 512:6 * 512]),
                           reads=[r_ps[4], r_ps[5]], writes=[r_tmp[ev[0]], r_tmp[ev[0] + 1]])
                    P.emit("act", lambda h, ev=ev: h.copy(out=tmp_all[:, ev[1] * 512:(ev[1] + 2) * 512],
                                                         in_=ps_all[:, 6 * 512:8 * 512]),
                           reads=[r_ps[6], r_ps[7]], writes=[r_tmp[ev[1]], r_tmp[ev[1] + 1]])
                else:
                    for j_ in range(2):
                        cp(tmps[ev[0] + j_][:, 0:n], ps[4 + j_][:, 0:n], [r_ps[4 + j_]], [r_tmp[ev[0] + j_]])
                        cp(tmps[ev[1] + j_][:, 0:n], ps[6 + j_][:, 0:n], [r_ps[6 + j_]], [r_tmp[ev[1] + j_]], eng="act")
                o0, z0, o1, z1 = ev[0], ev[0] + 1, ev[1], ev[1] + 1
                recip(tmps[z0][:, 0:n], tmps[z0][:, 0:n], [r_tmp[z0]], [r_tmp[z0]])
                tt(tmps[o0][:, 0:n], tmps[o0][:, 0:n], tmps[z0][:, 0:n], ALU.mult, [r_tmp[o0], r_tmp[z0]], [r_tmp[o0]])
                recip(tmps[z1][:, 0:n], tmps[z1][:, 0:n], [r_tmp[z1]], [r_tmp[z1]])
                tt(tmps[o1][:, 0:n], tmps[o1][:, 0:n], tmps[z1][:, 0:n], ALU.mult, [r_tmp[o1], r_tmp[z1]], [r_tmp[o1]])
                stt(tmps[o0][:, 0:n], tmps[o1][:, 0:n], lamT[:, l, 0:1], tmps[o0][:, 0:n], ALU.mult, ALU.add,
                    [r_tmp[o1], r_tmp[o0], r_small], [r_tmp[o0]])

                def finish(hd=hd, o0=o0, rz2=z0):
                    act(sqs[:, 0:n], tmps[o0][:, 0:n], AF.Square, [r_tmp[o0]], [r_sqs])
                    pi = pairA()
                    mm(pi, ps[pi][:, 0:n], ones_bf[:], sqs[:, 0:n], True, True, [r_sqs, r_const])
                    act(tmps[rz2][:, 0:n], ps[pi][:, 0:n], AF.Sqrt, [r_ps[pi]], [r_tmp[rz2]], bias=EPS, scale=1.0 / 128)
                    recip(tmps[rz2][:, 0:n], tmps[rz2][:, 0:n], [r_tmp[rz2]], [r_tmp[rz2]])
                    stt(ocT[:, hd, 0:n], tmps[o0][:, 0:n], lamT[:, l, 1:2], tmps[rz2][:, 0:n], ALU.mult, ALU.mult,
                        [r_tmp[o0], r_tmp[rz2], r_small], [r_oc])

                pending.append(finish)
                if l == 0 and latent:
                    ada_some(2 if (t == 0 and hd < 2) else 1)
            if hooks or active:
                while pending:
                    pending.pop(0)()
                for g_ in active:
                    for _ in g_:
                        pass
                for key_ in sorted(hooks):
                    for _ in hooks[key_]():
                        pass
            return pending

        def load_window(t, l):
            ti, t0, n = TILES[t]
            memset(win[:], 0.0, [r_win], eng="pool")
            if ti == 4:
                src = CVc_loc[l].rearrange("p (c w) -> p c w", c=4)
                P.emit("sp", lambda h: h.dma_start(out=win[:, :, 16:16 + n], in_=src), reads=[r_CVc[l]], writes=[r_win], dma=r_win)
                return
            src = CV_loc[l].rearrange("p (c w) -> p c w", c=4)
            lo = max(t0 - 16, 0)
            hi = min(t0 + n + 16, NT)
            fns = [lambda h: h.dma_start(out=win[:, :, 16 + (lo - t0):16 + (hi - t0)], in_=src[:, :, lo:hi])]
            hall = H_all[l].rearrange("q (c w) -> q c w", c=4)
            if ti == 0:
                fns.append(lambda h: h.dma_start(out=win[:, :, 0:16], in_=hall[0:128, :, 16:32]))
            if ti == 3:
                fns.append(lambda h: h.dma_start(out=win[:, :, 16 + n:32 + n], in_=hall[128:256, :, 0:16]))
            P.emit("sp", None, reads=[r_CV[l], r_Hall[l]], writes=[r_win], dma=r_win, multi=fns)
            if ti == 0:
                ts(win[:, :, 0:16], win[:, :, 0:16], pv("mleft"), None, ALU.mult, None, [r_win, r_pvec], [r_win])
            if ti == 3:
                ts(win[:, :, 16 + n:32 + n], win[:, :, 16 + n:32 + n], pv("mright"), None, ALU.mult, None,
                   [r_win, r_pvec], [r_win])

        def mixer_b(t, l):
            ti, t0, n = TILES[t]
            latent = ti < 4
            col = 0 if latent else 1
            modnorm(t, gsT[:, l, 1, :, col], modT[:, l, 24:32, col], mixer=True)
            if latent:
                P.emit("sp", None, writes=[r_rope], dma=r_rope, multi=[
                    lambda h: h.dma_start(out=ropec[:], in_=cos_d[:, t0:t0 + 512]),
                    lambda h: h.dma_start(out=ropes[:], in_=sin_d[:, t0:t0 + 512])])
            si, wv = load_slot(w_in_v[l][:, :, Q_OFF:Q_OFF + 512], v512)
            memset(q0T[:, :, :], 0.0, [r_q], eng="pool")
            memset(q1T[:, :, :], 0.0, [r_q], eng="pool")
            for hd in range(4):
                pi = proj(si, wv, hd, n)
                if latent:
                    a, b = rope_to(None, pi, n, r_q)
                    tt(q0T[0:64, hd, 0:n], tmps[a][0:64, 0:n], tmps[b][0:64, 0:n], ALU.add, [r_tmp[a], r_tmp[b]], [r_q])
                    tt(q1T[64:128, hd, 0:n], tmps[a][64:128, 0:n], tmps[b][64:128, 0:n], ALU.add, [r_tmp[a], r_tmp[b]], [r_q])
                else:
                    cp(q0T[0:64, hd, 0:n], ps[pi][0:64, 0:n], [r_ps[pi]], [r_q])
                    cp(q1T[64:128, hd, 0:n], ps[pi][64:128, 0:n], [r_ps[pi]], [r_q])
            if t == 0 and l == 0:
                for hd in range(4):
                    tap(hd, q0T[:, hd, :], r_q)
                    tap(4 + hd, q1T[:, hd, :], r_q)
                tap(32, hT[:, 0, :], r_hTc[0][0])
            load_window(t, l)
            si, wv = load_slot(w_in_v[l][:, :, 0:256], v256)
            cao = PV["conv_a_w"][0] + l * 6
            for k2 in range(2):
                pb = proj(si, wv, k2, n)
                a = ntmp()
                ts(tmps[a][:, 0:n], win[:, k2, 15:15 + n], pvec[:, cao + k2:cao + k2 + 1], None, ALU.mult, None,
                   [r_win, r_pvec], [r_tmp[a]])
                for j in (1, 2):
                    stt(tmps[a][:, 0:n], win[:, k2, 15 + j:15 + j + n], pvec[:, cao + 2 * j + k2:cao + 2 * j + k2 + 1],
                        tmps[a][:, 0:n], ALU.mult, ALU.add, [r_win, r_pvec, r_tmp[a]], [r_tmp[a]])
                tt(aT[:, k2, 0:n], tmps[a][:, 0:n], ps[pb][:, 0:n], ALU.mult, [r_tmp[a], r_ps[pb]], [r_a])
            cdo = PV["conf_dw"][0] + l * 62
            for k2 in range(2):
                eng = "dve"
                a = ntmp()
                dbo = PV["conf_db"][0] + l * 2 + k2
                ts(tmps[a][:, 0:n], win[:, 2 + k2, 1:1 + n], pvec[:, cdo + k2:cdo + k2 + 1], pvec[:, dbo:dbo + 1],
                   ALU.mult, ALU.add, [r_win, r_pvec], [r_tmp[a]], eng=eng)
                for j in range(1, 31):
                    stt(tmps[a][:, 0:n], win[:, 2 + k2, 1 + j:1 + j + n], pvec[:, cdo + 2 * j + k2:cdo + 2 * j + k2 + 1],
                        tmps[a][:, 0:n], ALU.mult, ALU.add, [r_win, r_pvec, r_tmp[a]], [r_tmp[a]], eng=eng)
                cp(dT[:, k2, 0:n], tmps[a][:, 0:n], [r_tmp[a]], [r_d])
            P.fence([r_sq8] + r_vnm + r_csq)
            ntc = n // 128
            nb = (ntc + 1) // 2
            vnm4 = [merged[:, tc, :].rearrange("p (g c) -> p g c", g=4) for tc in range(4)]
            csq = [merged[:, 4, :], merged[:, 5, :]]
            hk = {}

            def gelu_steps(src, w_, out_ap, out_res, src_res, g):
                tt(tmps[g][:, 0:w_], src, src, ALU.mult, [src_res], [r_tmp[g]])
                ts(tmps[g][:, 0:w_], tmps[g][:, 0:w_], 0.044715, 1.0, ALU.mult, ALU.add, [r_tmp[g]], [r_tmp[g]])
                tt(tmps[g][:, 0:w_], tmps[g][:, 0:w_], src, ALU.mult, [r_tmp[g], src_res], [r_tmp[g]])
                yield
                act(tmps[g][:, 0:w_], tmps[g][:, 0:w_], AF.Sigmoid, [r_tmp[g]], [r_tmp[g]], scale=1.5957691216057308)
                yield
                tt(out_ap, tmps[g][:, 0:w_], src, ALU.mult, [r_tmp[g], src_res], [out_res])

            def sgu_proj():
                si, wv = load_slot(w_in_v[l][:, :, B_OFF:B_OFF + 512], v512)
                pu = pairA()
                for k2 in range(2):
                    for c in range(DC):
                        mm(pu + k2, ps[pu + k2][:, 0:n], wv[:, c, k2 * 128:(k2 + 1) * 128], hT[:, c, 0:n], c == 0, c == DC - 1,
                           [r_slot[si], r_hTc[0][c]])
                pv_ = pairA()
                for bk in range(nb):
                    ntcb = min(2, ntc - 2 * bk)
                    for tcl in range(ntcb):
                        tc = 2 * bk + tcl
                        for c in range(DC):
                            mm(pv_ + bk, ps[pv_ + bk][:, tcl * 256:(tcl + 1) * 256], hT[:, c, tc * 128:(tc + 1) * 128],
                               wv[:, c, 256:512], c == 0, c == DC - 1, [r_slot[si], r_hTc[0][c]])
                memset(merged[:, 0:4, :], 0.0, r_vnm, eng="pool")
                cu = [ntmp(), ntmp()]
                for k2 in range(2):
                    cp(tmps[cu[k2]][:, 0:n], ps[pu + k2][:, 0:n], [r_ps[pu + k2]], [r_tmp[cu[k2]]])
                vtb = []
                for bk in range(nb):
                    ntcb = min(2, ntc - 2 * bk)
                    vb = ntmp()
                    vtb.append(vb)
                    cp(tmps[vb][:, 0:ntcb * 256], ps[pv_ + bk][:, 0:ntcb * 256], [r_ps[pv_ + bk]], [r_tmp[vb]])
                g = ntmp()
                sqs_ = [ntmp(), ntmp()]
                yield
                for k2 in range(2):
                    yield from gelu_steps(tmps[cu[k2]][:, 0:n], n, bT[:, k2, 0:n], r_b, r_tmp[cu[k2]], g)
                for bk in range(nb):
                    ntcb = min(2, ntc - 2 * bk)
                    vb = vtb[bk]
                    yield from gelu_steps(tmps[vb][:, 0:ntcb * 256], ntcb * 256, tmps[vb][:, 0:ntcb * 256], r_tmp[vb],
                                          r_tmp[vb], g)
                    P.emit("dve", lambda h, vb=vb, bk=bk, ntcb=ntcb: h.reduce_sum(
                        out=mv8[:, 2 * bk:2 * bk + ntcb], in_=tmps[vb][:, 0:ntcb * 256].rearrange("p (a b) -> p a b", a=ntcb),
                        axis=mybir.AxisListType.X), reads=[r_tmp[vb]], writes=[r_vt])
                ts(mv8[:, 0:4], mv8[:, 0:4], -1.0 / 256, None, ALU.mult, None, [r_vt], [r_vt])
                for bk in range(nb):
                    vb = vtb[bk]
                    ntcb = min(2, ntc - 2 * bk)
                    for tcl in range(ntcb):
                        tc = 2 * bk + tcl
                        ts(tmps[vb][:, tcl * 256:(tcl + 1) * 256], tmps[vb][:, tcl * 256:(tcl + 1) * 256], mv8[:, tc:tc + 1],
                           None, ALU.add, None, [r_tmp[vb], r_vt], [r_tmp[vb]])
                    sq_ = sqs_[bk]
                    tt(tmps[sq_][:, 0:ntcb * 256], tmps[vb][:, 0:ntcb * 256], tmps[vb][:, 0:ntcb * 256], ALU.mult,
                       [r_tmp[vb]], [r_tmp[sq_]])
                    P.emit("dve", lambda h, sq_=sq_, bk=bk, ntcb=ntcb: h.reduce_sum(
                        out=mv8[:, 4 + 2 * bk:4 + 2 * bk + ntcb],
                        in_=tmps[sq_][:, 0:ntcb * 256].rearrange("p (a b) -> p a b", a=ntcb),
                        axis=mybir.AxisListType.X), reads=[r_tmp[sq_]], writes=[r_vt])
                yield
                yield
                act(mv8[:, 4:8], mv8[:, 4:8], AF.Sqrt, [r_vt], [r_vt], bias=EPS, scale=1.0 / 256)
                yield
                recip(mv8[:, 4:8], mv8[:, 4:8], [r_vt], [r_vt])
                for tc in range(ntc):
                    vb = vtb[tc // 2]
                    tcl = tc % 2
                    for g_ in range(4):
                        gg = g_ % 2
                        ts(vnm4[tc][:, g_, gg * 64:(gg + 1) * 64], tmps[vb][:, tcl * 256 + g_ * 64:tcl * 256 + (g_ + 1) * 64],
                           mv8[:, 4 + tc:5 + tc], None, ALU.mult, None, [r_tmp[vb], r_vt], [r_vnm[tc]])

            def sgu_spatial():
                pS = pairA()
                for tc in range(ntc):
                    for k2 in range(2):
                        for gg in range(2):
                            g_ = 2 * k2 + gg
                            mm(pS + k2, ps[pS + k2][:, tc * 128:(tc + 1) * 128], vnm4[tc][:, g_, :], sguw[:, l, g_, :],
                               gg == 0, gg == 1, [r_vnm[tc], r_const])
                for k2 in range(2):
                    a = ntmp()
                    tt(tmps[a][:, 0:n], ps[pS + k2][:, 0:n], bias_rep[:, k2, 0:n], ALU.add, [r_ps[pS + k2], r_small],
                       [r_tmp[a]])
                    tt(bT[:, k2, 0:n], tmps[a][:, 0:n], bT[:, k2, 0:n], ALU.mult, [r_tmp[a], r_b], [r_b])
                yield

            def conf_ln():
                hacc = [ntmp(), ntmp()]
                rz = ntmp()
                pm = pairA()
                for k2 in range(2):
                    mm(pm, ps[pm][:, 0:n], ones_bf[:], dT[:, k2, 0:n], k2 == 0, k2 == 1, [r_d, r_const])
                for k2 in range(2):
                    a = hacc[k2]
                    stt(tmps[a][:, 0:n], ps[pm][:, 0:n], -1.0 / 256, dT[:, k2, 0:n], ALU.mult, ALU.add,
                        [r_ps[pm], r_d], [r_tmp[a]])
                    tt(csq[k2][:, 0:n], tmps[a][:, 0:n], tmps[a][:, 0:n], ALU.mult, [r_tmp[a]], [r_csq[k2]])
                yield
                yield
                yield
                pvv = pairA()
                for k2 in range(2):
                    mm(pvv, ps[pvv][:, 0:n], ones_bf[:], csq[k2][:, 0:n], k2 == 0, k2 == 1, [r_csq[k2], r_const])
                yield
                act(tmps[rz][:, 0:n], ps[pvv][:, 0:n], AF.Sqrt, [r_ps[pvv]], [r_tmp[rz]], bias=EPS, scale=1.0 / 256)
                recip(tmps[rz][:, 0:n], tmps[rz][:, 0:n], [r_tmp[rz]], [r_tmp[rz]])
                for k2 in range(2):
                    a = hacc[k2]
                    go = PV["conf_ln_g"][0] + l * 2 + k2
                    stt(tmps[a][:, 0:n], tmps[a][:, 0:n], pvec[:, go:go + 1], tmps[rz][:, 0:n], ALU.mult, ALU.mult,
                        [r_tmp[a], r_tmp[rz], r_pvec], [r_tmp[a]])
                yield
                yield
                for k2 in range(2):
                    a = hacc[k2]
                    bo = PV["conf_ln_b"][0] + l * 2 + k2
                    act(dT[:, k2, 0:n], tmps[a][:, 0:n], AF.Silu, [r_tmp[a], r_pvec], [r_d], bias=pvec[:, bo:bo + 1],
                        scale=1.0)

            hk[(1, 12)] = conf_ln
            hk[(2, 7)] = sgu_proj
            hk[(3, 8)] = sgu_spatial
            late = attention(t, l, hk)
            if t == 0 and l == 0:
                for hd in range(4):
                    tap(8 + hd, ocT[:, hd, :], r_oc)
            P.fence([r_sq8] + r_vnm + r_csq)
            if t == 0 and l == 0:
                for k2 in range(2):
                    tap(12 + k2, aT[:, k2, :], r_a)
                    tap(14 + k2, bT[:, k2, :], r_b)
                    tap(16 + k2, dT[:, k2, :], r_d)
                for c4 in range(4):
                    tap(28 + c4, win[:, c4, 0:512], r_win)
            wa = w_a_d[l].rearrange("(c p) d -> p c d", p=128)
            wb = w_b_d[l].rearrange("(c p) d -> p c d", p=128)
            wc = w_c_d[l].rearrange("(c p) d -> p c d", p=128)
            wd = w_d_d[l].rearrange("(c p) d -> p c d", p=128)
            gview = w_in_v[l][:, :, G_OFF:G_OFF + 4096].rearrange("p c (i d) -> p c i d", i=4)
            branches = [(aT, r_a, 2), (bT, r_b, 2), (ocT, r_oc, 4), (dT, r_d, 2)]
            for dc in range(DC):
                i = next_slot()
                gdst = slots[i][:, 0:4096].rearrange("p (c i f) -> p c i f", c=DC, i=4)
                P.emit("pool", None, writes=[r_slot[i]], dma=r_slot[i], multi=[
                    (lambda h, dc=dc, gdst=gdst, bi=bi: h.dma_start(out=gdst[:, :, bi, :],
                                                                   in_=gview[:, :, bi, dc * 128:(dc + 1) * 128]))
                    for bi in range(4)])
                j = next_slot()
                bdst = slots[j][:, 0:1280].rearrange("p (c f) -> p c f", c=10)
                P.emit("pool", None, writes=[r_slot[j]], dma=r_slot[j], multi=[
                    lambda h, dc=dc, bdst=bdst: h.dma_start(out=bdst[:, 0:2, :], in_=wa[:, :, dc * 128:(dc + 1) * 128]),
                    lambda h, dc=dc, bdst=bdst: h.dma_start(out=bdst[:, 2:4, :], in_=wb[:, :, dc * 128:(dc + 1) * 128]),
                    lambda h, dc=dc, bdst=bdst: h.dma_start(out=bdst[:, 4:8, :], in_=wc[:, :, dc * 128:(dc + 1) * 128]),
                    lambda h, dc=dc, bdst=bdst: h.dma_start(out=bdst[:, 8:10, :], in_=wd[:, :, dc * 128:(dc + 1) * 128])])
                koff = [0, 2, 4, 8]
                acc = ntmp()
                for bi, (bt_, br_, nk) in enumerate(branches):
                    if bi == 2:
                        while late:
                            late.pop(0)()
                    pg = psA()
                    for c in range(DC):
                        mm(pg, ps[pg][:, 0:n], gdst[:, c, bi, :], hT[:, c, 0:n], c == 0, c == DC - 1, [r_slot[i], r_hTc[0][c]])
                    act(sgs[bi][:, 0:n], ps[pg][:, 0:n], AF.Sigmoid, [r_ps[pg]], [r_sgs[bi]])
                    py = psB()
                    for kc in range(nk):
                        mm(py, ps[py][:, 0:n], bdst[:, koff[bi] + kc, :], bt_[:, kc, 0:n], kc == 0, kc == nk - 1,
                           [r_slot[j], br_])
                    if bi == 0:
                        tt(tmps[acc][:, 0:n], ps[py][:, 0:n], sgs[bi][:, 0:n], ALU.mult, [r_ps[py], r_sgs[bi]], [r_tmp[acc]])
                    else:
                        b2 = ntmp()
                        tt(tmps[b2][:, 0:n], ps[py][:, 0:n], sgs[bi][:, 0:n], ALU.mult, [r_ps[py], r_sgs[bi]], [r_tmp[b2]])
                        if bi < 3:
                            tt(tmps[acc][:, 0:n], tmps[acc][:, 0:n], tmps[b2][:, 0:n], ALU.add, [r_tmp[acc], r_tmp[b2]],
                               [r_tmp[acc]])
                        else:
                            tt(merged[:, dc, 0:n], tmps[acc][:, 0:n], tmps[b2][:, 0:n], ALU.add, [r_tmp[acc], r_tmp[b2]],
                               [r_sq8])
            if t == 0 and l == 0:
                for dc in range(DC):
                    tap(18 + dc, merged[:, dc, :], r_sq8)
            xv = xview(t)
            wo = w_o_d[l].rearrange("(c p) d -> p c d", p=128)
            for half in range(2):
                si, wv = load_slot(wo[:, :, half * 512:(half + 1) * 512], v512)
                for m in range(4):
                    dc = half * 4 + m
                    pi = psB()
                    for c in range(DC):
                        mm(pi, ps[pi][:, 0:n], wv[:, c, m * 128:(m + 1) * 128], merged[:, c, 0:n], c == 0, c == DC - 1,
                           [r_slot[si], r_sq8])
                    stt(xv[:, dc, :], ps[pi][:, 0:n], modT[:, l, 40 + dc:41 + dc, col], xv[:, dc, :], ALU.mult, ALU.add,
                        [r_ps[pi], r_small, r_x[t]], [r_x[t]])

        def final_norm(t):
            ti, t0, n = TILES[t]
            xv = xview(t)
            for c in range(DC):
                act(sq8[:, c, 0:n], xv[:, c, :], AF.Square, [r_x[t]], [r_sq8])
            pi = ps_any()
            for c in range(DC):
                mm(pi, ps[pi][:, 0:n], ones_bf[:], sq8[:, c, 0:n], c == 0, c == DC - 1, [r_sq8, r_const])
            ri = ntmp()
            act(tmps[ri][:, 0:n], ps[pi][:, 0:n], AF.Sqrt, [r_ps[pi]], [r_tmp[ri]], bias=EPS, scale=1.0 / D)
            recip(tmps[ri][:, 0:n], tmps[ri][:, 0:n], [r_tmp[ri]], [r_tmp[ri]])
            for c in range(DC):
                stt(xv[:, c, :], xv[:, c, :], fgT[:, c:c + 1], tmps[ri][:, 0:n], ALU.mult, ALU.mult,
                    [r_x[t], r_small, r_tmp[ri]], [r_x[t]])
            P.emit("sp", lambda h, t0=t0: h.dma_start(out=yT_d.rearrange("(c p) t -> p c t", p=128)[:, :, t0:t0 + 512],
                                                    in_=xT[:, :, t0:t0 + 512]),
                   reads=[r_x[t]], writes=[r_out], dma=r_out)

        stage = {"n": 0}
        big_res = r_kTp + r_vTp + [r_win, r_q, r_oc, r_a, r_b, r_d, r_kst, r_vst, r_cvst] + r_pT + r_gT

        def go():
            stage["n"] += 1
            P.fence(big_res)
            P.fence(r_hT_all)
            return stage["n"] <= DEBUG_STOP

        for l in range(L):
            last = l == L - 1
            so = PV["sgu_b"][0] + l * 256
            for k2 in range(2):
                for r4 in range(4):
                    cp(bias_rep[:, k2, r4 * 128:(r4 + 1) * 128], pvec[:, so + k2 * 128:so + (k2 + 1) * 128],
                       [r_pvec], [r_small])
            for grp_ in ([4, 0, 1], [2, 3]):
                if go():
                    ffn(grp_, l, 0)
                for t in grp_:
                    if t != 4 and go():
                        mixer_a(t, l)
            if go():
                exchange(l)
            if go():
                mixer_a(4, l)
            for grp_ in (([0, 1], [2, 3]) if last else ([4, 0, 1], [2, 3])):
                for t in grp_:
                    if go():
                        mixer_b(t, l)
                if go():
                    ffn(grp_, l, 1)
                if last:
                    for t in grp_:
                        final_norm(t)
        while ada_todo:
            ada_some(1)

        if DEBUG_STOP < 10 ** 8:
            for t in range(4):
                final_norm(t)
        P.emit("sp", lambda h: h.nop(), reads=[r_out, r_dbg])
        P.finalize()
    return nc


def _rope_tables(half):
    nf = 16
    t = np.arange(NT, dtype=np.float32) + np.float32(half * NT)
    row = np.floor(t / 64).astype(np.float32)
    colp = (t - row * 64).astype(np.float32)
    inv = (np.float32(10000.0) ** (-np.arange(nf, dtype=np.float32) / np.float32(nf))).astype(np.float32)
    cosT = np.zeros((128, NT), np.float32)
    sinT = np.zeros((128, NT), np.float32)
    for p in range(128):
        r = p % 64
        axis, hf, f = r // 32, (r % 32) // 16, r % 16
        pos = row if axis == 0 else colp
        ang = (pos * inv[f]).astype(np.float32)
        cosT[p] = np.cos(ang)
        sinT[p] = np.sin(ang) * (-1.0 if hf == 0 else 1.0)
    return cosT, sinT


def _rot_matrix():
    R = np.zeros((128, 128), np.float32)
    for p in range(128):
        hf = (p % 32) // 16
        partner = p + 16 if hf == 0 else p - 16
        R[partner, p] = 1.0
    return R


def _pack_pvec(inp, half):
    pv = np.zeros((128, NPV), np.float32)

    def put(name, arr):
        o, w = PV[name]
        arr = np.asarray(arr, np.float32)
        assert arr.shape == (128, w), (name, arr.shape, w)
        pv[:, o:o + w] = arr

    def fm(v):
        v = np.asarray(v, np.float32)
        lead = v.shape[:-1]
        n = v.shape[-1] // 128
        a = v.reshape(lead + (n, 128))
        a = np.moveaxis(a, -1, 0)
        return a.reshape(128, -1)

    put("b_ada", fm(inp["b_ada"]))
    put("norm_g", fm(inp["norm_g"]))
    put("final_g", fm(inp["final_g"]))
    put("conv_a_w", fm(inp["conv_a_w"]))
    put("conf_dw", fm(inp["conf_dw"]))
    put("conf_db", fm(inp["conf_db"]))
    put("conf_ln_g", fm(inp["conf_ln_g"]))
    put("conf_ln_b", fm(inp["conf_ln_b"]))
    put("subln_g", fm(inp["subln_g"]))
    put("lam_p", np.broadcast_to(np.asarray(inp["lam_p"], np.float32).reshape(1, L * 256), (128, L * 256)))
    sb_ = np.asarray(inp["sgu_b"], np.float32)
    rep = np.zeros((128, L, 2, 128), np.float32)
    for l in range(L):
        for g in range(4):
            rep[(g % 2) * 64:(g % 2) * 64 + 64, l, g // 2, :] = sb_[l, g][None, :]
    put("sgu_b", rep.reshape(128, -1))
    put("mleft", np.full((128, 1), 1.0 if half == 1 else 0.0, np.float32))
    put("mright", np.full((128, 1), 1.0 if half == 0 else 0.0, np.float32))
    return pv


_NC_CACHE = {}


def kernel(**inp):
    if "nc" not in _NC_CACHE:
        _NC_CACHE["nc"] = build_program()
    nc = _NC_CACHE["nc"]
    x = np.asarray(inp["x"], np.float32)
    ctx = np.asarray(inp["ctx"], np.float32)
    c = np.asarray(inp["c"], np.float32)
    c_ctx = np.asarray(inp["c_ctx"], np.float32)
    rot = _rot_matrix()
    sgu_wT = np.ascontiguousarray(np.swapaxes(np.asarray(inp["sgu_w"], np.float32), 2, 3))
    shared = {k: np.ascontiguousarray(np.asarray(inp[k], np.float32)) for k in
              ("w_ada", "ffn1_w1", "ffn1_w3", "ffn1_w2", "ffn2_w1", "ffn2_w3", "ffn2_w2", "w_in",
               "w_a_out", "w_b_out", "w_c_out", "w_d_out", "w_o")}
    tabs = [_rope_tables(0), _rope_tables(1)]
    in_maps = []
    for core in range(8):
        b, half = core // 2, core % 2
        m = dict(shared)
        m["xT"] = np.ascontiguousarray(x[b, half * NT:(half + 1) * NT, :].T)
        m["cT"] = np.ascontiguousarray(ctx[b].T)
        m["cvec"] = np.ascontiguousarray(np.stack([c[b], c_ctx], axis=1))
        m["pvec"] = _pack_pvec(inp, half)
        m["rot"] = rot
        m["cosT"], m["sinT"] = tabs[half]
        m["sgu_wT"] = sgu_wT
        in_maps.append(m)
    res = run_bass_kernel_spmd(nc, in_maps, core_ids=list(range(8)))
    if DEBUG_TAPS:
        _NC_CACHE["dbg"] = [np.asarray(res.results[core]["dbg"]) for core in range(8)]
    out = np.empty((4, 2 * NT, D), np.float32)
    for core in range(8):
        b, half = core // 2, core % 2
        out[b, half * NT:(half + 1) * NT, :] = np.asarray(res.results[core]["yT"]).T
    return out
```

```python
import numpy as np
import concourse.bass as bass
import concourse.mybir as mybir
from concourse.bass_utils import run_bass_kernel_spmd
from contextlib import ExitStack

F32 = mybir.dt.float32
BF16 = mybir.dt.bfloat16
AF = mybir.ActivationFunctionType
ALU = mybir.AluOpType

D = 1024
DC = 8
L = 2
NT = 2048
NCX = 256
DFF = 2816
FC = 22
IN_COLS = 7424
A_OFF, B_OFF, Q_OFF, K_OFF, V_OFF, D_OFF, G_OFF = 0, 768, 1280, 1792, 2304, 2816, 3328
EPS = 1e-6
KW = 2048
NKT = 34

SAME_ENGINE_SYNC = True
DEBUG_STOP = 10 ** 9
DEBUG_TAPS = False


class Res:
    __slots__ = ("name", "w", "r", "dsem", "dcount")

    def __init__(self, name):
        self.name = name
        self.w = {}
        self.r = {}
        self.dsem = None
        self.dcount = 0


class Prog:
    ENGS = ("pe", "act", "dve", "pool", "sp")

    def __init__(self, nc, stack):
        self.nc = nc
        self.stack = stack
        self.streams = {e: [] for e in self.ENGS}
        self.seen = {e: {} for e in self.ENGS}
        self.esem = {}
        for e in self.ENGS:
            self.esem[e] = stack.enter_context(nc.semaphore("es_" + e))
        self.nres = 0
        self.dma_res = []
        self.bar_seen = {}

    def res(self, name=None):
        self.nres += 1
        return Res(name or f"r{self.nres}")

    def emit(self, eng, fn, reads=(), writes=(), dma=None, inc=16, multi=None):
        deps = {}
        for r in reads:
            for k, v in r.w.items():
                if deps.get(k, -1) < v:
                    deps[k] = v
        for r in writes:
            for d in (r.w, r.r):
                for k, v in d.items():
                    if deps.get(k, -1) < v:
                        deps[k] = v
        seen = self.seen[eng]
        waits = []
        for k, v in deps.items():
            if k == eng and dma is None:
                if eng == "pe" or not SAME_ENGINE_SYNC:
                    continue
            if seen.get(k, -1) >= v:
                continue
            seen[k] = v
            waits.append((k, v))
        fns = multi if multi is not None else [fn]
        for i, f in enumerate(fns):
            idx = len(self.streams[eng])
            rec = {"fn": f, "waits": waits if i == 0 else [], "signal": False, "dma": dma, "inc": inc}
            self.streams[eng].append(rec)
            if dma is not None:
                if dma.dsem is None:
                    dma.dsem = self.stack.enter_context(self.nc.semaphore("ds_" + dma.name))
                    self.dma_res.append(dma)
                dma.dcount += inc
        if dma is not None:
            key, val = ("d", dma), dma.dcount
        else:
            key, val = eng, idx
        for r in reads:
            r.r[key] = max(r.r.get(key, -1), val)
        for r in writes:
            r.w = {key: val}
            r.r = {}

    def barrier(self):
        b = Res("bar")
        for e in self.ENGS:
            st_ = self.streams[e]
            for i in range(len(st_) - 1, -1, -1):
                if st_[i]["dma"] is None:
                    b.w[e] = i
                    break
        for r in self.dma_res:
            if r.dcount > self.bar_seen.get(r, 0):
                b.w[("d", r)] = r.dcount
                self.bar_seen[r] = r.dcount
        for e in self.ENGS:
            self.emit(e, lambda h: h.nop(), reads=[b])

    def fence(self, group):
        deps = {}
        for r in group:
            for d in (r.w, r.r):
                for k, v in d.items():
                    if deps.get(k, -1) < v:
                        deps[k] = v
        for r in group:
            for k, v in deps.items():
                if r.w.get(k, -1) < v:
                    r.w[k] = v

    def finalize(self):
        nc = self.nc
        for e in self.ENGS:
            for rec in self.streams[e]:
                for k, v in rec["waits"]:
                    if isinstance(k, str):
                        assert self.streams[k][v]["dma"] is None
                        self.streams[k][v]["signal"] = True
        cnt = {}
        for e in self.ENGS:
            c = 0
            arr = []
            for rec in self.streams[e]:
                if rec["signal"]:
                    c += 1
                arr.append(c)
            cnt[e] = arr
        handles = {"pe": "tensor", "act": "scalar", "dve": "vector", "pool": "gpsimd", "sp": "sync"}
        with nc.Block() as block:
            for e in self.ENGS:
                stream = self.streams[e]
                if not stream:
                    continue

                def body(h, e=e, stream=stream):
                    for rec in stream:
                        for k, v in rec["waits"]:
                            if isinstance(k, str):
                                h.wait_ge(self.esem[k], cnt[k][v])
                            else:
                                h.wait_ge(k[1].dsem, v)
                        ins = rec["fn"](h)
                        if rec["dma"] is not None:
                            ins.then_inc(rec["dma"].dsem, rec["inc"])
                        elif rec["signal"]:
                            ins.then_inc(self.esem[e], 1)

                getattr(block, handles[e])(body)


PV = {}
_o = 0
for _n, _w in [("b_ada", L * 72), ("norm_g", L * 3 * 8), ("final_g", 8), ("conv_a_w", L * 3 * 2),
               ("conf_dw", L * 31 * 2), ("conf_db", L * 2), ("conf_ln_g", L * 2), ("conf_ln_b", L * 2),
               ("subln_g", L), ("lam_p", L * 256), ("sgu_b", L * 2 * 128), ("mleft", 1), ("mright", 1)]:
    PV[_n] = (_o, _w)
    _o += _w
NPV = _o


def build_program():
    nc = bass.Bass("TRN2", target_bir_lowering=False)

    def din(name, shape, dt=F32):
        return nc.dram_tensor(name, list(shape), dt, kind="ExternalInput").ap()

    xT_d = din("xT", [D, NT])
    cT_d = din("cT", [D, NCX])
    cvec_d = din("cvec", [D, 2])
    pvec_d = din("pvec", [128, NPV])
    rot_d = din("rot", [128, 128])
    cos_d = din("cosT", [128, NT])
    sin_d = din("sinT", [128, NT])
    w_ada_d = din("w_ada", [L, D, 9 * D])
    ffw = {}
    for nm in ("ffn1_w1", "ffn1_w3", "ffn2_w1", "ffn2_w3"):
        ffw[nm] = din(nm, [L, D, DFF])
    for nm in ("ffn1_w2", "ffn2_w2"):
        ffw[nm] = din(nm, [L, DFF, D])
    w_in_d = din("w_in", [L, D, IN_COLS])
    w_a_d = din("w_a_out", [L, 256, D])
    w_b_d = din("w_b_out", [L, 256, D])
    w_c_d = din("w_c_out", [L, 512, D])
    w_d_d = din("w_d_out", [L, 256, D])
    w_o_d = din("w_o", [L, D, D])
    sguw_d = din("sgu_wT", [L, 4, 128, 128])
    yT_d = nc.dram_tensor("yT", [D, NT], F32, kind="ExternalOutput").ap()
    dbg_d = nc.dram_tensor("dbg", [128, 40, 512], BF16, kind="ExternalOutput").ap() if DEBUG_TAPS else None

    def dscr(name, shape):
        return nc.dram_tensor(name, list(shape), BF16, kind="Internal").ap()

    K_loc = [dscr(f"K_loc{l}", [128, 4 * KW]) for l in range(L)]
    K_all = [dscr(f"K_all{l}", [256, 4 * KW]) for l in range(L)]
    V_loc = [dscr(f"V_loc{l}", [NT, 512]) for l in range(L)]
    V_all = [dscr(f"V_all{l}", [2 * NT, 512]) for l in range(L)]
    H_loc = [dscr(f"H_loc{l}", [128, 128]) for l in range(L)]
    H_all = [dscr(f"H_all{l}", [256, 128]) for l in range(L)]
    Kc_loc = [dscr(f"Kc_loc{l}", [128, 4 * NCX]) for l in range(L)]
    Vc_loc = [dscr(f"Vc_loc{l}", [NCX, 512]) for l in range(L)]
    CV_loc = [dscr(f"CV_loc{l}", [128, 4 * NT]) for l in range(L)]
    CVc_loc = [dscr(f"CVc_loc{l}", [128, 4 * NCX]) for l in range(L)]

    with ExitStack() as st:
        P = Prog(nc, st)

        def sb(name, shape, dt):
            return st.enter_context(nc.sbuf_tensor(name, list(shape), dt))

        xT = sb("xT_s", [128, DC, NT], F32)
        cT = sb("cT_s", [128, DC, NCX], F32)
        NSLOT = 4
        slots = [sb(f"wslot{i}", [128, 4096], BF16) for i in range(NSLOT)]
        hT = sb("hT", [128, DC, 1280], BF16)
        BIGN = 24192
        big = sb("big", [128, BIGN], BF16)
        NTMP = 7
        tmp_all = sb("tmp_all", [128, NTMP * 512], F32)
        tmps = [tmp_all[:, i * 512:(i + 1) * 512] for i in range(NTMP)]
        pvec = sb("pvec_s", [128, NPV], F32)
        cvec = sb("cvec_s", [128, DC, 2], F32)
        scb = sb("scb", [128, DC, 2], BF16)
        modT = sb("modT", [128, L, 72, 2], F32)
        gsT = sb("gsT", [128, L, 3, DC, 2], F32)
        ghT = sb("ghT", [128, L, 2, DC, 2], F32)
        fgT = sb("fgT", [128, DC], F32)
        lamT = sb("lamT", [128, L, 4], F32)
        lscr = sb("lscr", [128, 136], F32)
        ones_bf = sb("ones_bf", [128, 128], BF16)
        rot_bf = sb("rot_bf", [128, 128], BF16)
        sguw = sb("sguw", [128, L, 4, 128], BF16)
        bias_rep = sb("bias_rep", [128, 2, 512], F32)
        sqs = sb("sqs", [128, 512], BF16)
        mv8 = sb("mv8", [128, 8], F32)
        ps_all = st.enter_context(nc.psum_tensor("ps_all", [128, 4096], F32))
        ps = [ps_all[:, i * 512:(i + 1) * 512] for i in range(8)]

        def bigv(off, n):
            return big[:, off:off + n]
        gT = bigv(0, 11 * 1280).rearrange("p (c t) -> p c t", c=11)
        o = 0
        kT = bigv(o, NKT * 128); o += NKT * 128
        vT = bigv(o, NKT * 128).rearrange("p (k e) -> p k e", k=NKT); o += NKT * 128
        owin = o
        win = bigv(o, 4 * 544).rearrange("p (c t) -> p c t", c=4); o += 4 * 544
        oq = o
        q0T = bigv(o, 2048).rearrange("p (h t) -> p h t", h=4); o += 2048
        q1T = bigv(o, 2048).rearrange("p (h t) -> p h t", h=4); o += 2048
        pT = [bigv(o + i * 512, 512) for i in range(4)]
        pT2 = [bigv(o + i * 1024, 1024) for i in range(2)]; o += 2048
        ocT = bigv(owin, 2048).rearrange("p (h t) -> p h t", h=4)
        aT = bigv(o, 1024).rearrange("p (c t) -> p c t", c=2); o += 1024
        bT = bigv(o, 1024).rearrange("p (c t) -> p c t", c=2); o += 1024
        dT = bigv(o, 1024).rearrange("p (c t) -> p c t", c=2); o += 1024
        kst = bigv(oq, 2048).rearrange("p (h t) -> p h t", h=4)
        vst = bigv(oq + 2048, 2048).rearrange("p (k e) -> p k e", k=4)
        cvst = bigv(oq + 4096, 2048).rearrange("p (c t) -> p c t", c=4)
        sgs = pT
        o = max(o, 11 * 1280)
        sq8 = bigv(o, 4096).rearrange("p (c t) -> p c t", c=8)
        assert o + 4096 <= BIGN, o
        merged = sq8
        ropec = sb("ropec", [128, 512], F32)
        ropes = sb("ropes", [128, 512], F32)

        r_x = [P.res(f"x{t}") for t in range(5)]
        r_slot = [P.res(f"slot{i}") for i in range(NSLOT)]
        r_hTc = [[P.res(f"hT{i}_{c}") for c in range(DC)] for i in range(3)]
        r_hT_all = [r for lst in r_hTc for r in lst]
        r_gT = [P.res(f"gT{i}") for i in range(11)]
        r_tmp = [P.res(f"tmp{i}") for i in range(NTMP)]
        r_ps = [P.res(f"ps{i}") for i in range(8)]
        r_small = P.res("small")
        r_mod1 = P.res("mod1")
        r_pvec = P.res("pvec")
        r_cvec = P.res("cvec")
        r_const = P.res("const")
        r_kTp = [P.res(f"kT{i}") for i in range(3)]
        r_vTp = [P.res(f"vT{i}") for i in range(3)]
        r_kT, r_vT = r_kTp[2], r_vTp[2]
        r_win = P.res("win")
        r_q, r_oc, r_a, r_b, r_d = P.res("q"), P.res("oc"), P.res("a"), P.res("b"), P.res("d")
        r_pT = [P.res(f"pT{i}") for i in range(4)]
        r_kst, r_vst, r_cvst = P.res("kst"), P.res("vst"), P.res("cvst")
        r_sgs = r_pT
        r_sq8 = P.res("sq8")
        r_vnm = [P.res(f"vnm{i}") for i in range(4)]
        r_csq = [P.res("csq0"), P.res("csq1")]
        r_rope = P.res("rope")
        r_sqs = P.res("sqs")
        r_vn = P.res("vn")
        r_vt = P.res("vt")
        r_Kloc = [P.res(f"Kloc{l}") for l in range(L)]
        r_Kall = [P.res(f"Kall{l}") for l in range(L)]
        r_Vloc = [P.res(f"Vloc{l}") for l in range(L)]
        r_Vall = [P.res(f"Vall{l}") for l in range(L)]
        r_Hloc = [P.res(f"Hloc{l}") for l in range(L)]
        r_Hall = [P.res(f"Hall{l}") for l in range(L)]
        r_Kc = [P.res(f"Kc{l}") for l in range(L)]
        r_Vc = [P.res(f"Vc{l}") for l in range(L)]
        r_CV = [P.res(f"CV{l}") for l in range(L)]
        r_CVc = [P.res(f"CVc{l}") for l in range(L)]
        r_out = P.res("out")

        r_dbg = P.res("dbg")

        def tap(slot, ap, res, width=512):
            if not DEBUG_TAPS:
                return
            P.emit("sp", lambda h: h.dma_start(out=dbg_d[:, slot, 0:width], in_=ap), reads=[res], writes=[r_dbg], dma=r_dbg)

        state = {"slot": 0, "psa": 0, "psb": 0, "tmp": 0}

        def next_slot():
            i = state["slot"]
            state["slot"] = (i + 1) % NSLOT
            return i

        def psA():
            i = state["psa"]
            state["psa"] = (i + 1) % 4
            return i

        def psB():
            i = state["psb"]
            state["psb"] = (i + 1) % 4
            return 4 + i

        def pairA():
            i = state.get("pair", 0)
            state["pair"] = 2 - i
            return i

        def ntmp2():
            i = state["tmp"]
            if i + 1 >= NTMP:
                i = 0
            state["tmp"] = (i + 2) % NTMP
            return i

        def ps_any():
            i = state.get("psr", 0)
            state["psr"] = (i + 1) % 8
            return i

        def ntmp():
            i = state["tmp"]
            state["tmp"] = (i + 1) % NTMP
            return i

        def mm(pi, out, lhsT, rhs, start, stop, reads):
            P.emit("pe", lambda h: h.matmul(out, lhsT=lhsT, rhs=rhs, start=start, stop=stop),
                   reads=reads, writes=[r_ps[pi]])

        def act(out, in_, func, reads, writes, bias=0.0, scale=1.0):
            P.emit("act", lambda h: h.activation(out=out, in_=in_, func=func, bias=bias, scale=scale),
                   reads=reads, writes=writes)

        def tt(out, in0, in1, op, reads, writes, eng="dve"):
            P.emit(eng, lambda h: h.tensor_tensor(out=out, in0=in0, in1=in1, op=op), reads=reads, writes=writes)

        def ts(out, in0, s1, s2, op0, op1, reads, writes, eng="dve"):
            if s2 is None:
                P.emit(eng, lambda h: h.tensor_scalar(out=out, in0=in0, scalar1=s1, scalar2=None, op0=op0),
                       reads=reads, writes=writes)
            else:
                P.emit(eng, lambda h: h.tensor_scalar(out=out, in0=in0, scalar1=s1, scalar2=s2, op0=op0, op1=op1),
                       reads=reads, writes=writes)

        def stt(out, in0, scalar, in1, op0, op1, reads, writes, eng="dve"):
            P.emit(eng, lambda h: h.scalar_tensor_tensor(out=out, in0=in0, scalar=scalar, in1=in1, op0=op0, op1=op1),
                   reads=reads, writes=writes)

        def cp(out, in_, reads, writes, eng="dve"):
            if eng == "act":
                P.emit(eng, lambda h: h.copy(out=out, in_=in_), reads=reads, writes=writes)
            else:
                P.emit(eng, lambda h: h.tensor_copy(out=out, in_=in_), reads=reads, writes=writes)

        def recip(out, in_, reads, writes):
            P.emit("dve", lambda h: h.reciprocal(out=out, in_=in_), reads=reads, writes=writes)

        def memset(ap, val, writes, eng="pool"):
            P.emit(eng, lambda h: h.memset(ap, val), writes=writes)

        def load_slot(src_ap, view_fn):
            i = next_slot()
            dst = view_fn(slots[i])
            P.emit("pool", lambda h: h.dma_start(out=dst, in_=src_ap), writes=[r_slot[i]], dma=r_slot[i])
            return i, dst

        def pv(name, idx=0, n=1):
            o_, w_ = PV[name]
            return pvec[:, o_ + idx:o_ + idx + n]

        TILES = [(0, 0, 512), (1, 512, 512), (2, 1024, 512), (3, 1536, 512), (4, 0, NCX)]

        def xview(t):
            ti, t0, n = TILES[t]
            if ti < 4:
                return xT[:, :, t0:t0 + n]
            return cT[:, :, 0:n]

        P.emit("sp", lambda h: h.dma_start(out=pvec[:], in_=pvec_d), writes=[r_pvec], dma=r_pvec)
        P.emit("sp", lambda h: h.dma_start(out=cvec[:], in_=cvec_d.rearrange("(c p) j -> p c j", p=128)),
               writes=[r_cvec], dma=r_cvec)
        for t in range(4):
            t0 = TILES[t][1]
            P.emit("sp", lambda h, t0=t0: h.dma_start(out=xT[:, :, t0:t0 + 512],
                                                    in_=xT_d.rearrange("(c p) t -> p c t", p=128)[:, :, t0:t0 + 512]),
                   writes=[r_x[t]], dma=r_x[t])
        P.emit("sp", lambda h: h.dma_start(out=cT[:], in_=cT_d.rearrange("(c p) t -> p c t", p=128)),
               writes=[r_x[4]], dma=r_x[4])
        P.emit("pool", lambda h: h.dma_start(out=rot_bf[:], in_=rot_d), writes=[r_const], dma=r_const)
        P.emit("pool", lambda h: h.dma_start(out=sguw[:], in_=sguw_d.rearrange("l g q p -> q l g p")),
               writes=[r_const], dma=r_const)
        memset(ones_bf[:], 1.0, [r_const], eng="dve")
        memset(mv8[:], 1.0, [r_vt], eng="dve")
        act(scb[:], cvec[:], AF.Silu, [r_cvec], [r_small])

        def ada_block(l, j):
            rm = r_small if l == 0 else r_mod1
            pi = ps_any()
            si, wv = load_slot(w_ada_d[l].rearrange("(c p) f -> p c f", p=128)[:, :, j * 512:(j + 1) * 512],
                               lambda s: s[:, 0:4096].rearrange("p (c f) -> p c f", c=DC))
            for m in range(4):
                for c in range(DC):
                    mm(pi, ps[pi][:, 2 * m:2 * m + 2], wv[:, c, m * 128:(m + 1) * 128], scb[:, c, :],
                       c == 0, c == DC - 1, [r_slot[si], r_small])
            bo = PV["b_ada"][0] + l * 72 + j * 4
            for col in range(2):
                tt(modT[:, l, j * 4:(j + 1) * 4, col], ps[pi][:, 0:8].rearrange("p (j c) -> p j c", c=2)[:, :, col],
                   pvec[:, bo:bo + 4], ALU.add, [r_ps[pi], r_pvec], [rm])

        def ada_finish_k(l, k):
            rm = r_small if l == 0 else r_mod1
            for col in range(2):
                go = PV["norm_g"][0] + (l * 3 + k) * 8
                stt(gsT[:, l, k, :, col], modT[:, l, (3 * k + 1) * 8:(3 * k + 2) * 8, col], 1.0,
                    pvec[:, go:go + 8], ALU.add, ALU.mult, [rm, r_pvec], [rm])
            if k != 1:
                which = 0 if k == 0 else 1
                for col in range(2):
                    ts(ghT[:, l, which, :, col], modT[:, l, (3 * k + 2) * 8:(3 * k + 3) * 8, col], 0.5, None,
                       ALU.mult, None, [rm], [rm])
            else:
                lo = PV["lam_p"][0] + l * 256
                ls = lscr[:, l * 68:(l + 1) * 68] if False else lscr
                tt(lscr[:, 0:64], pvec[:, lo:lo + 64], pvec[:, lo + 64:lo + 128], ALU.mult, [r_pvec], [rm])
                tt(lscr[:, 64:128], pvec[:, lo + 128:lo + 192], pvec[:, lo + 192:lo + 256], ALU.mult, [r_pvec], [rm])
                P.emit("dve", lambda h: h.reduce_sum(out=lscr[:, 128:130], in_=lscr[:, 0:128].rearrange("p (a b) -> p a b", a=2),
                                                     axis=mybir.AxisListType.X), reads=[rm], writes=[rm])
                act(lscr[:, 130:132], lscr[:, 128:130], AF.Exp, [rm], [rm])
                lam_init = 0.8 - 0.6 * float(np.exp(-0.3 * l))
                stt(lamT[:, l, 0:1], lscr[:, 131:132], -lam_init, lscr[:, 130:131], ALU.add, ALU.subtract,
                    [rm], [rm])
                so = PV["subln_g"][0] + l
                ts(lamT[:, l, 1:2], pvec[:, so:so + 1], 1.0 - lam_init, None, ALU.mult, None, [r_pvec], [rm])
            if l == 1 and k == 2:
                P.fence([r_small, r_mod1])

        ada_todo = [(0, j) for j in range(6, 18)] + [(1, j) for j in range(18)]

        def ada_some(n_):
            for _ in range(n_):
                if ada_todo:
                    l_, j_ = ada_todo.pop(0)
                    ada_block(l_, j_)
                    if j_ % 6 == 5:
                        ada_finish_k(l_, j_ // 6)

        for j in range(6):
            ada_block(0, j)
        ada_finish_k(0, 0)
        cp(fgT[:], pv("final_g", 0, 8), [r_pvec], [r_small])

        def modnorm(t, gs_ap, shift_ap, hoff=0, rh=None, mixer=False):
            ti, t0, n = TILES[t]
            rh = rh or r_hTc[0]
            xv = xview(t)
            sqv = hT[:, :, 768:1280] if mixer else sq8
            sqr = r_hTc[2] if mixer else [r_sq8] * DC
            for c in range(DC):
                if c < 5:
                    act(sqv[:, c, 0:n], xv[:, c, :], AF.Square, [r_x[t]], [sqr[c]])
                else:
                    tt(sqv[:, c, 0:n], xv[:, c, :], xv[:, c, :], ALU.mult, [r_x[t]], [sqr[c]])
            pi = ps_any()
            for c in range(DC):
                mm(pi, ps[pi][:, 0:n], ones_bf[:], sqv[:, c, 0:n], c == 0, c == DC - 1, [sqr[c], r_const])
            ri = ntmp()
            act(tmps[ri][:, 0:n], ps[pi][:, 0:n], AF.Sqrt, [r_ps[pi]], [r_tmp[ri]], bias=EPS, scale=1.0 / D)
            recip(tmps[ri][:, 0:n], tmps[ri][:, 0:n], [r_tmp[ri]], [r_tmp[ri]])
            wis = [ntmp(), ntmp(), ntmp()]
            for c in range(DC):
                wi = wis[c % 3]
                stt(tmps[wi][:, 0:n], xv[:, c, :], gs_ap[:, c:c + 1], tmps[ri][:, 0:n], ALU.mult, ALU.mult,
                    [r_x[t], r_small, r_tmp[ri]], [r_tmp[wi]])
                act(hT[:, c, hoff:hoff + n], tmps[wi][:, 0:n], AF.Identity, [r_tmp[wi], r_small], [rh[c]],
                    bias=shift_ap[:, c:c + 1], scale=1.0)

        def ffn_subs(tlist):
            subs = []
            ho = 0
            for si_, t in enumerate(tlist):
                ti, t0, n = TILES[t]
                subs.append((t, ho, n, 0 if ti < 4 else 1, r_hTc[si_]))
                ho += n
            return subs

        def ffn_prenorm(tlist, l, which, count):
            k = 0 if which == 0 else 2
            for (t, ho, n, col, rh) in ffn_subs(tlist)[:count]:
                modnorm(t, gsT[:, l, k, :, col], modT[:, l, (3 * k) * 8:(3 * k + 1) * 8, col], hoff=ho, rh=rh, mixer=True)

        def ffn(tlist, l, which, prenormed=0):
            k = 0 if which == 0 else 2
            w1 = ffw[f"ffn{which + 1}_w1"][l].rearrange("(c p) f -> p c f", p=128)
            w3 = ffw[f"ffn{which + 1}_w3"][l].rearrange("(c p) f -> p c f", p=128)
            w2 = ffw[f"ffn{which + 1}_w2"][l].rearrange("(c p) d -> p c d", p=128)
            subs = ffn_subs(tlist)

            def norm_sub(sub):
                t, ho, n, col, rh = sub
                if any(sub is s_ for s_ in subs[:prenormed]):
                    return
                modnorm(t, gsT[:, l, k, :, col], modT[:, l, (3 * k) * 8:(3 * k + 1) * 8, col], hoff=ho, rh=rh)

            for fh in range(2):
                fbase = fh * 11
                for fb in range(3):
                    nf = 4 if fb < 2 else 3
                    wcols = nf * 128
                    c0 = (fbase + fb * 4) * 128
                    s1, v1 = load_slot(w1[:, :, c0:c0 + wcols],
                                       lambda s, wcols=wcols: s[:, 0:DC * wcols].rearrange("p (c f) -> p c f", c=DC))
                    s3, v3 = load_slot(w3[:, :, c0:c0 + wcols],
                                       lambda s, wcols=wcols: s[:, 0:DC * wcols].rearrange("p (c f) -> p c f", c=DC))
                    first_blk = fh == 0 and fb == 0
                    order = ([(m, sb_) for sb_ in subs for m in range(nf)] if first_blk
                             else [(m, sb_) for m in range(nf) for sb_ in subs])
                    if first_blk:
                        norm_sub(subs[0])
                        if len(subs) > 1:
                            norm_sub(subs[1])
                    for m, sb_ in order:
                        fc = fb * 4 + m
                        if first_blk and len(subs) > 2 and sb_ is subs[1] and m == 0:
                            norm_sub(subs[2])
                        for (t, ho, n, col, rh) in (sb_,):
                            p1, p3 = ps_any(), ps_any()
                            for c in range(DC):
                                mm(p1, ps[p1][:, 0:n], v1[:, c, m * 128:(m + 1) * 128], hT[:, c, ho:ho + n], c == 0,
                                   c == DC - 1, [r_slot[s1], rh[c]])
                            for c in range(DC):
                                mm(p3, ps[p3][:, 0:n], v3[:, c, m * 128:(m + 1) * 128], hT[:, c, ho:ho + n], c == 0,
                                   c == DC - 1, [r_slot[s3], rh[c]])
                            wi = ntmp()
                            act(tmps[wi][:, 0:n], ps[p1][:, 0:n], AF.Silu, [r_ps[p1]], [r_tmp[wi]])
                            tt(gT[:, fc, ho:ho + n], tmps[wi][:, 0:n], ps[p3][:, 0:n], ALU.mult, [r_tmp[wi], r_ps[p3]],
                               [r_gT[fc]])
                    if l == 0:
                        ada_some(1)
                for dc in range(DC):
                    s2, v2 = load_slot(w2[:, fbase:fbase + 11, dc * 128:(dc + 1) * 128],
                                       lambda s: s[:, 0:11 * 128].rearrange("p (c d) -> p c d", c=11))
                    for (t, ho, n, col, rh) in subs:
                        xv = xview(t)
                        pi = ps_any()
                        for fc in range(11):
                            mm(pi, ps[pi][:, 0:n], v2[:, fc, :], gT[:, fc, ho:ho + n], fc == 0, fc == 10,
                               [r_slot[s2], r_gT[fc]])
                        stt(xv[:, dc, :], ps[pi][:, 0:n], ghT[:, l, which, dc:dc + 1, col], xv[:, dc, :], ALU.mult, ALU.add,
                            [r_ps[pi], r_small, r_x[t]], [r_x[t]])
                    if l == 0 and dc % 2 == 1:
                        ada_some(1)

        w_in_v = [w_in_d[l].rearrange("(c p) f -> p c f", p=128) for l in range(L)]

        def v512(s):
            return s[:, 0:4096].rearrange("p (c f) -> p c f", c=DC)

        def v256(s):
            return s[:, 0:2048].rearrange("p (c f) -> p c f", c=DC)

        def proj(si, wv, m, n, reads_extra=()):
            pi = ps_any()
            for c in range(DC):
                mm(pi, ps[pi][:, 0:n], wv[:, c, m * 128:(m + 1) * 128], hT[:, c, 0:n], c == 0, c == DC - 1,
                   [r_slot[si], r_hTc[0][c]])
            return pi

        def rope_to(dst_fn, pi, n, dst_res):
            a = ntmp()
            cp(sqs[:, 0:n], ps[pi][:, 0:n], [r_ps[pi]], [r_sqs])
            p2 = ps_any()
            mm(p2, ps[p2][:, 0:n], rot_bf[:], sqs[:, 0:n], True, True, [r_sqs, r_const])
            tt(tmps[a][:, 0:n], ps[pi][:, 0:n], ropec[:, 0:n], ALU.mult, [r_ps[pi], r_rope], [r_tmp[a]])
            b = ntmp()
            tt(tmps[b][:, 0:n], ps[p2][:, 0:n], ropes[:, 0:n], ALU.mult, [r_ps[p2], r_rope], [r_tmp[b]])
            return a, b

        def mixer_a(t, l):
            ti, t0, n = TILES[t]
            latent = ti < 4
            col = 0 if latent else 1
            modnorm(t, gsT[:, l, 1, :, col], modT[:, l, 24:32, col], mixer=True)
            if latent:
                P.emit("sp", None, writes=[r_rope], dma=r_rope, multi=[
                    lambda h: h.dma_start(out=ropec[:], in_=cos_d[:, t0:t0 + 512]),
                    lambda h: h.dma_start(out=ropes[:], in_=sin_d[:, t0:t0 + 512])])
            si, wv = load_slot(w_in_v[l][:, :, K_OFF:K_OFF + 512], v512)
            for hd in range(4):
                pi = proj(si, wv, hd, n)
                if latent:
                    a, b = rope_to(None, pi, n, r_kst)
                    tt(kst[:, hd, 0:n], tmps[a][:, 0:n], tmps[b][:, 0:n], ALU.add, [r_tmp[a], r_tmp[b]], [r_kst])
                else:
                    cp(kst[:, hd, 0:n], ps[pi][:, 0:n], [r_ps[pi]], [r_kst])
            if latent:
                dstK = K_loc[l].rearrange("p (h w) -> p h w", h=4)[:, :, t0:t0 + n]
                P.emit("sp", lambda h: h.dma_start(out=dstK, in_=kst[:, :, 0:n]), reads=[r_kst], writes=[r_Kloc[l]], dma=r_Kloc[l])
            else:
                dstK = Kc_loc[l].rearrange("p (h w) -> p h w", h=4)
                P.emit("sp", lambda h: h.dma_start(out=dstK, in_=kst[:, :, 0:n]), reads=[r_kst], writes=[r_Kc[l]], dma=r_Kc[l])
            si, wv = load_slot(w_in_v[l][:, :, V_OFF:V_OFF + 512], v512)
            for tc in range(n // 128):
                pi = ps_any()
                for c in range(DC):
                    mm(pi, ps[pi][:, :], hT[:, c, tc * 128:(tc + 1) * 128], wv[:, c, :], c == 0, c == DC - 1,
                       [r_slot[si], r_hTc[0][c]])
                cp(vst[:, tc, :], ps[pi][:, :], [r_ps[pi]], [r_vst], eng="act" if tc % 2 else "dve")
            if latent:
                dstV = V_loc[l][t0:t0 + n, :].rearrange("(k p) e -> p k e", p=128)
                P.emit("sp", lambda h: h.dma_start(out=dstV, in_=vst[:, 0:n // 128, :]), reads=[r_vst], writes=[r_Vloc[l]], dma=r_Vloc[l])
            else:
                dstV = Vc_loc[l].rearrange("(k p) e -> p k e", p=128)
                P.emit("sp", lambda h: h.dma_start(out=dstV, in_=vst[:, 0:n // 128, :]), reads=[r_vst], writes=[r_Vc[l]], dma=r_Vc[l])
            if (not latent) and l == L - 1:
                return
            si, wv = load_slot(w_in_v[l][:, :, 256:768], v512)
            for k2 in range(2):
                pc = proj(si, wv, k2, n)
                px = proj(si, wv, 2 + k2, n)
                a = ntmp()
                act(tmps[a][:, 0:n], ps[pc][:, 0:n], AF.Identity, [r_ps[pc]], [r_tmp[a]])
                tt(cvst[:, k2, 0:n], tmps[a][:, 0:n], ps[px][:, 0:n], ALU.mult, [r_tmp[a], r_ps[px]], [r_cvst])
            si, wv = load_slot(w_in_v[l][:, :, D_OFF:D_OFF + 512], v512)
            for k2 in range(2):
                pz = proj(si, wv, k2, n)
                pg = proj(si, wv, 2 + k2, n)
                a = ntmp()
                act(tmps[a][:, 0:n], ps[pg][:, 0:n], AF.Sigmoid, [r_ps[pg]], [r_tmp[a]])
                tt(cvst[:, 2 + k2, 0:n], tmps[a][:, 0:n], ps[pz][:, 0:n], ALU.mult, [r_tmp[a], r_ps[pz]], [r_cvst])
            if latent:
                dst = CV_loc[l].rearrange("p (c w) -> p c w", c=4)[:, :, t0:t0 + n]
                P.emit("sp", lambda h: h.dma_start(out=dst, in_=cvst[:, :, 0:n]), reads=[r_cvst], writes=[r_CV[l]], dma=r_CV[l])
                hv = H_loc[l].rearrange("p (c w) -> p c w", c=4)
                if ti == 0:
                    P.emit("sp", lambda h: h.dma_start(out=hv[:, :, 0:16], in_=cvst[:, :, 0:16]),
                           reads=[r_cvst], writes=[r_Hloc[l]], dma=r_Hloc[l])
                if ti == 3:
                    P.emit("sp", lambda h: h.dma_start(out=hv[:, :, 16:32], in_=cvst[:, :, 496:512]),
                           reads=[r_cvst], writes=[r_Hloc[l]], dma=r_Hloc[l])
            else:
                dst = CVc_loc[l].rearrange("p (c w) -> p c w", c=4)
                P.emit("sp", lambda h: h.dma_start(out=dst, in_=cvst[:, :, 0:n]), reads=[r_cvst], writes=[r_CVc[l]], dma=r_CVc[l])

        def exchange(l):
            grp = [[0, 1], [2, 3], [4, 5], [6, 7]]
            P.emit("pool", lambda h: h.collective_compute("AllGather", ALU.bypass, replica_groups=grp,
                                                          ins=[K_loc[l]], outs=[K_all[l]]),
                   reads=[r_Kloc[l]], writes=[r_Kall[l]], dma=r_Kall[l], inc=1)
            P.emit("pool", lambda h: h.collective_compute("AllGather", ALU.bypass, replica_groups=grp,
                                                          ins=[H_loc[l]], outs=[H_all[l]]),
                   reads=[r_Hloc[l]], writes=[r_Hall[l]], dma=r_Hall[l], inc=1)
            P.emit("pool", lambda h: h.collective_compute("AllGather", ALU.bypass, replica_groups=grp,
                                                          ins=[V_loc[l]], outs=[V_all[l]]),
                   reads=[r_Vloc[l]], writes=[r_Vall[l]], dma=r_Vall[l], inc=1)

        def attention(t, l, hooks=None):
            hooks = dict(hooks or {})
            ti, t0, n = TILES[t]
            latent = ti < 4
            kts = list(range(NKT)) if latent else [32, 33]
            pending = []
            active = []
            for hd in range(4):
                kall = K_all[l].rearrange("q (h w) -> q h w", h=4)
                if latent:
                    for part in range(2):
                        P.emit("sp", lambda h, hd=hd, part=part: h.dma_start(
                            out=kT[:, part * 2048:(part + 1) * 2048], in_=kall[part * 128:(part + 1) * 128, hd, 0:2048]),
                            reads=[r_Kall[l]], writes=[r_kTp[part]], dma=r_kTp[part])
                        P.emit("sp", lambda h, hd=hd, part=part: h.dma_start(
                            out=vT[:, part * 16:(part + 1) * 16, :],
                            in_=V_all[l][part * 2048:(part + 1) * 2048, hd * 128:(hd + 1) * 128].rearrange("(k p) e -> p k e", p=128)),
                            reads=[r_Vall[l]], writes=[r_vTp[part]], dma=r_vTp[part])
                P.emit("sp", lambda h, hd=hd: h.dma_start(out=kT[:, 4096:4352],
                                                        in_=Kc_loc[l].rearrange("p (h w) -> p h w", h=4)[:, hd, :]),
                       reads=[r_Kc[l]], writes=[r_kTp[2]], dma=r_kTp[2])
                P.emit("sp", lambda h, hd=hd: h.dma_start(
                    out=vT[:, 32:34, :], in_=Vc_loc[l][:, hd * 128:(hd + 1) * 128].rearrange("(k p) e -> p k e", p=128)),
                    reads=[r_Vc[l]], writes=[r_vTp[2]], dma=r_vTp[2])

                def pv_step(ki):
                    kt = kts[ki]
                    first, last = ki == 0, ki == len(kts) - 1
                    for m in range(2):
                        pj = (ki % 2) * 2 + m
                        mm(4 + 2 * m, ps[4 + 2 * m][:, 0:n], vT[:, kt, :], pT[pj][:, 0:n], first, last, [r_vTp[min(kt // 16, 2)], r_pT[pj]])
                        mm(5 + 2 * m, ps[5 + 2 * m][:, 0:n], ones_bf[:], pT[pj][:, 0:n], first, last, [r_const, r_pT[pj]])

                for ki, kt in enumerate(kts):
                    pb = pairA()
                    for m in range(2):
                        qsrc = q0T if m == 0 else q1T
                        mm(pb + m, ps[pb + m][:, 0:n], kT[:, kt * 128:(kt + 1) * 128], qsrc[:, hd, 0:n], True, True,
                           [r_kTp[min(kt // 16, 2)], r_q])
                    pj0 = (ki % 2) * 2
                    if n == 512:
                        P.emit("act", lambda h, pb=pb, pj0=pj0: h.activation(
                            out=pT2[pj0 // 2], in_=ps_all[:, pb * 512:(pb + 2) * 512], func=AF.Exp, bias=0.0, scale=0.125),
                            reads=[r_ps[pb], r_ps[pb + 1]], writes=[r_pT[pj0], r_pT[pj0 + 1]])
                    else:
                        for m in range(2):
                            act(pT[pj0 + m][:, 0:n], ps[pb + m][:, 0:n], AF.Exp, [r_ps[pb + m]], [r_pT[pj0 + m]], scale=0.125)
                    if ki > 0:
                        pv_step(ki - 1)
                    if ki == min(6, len(kts) - 1) and pending:
                        pending.pop(0)()
                    fn_ = hooks.pop((hd, ki), None)
                    if fn_ is not None:
                        active.append(fn_())
                    for g_ in list(active):
                        try:
                            next(g_)
                        except StopIteration:
                            active.remove(g_)
                pv_step(len(kts) - 1)
                ev = [ntmp2(), ntmp2()]
                if n == 512:
                    P.emit("dve", lambda h, ev=ev: h.tensor_copy(out=tmp_all[:, ev[0] * 512:(ev[0] + 2) * 512],
                                                                in_=ps_all[:, 4 * 512:6 * 512]),
                           reads=[r_ps[4], r_ps[5]], writes=[r_tmp[ev[0]], r_tmp[ev[0] + 1]])
                    P.emit("act", lambda h, ev=ev: h.copy(out=tmp_all[:, ev[1] * 512:(ev[1] + 2) * 512],
                                                         in_=ps_all[:, 6 * 512:8 * 512]),
                           reads=[r_ps[6], r_ps[7]], writes=[r_tmp[ev[1]], r_tmp[ev[1] + 1]])
                else:
                    for j_ in range(2):
                        cp(tmps[ev[0] + j_][:, 0:n], ps[4 + j_][:, 0:n], [r_ps[4 + j_]], [r_tmp[ev[0] + j_]])
                        cp(tmps[ev[1] + j_][:, 0:n], ps[6 + j_][:, 0:n], [r_ps[6 + j_]], [r_tmp[ev[1] + j_]], eng="act")
                o0, z0, o1, z1 = ev[0], ev[0] + 1, ev[1], ev[1] + 1
                recip(tmps[z0][:, 0:n], tmps[z0][:, 0:n], [r_tmp[z0]], [r_tmp[z0]])
                tt(tmps[o0][:, 0:n], tmps[o0][:, 0:n], tmps[z0][:, 0:n], ALU.mult, [r_tmp[o0], r_tmp[z0]], [r_tmp[o0]])
                recip(tmps[z1][:, 0:n], tmps[z1][:, 0:n], [r_tmp[z1]], [r_tmp[z1]])
                tt(tmps[o1][:, 0:n], tmps[o1][:, 0:n], tmps[z1][:, 0:n], ALU.mult, [r_tmp[o1], r_tmp[z1]], [r_tmp[o1]])
                stt(tmps[o0][:, 0:n], tmps[o1][:, 0:n], lamT[:, l, 0:1], tmps[o0][:, 0:n], ALU.mult, ALU.add,
                    [r_tmp[o1], r_tmp[o0], r_small], [r_tmp[o0]])

                def finish(hd=hd, o0=o0, rz2=z0):
                    act(sqs[:, 0:n], tmps[o0][:, 0:n], AF.Square, [r_tmp[o0]], [r_sqs])
                    pi = pairA()
                    mm(pi, ps[pi][:, 0:n], ones_bf[:], sqs[:, 0:n], True, True, [r_sqs, r_const])
                    act(tmps[rz2][:, 0:n], ps[pi][:, 0:n], AF.Sqrt, [r_ps[pi]], [r_tmp[rz2]], bias=EPS, scale=1.0 / 128)
                    recip(tmps[rz2][:, 0:n], tmps[rz2][:, 0:n], [r_tmp[rz2]], [r_tmp[rz2]])
                    stt(ocT[:, hd, 0:n], tmps[o0][:, 0:n], lamT[:, l, 1:2], tmps[rz2][:, 0:n], ALU.mult, ALU.mult,
                        [r_tmp[o0], r_tmp[rz2], r_small], [r_oc])

                pending.append(finish)
                if l == 0 and latent:
                    ada_some(2 if (t == 0 and hd < 2) else 1)
            if hooks or active:
                while pending:
                    pending.pop(0)()
                for g_ in active:
                    for _ in g_:
                        pass
                for key_ in sorted(hooks):
                    for _ in hooks[key_]():
                        pass
            return pending

        def load_window(t, l):
            ti, t0, n = TILES[t]
            memset(win[:], 0.0, [r_win], eng="pool")
            if ti == 4:
                src = CVc_loc[l].rearrange("p (c w) -> p c w", c=4)
                P.emit("sp", lambda h: h.dma_start(out=win[:, :, 16:16 + n], in_=src), reads=[r_CVc[l]], writes=[r_win], dma=r_win)
                return
            src = CV_loc[l].rearrange("p (c w) -> p c w", c=4)
            lo = max(t0 - 16, 0)
            hi = min(t0 + n + 16, NT)
            fns = [lambda h: h.dma_start(out=win[:, :, 16 + (lo - t0):16 + (hi - t0)], in_=src[:, :, lo:hi])]
            hall = H_all[l].rearrange("q (c w) -> q c w", c=4)
            if ti == 0:
                fns.append(lambda h: h.dma_start(out=win[:, :, 0:16], in_=hall[0:128, :, 16:32]))
            if ti == 3:
                fns.append(lambda h: h.dma_start(out=win[:, :, 16 + n:32 + n], in_=hall[128:256, :, 0:16]))
            P.emit("sp", None, reads=[r_CV[l], r_Hall[l]], writes=[r_win], dma=r_win, multi=fns)
            if ti == 0:
                ts(win[:, :, 0:16], win[:, :, 0:16], pv("mleft"), None, ALU.mult, None, [r_win, r_pvec], [r_win])
            if ti == 3:
                ts(win[:, :, 16 + n:32 + n], win[:, :, 16 + n:32 + n], pv("mright"), None, ALU.mult, None,
                   [r_win, r_pvec], [r_win])

        def mixer_b(t, l, pre=None, skip_norm=False):
            ti, t0, n = TILES[t]
            latent = ti < 4
            col = 0 if latent else 1
            if not skip_norm:
                modnorm(t, gsT[:, l, 1, :, col], modT[:, l, 24:32, col], mixer=True)
            if latent:
                P.emit("sp", None, writes=[r_rope], dma=r_rope, multi=[
                    lambda h: h.dma_start(out=ropec[:], in_=cos_d[:, t0:t0 + 512]),
                    lambda h: h.dma_start(out=ropes[:], in_=sin_d[:, t0:t0 + 512])])
            si, wv = load_slot(w_in_v[l][:, :, Q_OFF:Q_OFF + 512], v512)
            memset(q0T[:, :, :], 0.0, [r_q], eng="pool")
            memset(q1T[:, :, :], 0.0, [r_q], eng="pool")
            for hd in range(4):
                pi = proj(si, wv, hd, n)
                if latent:
                    a, b = rope_to(None, pi, n, r_q)
                    tt(q0T[0:64, hd, 0:n], tmps[a][0:64, 0:n], tmps[b][0:64, 0:n], ALU.add, [r_tmp[a], r_tmp[b]], [r_q])
                    tt(q1T[64:128, hd, 0:n], tmps[a][64:128, 0:n], tmps[b][64:128, 0:n], ALU.add, [r_tmp[a], r_tmp[b]], [r_q])
                else:
                    cp(q0T[0:64, hd, 0:n], ps[pi][0:64, 0:n], [r_ps[pi]], [r_q])
                    cp(q1T[64:128, hd, 0:n], ps[pi][64:128, 0:n], [r_ps[pi]], [r_q])
            if t == 0 and l == 0:
                for hd in range(4):
                    tap(hd, q0T[:, hd, :], r_q)
                    tap(4 + hd, q1T[:, hd, :], r_q)
                tap(32, hT[:, 0, :], r_hTc[0][0])
            load_window(t, l)
            si, wv = load_slot(w_in_v[l][:, :, 0:256], v256)
            cao = PV["conv_a_w"][0] + l * 6
            for k2 in range(2):
                pb = proj(si, wv, k2, n)
                a = ntmp()
                ts(tmps[a][:, 0:n], win[:, k2, 15:15 + n], pvec[:, cao + k2:cao + k2 + 1], None, ALU.mult, None,
                   [r_win, r_pvec], [r_tmp[a]])
                for j in (1, 2):
                    stt(tmps[a][:, 0:n], win[:, k2, 15 + j:15 + j + n], pvec[:, cao + 2 * j + k2:cao + 2 * j + k2 + 1],
                        tmps[a][:, 0:n], ALU.mult, ALU.add, [r_win, r_pvec, r_tmp[a]], [r_tmp[a]])
                tt(aT[:, k2, 0:n], tmps[a][:, 0:n], ps[pb][:, 0:n], ALU.mult, [r_tmp[a], r_ps[pb]], [r_a])
            cdo = PV["conf_dw"][0] + l * 62
            for k2 in range(2):
                eng = "dve"
                a = ntmp()
                dbo = PV["conf_db"][0] + l * 2 + k2
                ts(tmps[a][:, 0:n], win[:, 2 + k2, 1:1 + n], pvec[:, cdo + k2:cdo + k2 + 1], pvec[:, dbo:dbo + 1],
                   ALU.mult, ALU.add, [r_win, r_pvec], [r_tmp[a]], eng=eng)
                for j in range(1, 31):
                    stt(tmps[a][:, 0:n], win[:, 2 + k2, 1 + j:1 + j + n], pvec[:, cdo + 2 * j + k2:cdo + 2 * j + k2 + 1],
                        tmps[a][:, 0:n], ALU.mult, ALU.add, [r_win, r_pvec, r_tmp[a]], [r_tmp[a]], eng=eng)
                cp(dT[:, k2, 0:n], tmps[a][:, 0:n], [r_tmp[a]], [r_d])
            P.fence([r_sq8] + r_vnm + r_csq)
            ntc = n // 128
            nb = (ntc + 1) // 2
            vnm4 = [merged[:, tc, :].rearrange("p (g c) -> p g c", g=4) for tc in range(4)]
            csq = [merged[:, 4, :], merged[:, 5, :]]
            hk = {}

            def gelu_steps(src, w_, out_ap, out_res, src_res, g):
                tt(tmps[g][:, 0:w_], src, src, ALU.mult, [src_res], [r_tmp[g]])
                ts(tmps[g][:, 0:w_], tmps[g][:, 0:w_], 0.044715, 1.0, ALU.mult, ALU.add, [r_tmp[g]], [r_tmp[g]])
                tt(tmps[g][:, 0:w_], tmps[g][:, 0:w_], src, ALU.mult, [r_tmp[g], src_res], [r_tmp[g]])
                yield
                act(tmps[g][:, 0:w_], tmps[g][:, 0:w_], AF.Sigmoid, [r_tmp[g]], [r_tmp[g]], scale=1.5957691216057308)
                yield
                tt(out_ap, tmps[g][:, 0:w_], src, ALU.mult, [r_tmp[g], src_res], [out_res])

            def sgu_proj():
                si, wv = load_slot(w_in_v[l][:, :, B_OFF:B_OFF + 512], v512)
                pu = pairA()
                for k2 in range(2):
                    for c in range(DC):
                        mm(pu + k2, ps[pu + k2][:, 0:n], wv[:, c, k2 * 128:(k2 + 1) * 128], hT[:, c, 0:n], c == 0, c == DC - 1,
                           [r_slot[si], r_hTc[0][c]])
                pv_ = pairA()
                for bk in range(nb):
                    ntcb = min(2, ntc - 2 * bk)
                    for tcl in range(ntcb):
                        tc = 2 * bk + tcl
                        for c in range(DC):
                            mm(pv_ + bk, ps[pv_ + bk][:, tcl * 256:(tcl + 1) * 256], hT[:, c, tc * 128:(tc + 1) * 128],
                               wv[:, c, 256:512], c == 0, c == DC - 1, [r_slot[si], r_hTc[0][c]])
                memset(merged[:, 0:4, :], 0.0, r_vnm, eng="pool")
                cu = [ntmp(), ntmp()]
                for k2 in range(2):
                    cp(tmps[cu[k2]][:, 0:n], ps[pu + k2][:, 0:n], [r_ps[pu + k2]], [r_tmp[cu[k2]]])
                vtb = []
                for bk in range(nb):
                    ntcb = min(2, ntc - 2 * bk)
                    vb = ntmp()
                    vtb.append(vb)
                    cp(tmps[vb][:, 0:ntcb * 256], ps[pv_ + bk][:, 0:ntcb * 256], [r_ps[pv_ + bk]], [r_tmp[vb]])
                g = ntmp()
                sqs_ = [ntmp(), ntmp()]
                yield
                for k2 in range(2):
                    yield from gelu_steps(tmps[cu[k2]][:, 0:n], n, bT[:, k2, 0:n], r_b, r_tmp[cu[k2]], g)
                for bk in range(nb):
                    ntcb = min(2, ntc - 2 * bk)
                    vb = vtb[bk]
                    yield from gelu_steps(tmps[vb][:, 0:ntcb * 256], ntcb * 256, tmps[vb][:, 0:ntcb * 256], r_tmp[vb],
                                          r_tmp[vb], g)
                    P.emit("dve", lambda h, vb=vb, bk=bk, ntcb=ntcb: h.reduce_sum(
                        out=mv8[:, 2 * bk:2 * bk + ntcb], in_=tmps[vb][:, 0:ntcb * 256].rearrange("p (a b) -> p a b", a=ntcb),
                        axis=mybir.AxisListType.X), reads=[r_tmp[vb]], writes=[r_vt])
                ts(mv8[:, 0:4], mv8[:, 0:4], -1.0 / 256, None, ALU.mult, None, [r_vt], [r_vt])
                for bk in range(nb):
                    vb = vtb[bk]
                    ntcb = min(2, ntc - 2 * bk)
                    for tcl in range(ntcb):
                        tc = 2 * bk + tcl
                        ts(tmps[vb][:, tcl * 256:(tcl + 1) * 256], tmps[vb][:, tcl * 256:(tcl + 1) * 256], mv8[:, tc:tc + 1],
                           None, ALU.add, None, [r_tmp[vb], r_vt], [r_tmp[vb]])
                    sq_ = sqs_[bk]
                    tt(tmps[sq_][:, 0:ntcb * 256], tmps[vb][:, 0:ntcb * 256], tmps[vb][:, 0:ntcb * 256], ALU.mult,
                       [r_tmp[vb]], [r_tmp[sq_]])
                    P.emit("dve", lambda h, sq_=sq_, bk=bk, ntcb=ntcb: h.reduce_sum(
                        out=mv8[:, 4 + 2 * bk:4 + 2 * bk + ntcb],
                        in_=tmps[sq_][:, 0:ntcb * 256].rearrange("p (a b) -> p a b", a=ntcb),
                        axis=mybir.AxisListType.X), reads=[r_tmp[sq_]], writes=[r_vt])
                yield
                yield
                act(mv8[:, 4:8], mv8[:, 4:8], AF.Sqrt, [r_vt], [r_vt], bias=EPS, scale=1.0 / 256)
                yield
                recip(mv8[:, 4:8], mv8[:, 4:8], [r_vt], [r_vt])
                for tc in range(ntc):
                    vb = vtb[tc // 2]
                    tcl = tc % 2
                    for g_ in range(4):
                        gg = g_ % 2
                        ts(vnm4[tc][:, g_, gg * 64:(gg + 1) * 64], tmps[vb][:, tcl * 256 + g_ * 64:tcl * 256 + (g_ + 1) * 64],
                           mv8[:, 4 + tc:5 + tc], None, ALU.mult, None, [r_tmp[vb], r_vt], [r_vnm[tc]])

            def sgu_spatial():
                pS = pairA()
                for tc in range(ntc):
                    for k2 in range(2):
                        for gg in range(2):
                            g_ = 2 * k2 + gg
                            mm(pS + k2, ps[pS + k2][:, tc * 128:(tc + 1) * 128], vnm4[tc][:, g_, :], sguw[:, l, g_, :],
                               gg == 0, gg == 1, [r_vnm[tc], r_const])
                for k2 in range(2):
                    a = ntmp()
                    tt(tmps[a][:, 0:n], ps[pS + k2][:, 0:n], bias_rep[:, k2, 0:n], ALU.add, [r_ps[pS + k2], r_small],
                       [r_tmp[a]])
                    tt(bT[:, k2, 0:n], tmps[a][:, 0:n], bT[:, k2, 0:n], ALU.mult, [r_tmp[a], r_b], [r_b])
                yield

            def conf_ln():
                hacc = [ntmp(), ntmp()]
                rz = ntmp()
                pm = pairA()
                for k2 in range(2):
                    mm(pm, ps[pm][:, 0:n], ones_bf[:], dT[:, k2, 0:n], k2 == 0, k2 == 1, [r_d, r_const])
                for k2 in range(2):
                    a = hacc[k2]
                    stt(tmps[a][:, 0:n], ps[pm][:, 0:n], -1.0 / 256, dT[:, k2, 0:n], ALU.mult, ALU.add,
                        [r_ps[pm], r_d], [r_tmp[a]])
                    tt(csq[k2][:, 0:n], tmps[a][:, 0:n], tmps[a][:, 0:n], ALU.mult, [r_tmp[a]], [r_csq[k2]])
                yield
                yield
                yield
                pvv = pairA()
                for k2 in range(2):
                    mm(pvv, ps[pvv][:, 0:n], ones_bf[:], csq[k2][:, 0:n], k2 == 0, k2 == 1, [r_csq[k2], r_const])
                yield
                act(tmps[rz][:, 0:n], ps[pvv][:, 0:n], AF.Sqrt, [r_ps[pvv]], [r_tmp[rz]], bias=EPS, scale=1.0 / 256)
                recip(tmps[rz][:, 0:n], tmps[rz][:, 0:n], [r_tmp[rz]], [r_tmp[rz]])
                for k2 in range(2):
                    a = hacc[k2]
                    go = PV["conf_ln_g"][0] + l * 2 + k2
                    stt(tmps[a][:, 0:n], tmps[a][:, 0:n], pvec[:, go:go + 1], tmps[rz][:, 0:n], ALU.mult, ALU.mult,
                        [r_tmp[a], r_tmp[rz], r_pvec], [r_tmp[a]])
                yield
                yield
                for k2 in range(2):
                    a = hacc[k2]
                    bo = PV["conf_ln_b"][0] + l * 2 + k2
                    act(dT[:, k2, 0:n], tmps[a][:, 0:n], AF.Silu, [r_tmp[a], r_pvec], [r_d], bias=pvec[:, bo:bo + 1],
                        scale=1.0)

            hk[(1, 12)] = conf_ln
            hk[(2, 7)] = sgu_proj
            hk[(3, 8)] = sgu_spatial
            late = attention(t, l, hk)
            if t == 0 and l == 0:
                for hd in range(4):
                    tap(8 + hd, ocT[:, hd, :], r_oc)
            P.fence([r_sq8] + r_vnm + r_csq)
            if t == 0 and l == 0:
                for k2 in range(2):
                    tap(12 + k2, aT[:, k2, :], r_a)
                    tap(14 + k2, bT[:, k2, :], r_b)
                    tap(16 + k2, dT[:, k2, :], r_d)
                for c4 in range(4):
                    tap(28 + c4, win[:, c4, 0:512], r_win)
            wa = w_a_d[l].rearrange("(c p) d -> p c d", p=128)
            wb = w_b_d[l].rearrange("(c p) d -> p c d", p=128)
            wc = w_c_d[l].rearrange("(c p) d -> p c d", p=128)
            wd = w_d_d[l].rearrange("(c p) d -> p c d", p=128)
            gview = w_in_v[l][:, :, G_OFF:G_OFF + 4096].rearrange("p c (i d) -> p c i d", i=4)
            branches = [(aT, r_a, 2), (bT, r_b, 2), (ocT, r_oc, 4), (dT, r_d, 2)]
            for dc in range(DC):
                i = next_slot()
                gdst = slots[i][:, 0:4096].rearrange("p (c i f) -> p c i f", c=DC, i=4)
                P.emit("pool", None, writes=[r_slot[i]], dma=r_slot[i], multi=[
                    (lambda h, dc=dc, gdst=gdst, bi=bi: h.dma_start(out=gdst[:, :, bi, :],
                                                                   in_=gview[:, :, bi, dc * 128:(dc + 1) * 128]))
                    for bi in range(4)])
                j = next_slot()
                bdst = slots[j][:, 0:1280].rearrange("p (c f) -> p c f", c=10)
                P.emit("pool", None, writes=[r_slot[j]], dma=r_slot[j], multi=[
                    lambda h, dc=dc, bdst=bdst: h.dma_start(out=bdst[:, 0:2, :], in_=wa[:, :, dc * 128:(dc + 1) * 128]),
                    lambda h, dc=dc, bdst=bdst: h.dma_start(out=bdst[:, 2:4, :], in_=wb[:, :, dc * 128:(dc + 1) * 128]),
                    lambda h, dc=dc, bdst=bdst: h.dma_start(out=bdst[:, 4:8, :], in_=wc[:, :, dc * 128:(dc + 1) * 128]),
                    lambda h, dc=dc, bdst=bdst: h.dma_start(out=bdst[:, 8:10, :], in_=wd[:, :, dc * 128:(dc + 1) * 128])])
                koff = [0, 2, 4, 8]
                acc = ntmp()
                for bi, (bt_, br_, nk) in enumerate(branches):
                    if bi == 2:
                        while late:
                            late.pop(0)()
                    pg = psA()
                    for c in range(DC):
                        mm(pg, ps[pg][:, 0:n], gdst[:, c, bi, :], hT[:, c, 0:n], c == 0, c == DC - 1, [r_slot[i], r_hTc[0][c]])
                    act(sgs[bi][:, 0:n], ps[pg][:, 0:n], AF.Sigmoid, [r_ps[pg]], [r_sgs[bi]])
                    py = psB()
                    for kc in range(nk):
                        mm(py, ps[py][:, 0:n], bdst[:, koff[bi] + kc, :], bt_[:, kc, 0:n], kc == 0, kc == nk - 1,
                           [r_slot[j], br_])
                    if bi == 0:
                        tt(tmps[acc][:, 0:n], ps[py][:, 0:n], sgs[bi][:, 0:n], ALU.mult, [r_ps[py], r_sgs[bi]], [r_tmp[acc]])
                    else:
                        b2 = ntmp()
                        tt(tmps[b2][:, 0:n], ps[py][:, 0:n], sgs[bi][:, 0:n], ALU.mult, [r_ps[py], r_sgs[bi]], [r_tmp[b2]])
                        if bi < 3:
                            tt(tmps[acc][:, 0:n], tmps[acc][:, 0:n], tmps[b2][:, 0:n], ALU.add, [r_tmp[acc], r_tmp[b2]],
                               [r_tmp[acc]])
                        else:
                            tt(merged[:, dc, 0:n], tmps[acc][:, 0:n], tmps[b2][:, 0:n], ALU.add, [r_tmp[acc], r_tmp[b2]],
                               [r_sq8])
            if t == 0 and l == 0:
                for dc in range(DC):
                    tap(18 + dc, merged[:, dc, :], r_sq8)
            if pre is not None:
                P.fence(r_hT_all)
                pre()
            xv = xview(t)
            wo = w_o_d[l].rearrange("(c p) d -> p c d", p=128)
            for half in range(2):
                si, wv = load_slot(wo[:, :, half * 512:(half + 1) * 512], v512)
                for m in range(4):
                    dc = half * 4 + m
                    pi = psB()
                    for c in range(DC):
                        mm(pi, ps[pi][:, 0:n], wv[:, c, m * 128:(m + 1) * 128], merged[:, c, 0:n], c == 0, c == DC - 1,
                           [r_slot[si], r_sq8])
                    stt(xv[:, dc, :], ps[pi][:, 0:n], modT[:, l, 40 + dc:41 + dc, col], xv[:, dc, :], ALU.mult, ALU.add,
                        [r_ps[pi], r_small, r_x[t]], [r_x[t]])

        def final_norm(t):
            ti, t0, n = TILES[t]
            xv = xview(t)
            for c in range(DC):
                act(sq8[:, c, 0:n], xv[:, c, :], AF.Square, [r_x[t]], [r_sq8])
            pi = ps_any()
            for c in range(DC):
                mm(pi, ps[pi][:, 0:n], ones_bf[:], sq8[:, c, 0:n], c == 0, c == DC - 1, [r_sq8, r_const])
            ri = ntmp()
            act(tmps[ri][:, 0:n], ps[pi][:, 0:n], AF.Sqrt, [r_ps[pi]], [r_tmp[ri]], bias=EPS, scale=1.0 / D)
            recip(tmps[ri][:, 0:n], tmps[ri][:, 0:n], [r_tmp[ri]], [r_tmp[ri]])
            for c in range(DC):
                stt(xv[:, c, :], xv[:, c, :], fgT[:, c:c + 1], tmps[ri][:, 0:n], ALU.mult, ALU.mult,
                    [r_x[t], r_small, r_tmp[ri]], [r_x[t]])
            P.emit("sp", lambda h, t0=t0: h.dma_start(out=yT_d.rearrange("(c p) t -> p c t", p=128)[:, :, t0:t0 + 512],
                                                    in_=xT[:, :, t0:t0 + 512]),
                   reads=[r_x[t]], writes=[r_out], dma=r_out)

        stage = {"n": 0}
        big_res = r_kTp + r_vTp + [r_win, r_q, r_oc, r_a, r_b, r_d, r_kst, r_vst, r_cvst] + r_pT + r_gT

        def go():
            stage["n"] += 1
            P.fence(big_res)
            P.fence(r_hT_all)
            return stage["n"] <= DEBUG_STOP

        for l in range(L):
            last = l == L - 1
            so = PV["sgu_b"][0] + l * 256
            for k2 in range(2):
                for r4 in range(4):
                    cp(bias_rep[:, k2, r4 * 128:(r4 + 1) * 128], pvec[:, so + k2 * 128:so + (k2 + 1) * 128],
                       [r_pvec], [r_small])
            for grp_ in ([4, 0, 1], [2, 3]):
                if go():
                    ffn(grp_, l, 0)
                for t in grp_:
                    if t != 4 and go():
                        mixer_a(t, l)
            if go():
                exchange(l)
            if go():
                mixer_a(4, l)
            for grp_ in (([0, 1], [2, 3]) if last else ([4, 0, 1], [2, 3])):
                for i_, t in enumerate(grp_):
                    if go():
                        if i_ + 1 < len(grp_):
                            t2 = grp_[i_ + 1]
                            col2 = 0 if t2 < 4 else 1
                            pre_ = (lambda t2=t2, col2=col2: modnorm(t2, gsT[:, l, 1, :, col2], modT[:, l, 24:32, col2],
                                                                      mixer=True))
                        else:
                            pre_ = (lambda grp_=grp_: ffn_prenorm(grp_, l, 1, len(grp_) - 1))
                        mixer_b(t, l, pre=pre_, skip_norm=(i_ > 0))
                if go():
                    ffn(grp_, l, 1, prenormed=len(grp_) - 1)
                if last:
                    for t in grp_:
                        final_norm(t)
        while ada_todo:
            ada_some(1)

        if DEBUG_STOP < 10 ** 8:
            for t in range(4):
                final_norm(t)
        P.emit("sp", lambda h: h.nop(), reads=[r_out, r_dbg])
        P.finalize()
    return nc


def _rope_tables(half):
    nf = 16
    t = np.arange(NT, dtype=np.float32) + np.float32(half * NT)
    row = np.floor(t / 64).astype(np.float32)
    colp = (t - row * 64).astype(np.float32)
    inv = (np.float32(10000.0) ** (-np.arange(nf, dtype=np.float32) / np.float32(nf))).astype(np.float32)
    cosT = np.zeros((128, NT), np.float32)
    sinT = np.zeros((128, NT), np.float32)
    for p in range(128):
        r = p % 64
        axis, hf, f = r // 32, (r % 32) // 16, r % 16
        pos = row if axis == 0 else colp
        ang = (pos * inv[f]).astype(np.float32)
        cosT[p] = np.cos(ang)
        sinT[p] = np.sin(ang) * (-1.0 if hf == 0 else 1.0)
    return cosT, sinT


def _rot_matrix():
    R = np.zeros((128, 128), np.float32)
    for p in range(128):
        hf = (p % 32) // 16
        partner = p + 16 if hf == 0 else p - 16
        R[partner, p] = 1.0
    return R


def _pack_pvec(inp, half):
    pv = np.zeros((128, NPV), np.float32)

    def put(name, arr):
        o, w = PV[name]
        arr = np.asarray(arr, np.float32)
        assert arr.shape == (128, w), (name, arr.shape, w)
        pv[:, o:o + w] = arr

    def fm(v):
        v = np.asarray(v, np.float32)
        lead = v.shape[:-1]
        n = v.shape[-1] // 128
        a = v.reshape(lead + (n, 128))
        a = np.moveaxis(a, -1, 0)
        return a.reshape(128, -1)

    put("b_ada", fm(inp["b_ada"]))
    put("norm_g", fm(inp["norm_g"]))
    put("final_g", fm(inp["final_g"]))
    put("conv_a_w", fm(inp["conv_a_w"]))
    put("conf_dw", fm(inp["conf_dw"]))
    put("conf_db", fm(inp["conf_db"]))
    put("conf_ln_g", fm(inp["conf_ln_g"]))
    put("conf_ln_b", fm(inp["conf_ln_b"]))
    put("subln_g", fm(inp["subln_g"]))
    put("lam_p", np.broadcast_to(np.asarray(inp["lam_p"], np.float32).reshape(1, L * 256), (128, L * 256)))
    sb_ = np.asarray(inp["sgu_b"], np.float32)
    rep = np.zeros((128, L, 2, 128), np.float32)
    for l in range(L):
        for g in range(4):
            rep[(g % 2) * 64:(g % 2) * 64 + 64, l, g // 2, :] = sb_[l, g][None, :]
    put("sgu_b", rep.reshape(128, -1))
    put("mleft", np.full((128, 1), 1.0 if half == 1 else 0.0, np.float32))
    put("mright", np.full((128, 1), 1.0 if half == 0 else 0.0, np.float32))
    return pv


_NC_CACHE = {}


def kernel(**inp):
    if "nc" not in _NC_CACHE:
        _NC_CACHE["nc"] = build_program()
    nc = _NC_CACHE["nc"]
    x = np.asarray(inp["x"], np.float32)
    ctx = np.asarray(inp["ctx"], np.float32)
    c = np.asarray(inp["c"], np.float32)
    c_ctx = np.asarray(inp["c_ctx"], np.float32)
    rot = _rot_matrix()
    sgu_wT = np.ascontiguousarray(np.swapaxes(np.asarray(inp["sgu_w"], np.float32), 2, 3))
    shared = {k: np.ascontiguousarray(np.asarray(inp[k], np.float32)) for k in
              ("w_ada", "ffn1_w1", "ffn1_w3", "ffn1_w2", "ffn2_w1", "ffn2_w3", "ffn2_w2", "w_in",
               "w_a_out", "w_b_out", "w_c_out", "w_d_out", "w_o")}
    tabs = [_rope_tables(0), _rope_tables(1)]
    in_maps = []
    for core in range(8):
        b, half = core // 2, core % 2
        m = dict(shared)
        m["xT"] = np.ascontiguousarray(x[b, half * NT:(half + 1) * NT, :].T)
        m["cT"] = np.ascontiguousarray(ctx[b].T)
        m["cvec"] = np.ascontiguousarray(np.stack([c[b], c_ctx], axis=1))
        m["pvec"] = _pack_pvec(inp, half)
        m["rot"] = rot
        m["cosT"], m["sinT"] = tabs[half]
        m["sgu_wT"] = sgu_wT
        in_maps.append(m)
    res = run_bass_kernel_spmd(nc, in_maps, core_ids=list(range(8)))
    if DEBUG_TAPS:
        _NC_CACHE["dbg"] = [np.asarray(res.results[core]["dbg"]) for core in range(8)]
    out = np.empty((4, 2 * NT, D), np.float32)
    for core in range(8):
        b, half = core // 2, core % 2
        out[b, half * NT:(half + 1) * NT, :] = np.asarray(res.results[core]["yT"]).T
    return out
```

```python
import numpy as np
import concourse.bass as bass
import concourse.mybir as mybir
from concourse.bass_utils import run_bass_kernel_spmd
from contextlib import ExitStack

F32 = mybir.dt.float32
BF16 = mybir.dt.bfloat16
AF = mybir.ActivationFunctionType
ALU = mybir.AluOpType

D = 1024
DC = 8
L = 2
NT = 2048
NCX = 256
DFF = 2816
FC = 22
IN_COLS = 7424
A_OFF, B_OFF, Q_OFF, K_OFF, V_OFF, D_OFF, G_OFF = 0, 768, 1280, 1792, 2304, 2816, 3328
EPS = 1e-6
KW = 2048
NKT = 34

SAME_ENGINE_SYNC = True
DEBUG_STOP = 10 ** 9
DEBUG_TAPS = False


class Res:
    __slots__ = ("name", "w", "r", "dsem", "dcount")

    def __init__(self, name):
        self.name = name
        self.w = {}
        self.r = {}
        self.dsem = None
        self.dcount = 0


class Prog:
    ENGS = ("pe", "act", "dve", "pool", "sp")

    def __init__(self, nc, stack):
        self.nc = nc
        self.stack = stack
        self.streams = {e: [] for e in self.ENGS}
        self.seen = {e: {} for e in self.ENGS}
        self.esem = {}
        for e in self.ENGS:
            self.esem[e] = stack.enter_context(nc.semaphore("es_" + e))
        self.nres = 0
        self.dma_res = []
        self.bar_seen = {}

    def res(self, name=None):
        self.nres += 1
        return Res(name or f"r{self.nres}")

    def emit(self, eng, fn, reads=(), writes=(), dma=None, inc=16, multi=None):
        deps = {}
        for r in reads:
            for k, v in r.w.items():
                if deps.get(k, -1) < v:
                    deps[k] = v
        for r in writes:
            for d in (r.w, r.r):
                for k, v in d.items():
                    if deps.get(k, -1) < v:
                        deps[k] = v
        seen = self.seen[eng]
        waits = []
        for k, v in deps.items():
            if k == eng and dma is None:
                if eng == "pe" or not SAME_ENGINE_SYNC:
                    continue
            if seen.get(k, -1) >= v:
                continue
            seen[k] = v
            waits.append((k, v))
        fns = multi if multi is not None else [fn]
        for i, f in enumerate(fns):
            idx = len(self.streams[eng])
            rec = {"fn": f, "waits": waits if i == 0 else [], "signal": False, "dma": dma, "inc": inc}
            self.streams[eng].append(rec)
            if dma is not None:
                if dma.dsem is None:
                    dma.dsem = self.stack.enter_context(self.nc.semaphore("ds_" + dma.name))
                    self.dma_res.append(dma)
                dma.dcount += inc
        if dma is not None:
            key, val = ("d", dma), dma.dcount
        else:
            key, val = eng, idx
        for r in reads:
            r.r[key] = max(r.r.get(key, -1), val)
        for r in writes:
            r.w = {key: val}
            r.r = {}

    def barrier(self):
        b = Res("bar")
        for e in self.ENGS:
            st_ = self.streams[e]
            for i in range(len(st_) - 1, -1, -1):
                if st_[i]["dma"] is None:
                    b.w[e] = i
                    break
        for r in self.dma_res:
            if r.dcount > self.bar_seen.get(r, 0):
                b.w[("d", r)] = r.dcount
                self.bar_seen[r] = r.dcount
        for e in self.ENGS:
            self.emit(e, lambda h: h.nop(), reads=[b])

    def fence(self, group):
        deps = {}
        for r in group:
            for d in (r.w, r.r):
                for k, v in d.items():
                    if deps.get(k, -1) < v:
                        deps[k] = v
        for r in group:
            for k, v in deps.items():
                if r.w.get(k, -1) < v:
                    r.w[k] = v

    def finalize(self):
        nc = self.nc
        for e in self.ENGS:
            for rec in self.streams[e]:
                for k, v in rec["waits"]:
                    if isinstance(k, str):
                        assert self.streams[k][v]["dma"] is None
                        self.streams[k][v]["signal"] = True
        cnt = {}
        for e in self.ENGS:
            c = 0
            arr = []
            for rec in self.streams[e]:
                if rec["signal"]:
                    c += 1
                arr.append(c)
            cnt[e] = arr
        handles = {"pe": "tensor", "act": "scalar", "dve": "vector", "pool": "gpsimd", "sp": "sync"}
        with nc.Block() as block:
            for e in self.ENGS:
                stream = self.streams[e]
                if not stream:
                    continue

                def body(h, e=e, stream=stream):
                    for rec in stream:
                        for k, v in rec["waits"]:
                            if isinstance(k, str):
                                h.wait_ge(self.esem[k], cnt[k][v])
                            else:
                                h.wait_ge(k[1].dsem, v)
                        ins = rec["fn"](h)
                        if rec["dma"] is not None:
                            ins.then_inc(rec["dma"].dsem, rec["inc"])
                        elif rec["signal"]:
                            ins.then_inc(self.esem[e], 1)

                getattr(block, handles[e])(body)


PV = {}
_o = 0
for _n, _w in [("b_ada", L * 72), ("norm_g", L * 3 * 8), ("final_g", 8), ("conv_a_w", L * 3 * 2),
               ("conf_dw", L * 31 * 2), ("conf_db", L * 2), ("conf_ln_g", L * 2), ("conf_ln_b", L * 2),
               ("subln_g", L), ("lam_p", L * 256), ("sgu_b", L * 2 * 128), ("mleft", 1), ("mright", 1)]:
    PV[_n] = (_o, _w)
    _o += _w
NPV = _o


def build_program():
    nc = bass.Bass("TRN2", target_bir_lowering=False)

    def din(name, shape, dt=F32):
        return nc.dram_tensor(name, list(shape), dt, kind="ExternalInput").ap()

    xT_d = din("xT", [D, NT])
    cT_d = din("cT", [D, NCX])
    cvec_d = din("cvec", [D, 2])
    pvec_d = din("pvec", [128, NPV])
    rot_d = din("rot", [128, 128])
    cos_d = din("cosT", [128, NT])
    sin_d = din("sinT", [128, NT])
    w_ada_d = din("w_ada", [L, D, 9 * D])
    ffw = {}
    for nm in ("ffn1_w1", "ffn1_w3", "ffn2_w1", "ffn2_w3"):
        ffw[nm] = din(nm, [L, D, DFF])
    for nm in ("ffn1_w2", "ffn2_w2"):
        ffw[nm] = din(nm, [L, DFF, D])
    w_in_d = din("w_in", [L, D, IN_COLS])
    w_a_d = din("w_a_out", [L, 256, D])
    w_b_d = din("w_b_out", [L, 256, D])
    w_c_d = din("w_c_out", [L, 512, D])
    w_d_d = din("w_d_out", [L, 256, D])
    w_o_d = din("w_o", [L, D, D])
    sguw_d = din("sgu_wT", [L, 4, 128, 128])
    yT_d = nc.dram_tensor("yT", [D, NT], F32, kind="ExternalOutput").ap()
    dbg_d = nc.dram_tensor("dbg", [128, 40, 512], BF16, kind="ExternalOutput").ap() if DEBUG_TAPS else None

    def dscr(name, shape):
        return nc.dram_tensor(name, list(shape), BF16, kind="Internal").ap()

    K_loc = [dscr(f"K_loc{l}", [128, 4 * KW]) for l in range(L)]
    K_all = [dscr(f"K_all{l}", [256, 4 * KW]) for l in range(L)]
    V_loc = [dscr(f"V_loc{l}", [NT, 512]) for l in range(L)]
    V_all = [dscr(f"V_all{l}", [2 * NT, 512]) for l in range(L)]
    H_loc = [dscr(f"H_loc{l}", [128, 128]) for l in range(L)]
    H_all = [dscr(f"H_all{l}", [256, 128]) for l in range(L)]
    Kc_loc = [dscr(f"Kc_loc{l}", [128, 4 * NCX]) for l in range(L)]
    Vc_loc = [dscr(f"Vc_loc{l}", [NCX, 512]) for l in range(L)]
    CV_loc = [dscr(f"CV_loc{l}", [128, 4 * NT]) for l in range(L)]
    CVc_loc = [dscr(f"CVc_loc{l}", [128, 4 * NCX]) for l in range(L)]

    with ExitStack() as st:
        P = Prog(nc, st)

        def sb(name, shape, dt):
            return st.enter_context(nc.sbuf_tensor(name, list(shape), dt))

        xT = sb("xT_s", [128, DC, NT], F32)
        cT = sb("cT_s", [128, DC, NCX], F32)
        NSLOT = 4
        slots = [sb(f"wslot{i}", [128, 4096], BF16) for i in range(NSLOT)]
        hT = sb("hT", [128, DC, 1280], BF16)
        BIGN = 24192
        big = sb("big", [128, BIGN], BF16)
        NTMP = 7
        tmp_all = sb("tmp_all", [128, NTMP * 512], F32)
        tmps = [tmp_all[:, i * 512:(i + 1) * 512] for i in range(NTMP)]
        pvec = sb("pvec_s", [128, NPV], F32)
        cvec = sb("cvec_s", [128, DC, 2], F32)
        scb = sb("scb", [128, DC, 2], BF16)
        modT = sb("modT", [128, L, 72, 2], F32)
        gsT = sb("gsT", [128, L, 3, DC, 2], F32)
        ghT = sb("ghT", [128, L, 2, DC, 2], F32)
        fgT = sb("fgT", [128, DC], F32)
        lamT = sb("lamT", [128, L, 4], F32)
        lscr = sb("lscr", [128, 136], F32)
        ones_bf = sb("ones_bf", [128, 128], BF16)
        rot_bf = sb("rot_bf", [128, 128], BF16)
        sguw = sb("sguw", [128, L, 4, 128], BF16)
        bias_rep = sb("bias_rep", [128, 2, 512], F32)
        sqs = sb("sqs", [128, 512], BF16)
        mv8 = sb("mv8", [128, 8], F32)
        ps_all = st.enter_context(nc.psum_tensor("ps_all", [128, 4096], F32))
        ps = [ps_all[:, i * 512:(i + 1) * 512] for i in range(8)]

        def bigv(off, n):
            return big[:, off:off + n]
        gT = bigv(0, 11 * 1280).rearrange("p (c t) -> p c t", c=11)
        o = 0
        kT = bigv(o, NKT * 128); o += NKT * 128
        vT = bigv(o, NKT * 128).rearrange("p (k e) -> p k e", k=NKT); o += NKT * 128
        owin = o
        win = bigv(o, 4 * 544).rearrange("p (c t) -> p c t", c=4); o += 4 * 544
        oq = o
        q0T = bigv(o, 2048).rearrange("p (h t) -> p h t", h=4); o += 2048
        q1T = bigv(o, 2048).rearrange("p (h t) -> p h t", h=4); o += 2048
        pT = [bigv(o + i * 512, 512) for i in range(4)]
        pT2 = [bigv(o + i * 1024, 1024) for i in range(2)]; o += 2048
        ocT = bigv(owin, 2048).rearrange("p (h t) -> p h t", h=4)
        aT = bigv(o, 1024).rearrange("p (c t) -> p c t", c=2); o += 1024
        bT = bigv(o, 1024).rearrange("p (c t) -> p c t", c=2); o += 1024
        dT = bigv(o, 1024).rearrange("p (c t) -> p c t", c=2); o += 1024
        kst = bigv(oq, 2048).rearrange("p (h t) -> p h t", h=4)
        vst = bigv(oq + 2048, 2048).rearrange("p (k e) -> p k e", k=4)
        cvst = bigv(oq + 4096, 2048).rearrange("p (c t) -> p c t", c=4)
        sgs = pT
        o = max(o, 11 * 1280)
        sq8 = bigv(o, 4096).rearrange("p (c t) -> p c t", c=8)
        assert o + 4096 <= BIGN, o
        merged = sq8
        ropec = sb("ropec", [128, 512], F32)
        ropes = sb("ropes", [128, 512], F32)

        r_x = [P.res(f"x{t}") for t in range(5)]
        r_slot = [P.res(f"slot{i}") for i in range(NSLOT)]
        r_hTc = [[P.res(f"hT{i}_{c}") for c in range(DC)] for i in range(3)]
        r_hT_all = [r for lst in r_hTc for r in lst]
        r_gT = [P.res(f"gT{i}") for i in range(11)]
        r_tmp = [P.res(f"tmp{i}") for i in range(NTMP)]
        r_ps = [P.res(f"ps{i}") for i in range(8)]
        r_small = P.res("small")
        r_mod1 = P.res("mod1")
        r_pvec = P.res("pvec")
        r_cvec = P.res("cvec")
        r_const = P.res("const")
        r_kTp = [P.res(f"kT{i}") for i in range(3)]
        r_vTp = [P.res(f"vT{i}") for i in range(3)]
        r_kT, r_vT = r_kTp[2], r_vTp[2]
        r_win = P.res("win")
        r_q, r_oc, r_a, r_b, r_d = P.res("q"), P.res("oc"), P.res("a"), P.res("b"), P.res("d")
        r_pT = [P.res(f"pT{i}") for i in range(4)]
        r_kst, r_vst, r_cvst = P.res("kst"), P.res("vst"), P.res("cvst")
        r_sgs = r_pT
        r_sq8 = P.res("sq8")
        r_vnm = [P.res(f"vnm{i}") for i in range(4)]
        r_csq = [P.res("csq0"), P.res("csq1")]
        r_rope = P.res("rope")
        r_sqs = P.res("sqs")
        r_vn = P.res("vn")
        r_vt = P.res("vt")
        r_Kloc = [P.res(f"Kloc{l}") for l in range(L)]
        r_Kall = [P.res(f"Kall{l}") for l in range(L)]
        r_Vloc = [P.res(f"Vloc{l}") for l in range(L)]
        r_Vall = [P.res(f"Vall{l}") for l in range(L)]
        r_Hloc = [P.res(f"Hloc{l}") for l in range(L)]
        r_Hall = [P.res(f"Hall{l}") for l in range(L)]
        r_Kc = [P.res(f"Kc{l}") for l in range(L)]
        r_Vc = [P.res(f"Vc{l}") for l in range(L)]
        r_CV = [P.res(f"CV{l}") for l in range(L)]
        r_CVc = [P.res(f"CVc{l}") for l in range(L)]
        r_out = P.res("out")

        r_dbg = P.res("dbg")

        def tap(slot, ap, res, width=512):
            if not DEBUG_TAPS:
                return
            P.emit("sp", lambda h: h.dma_start(out=dbg_d[:, slot, 0:width], in_=ap), reads=[res], writes=[r_dbg], dma=r_dbg)

        state = {"slot": 0, "psa": 0, "psb": 0, "tmp": 0}

        def next_slot():
            i = state["slot"]
            state["slot"] = (i + 1) % NSLOT
            return i

        def psA():
            i = state["psa"]
            state["psa"] = (i + 1) % 4
            return i

        def psB():
            i = state["psb"]
            state["psb"] = (i + 1) % 4
            return 4 + i

        def pairA():
            i = state.get("pair", 0)
            state["pair"] = 2 - i
            return i

        def ntmp2():
            i = state["tmp"]
            if i + 1 >= NTMP:
                i = 0
            state["tmp"] = (i + 2) % NTMP
            return i

        def ps_any():
            i = state.get("psr", 0)
            state["psr"] = (i + 1) % 8
            return i

        def ntmp():
            i = state["tmp"]
            state["tmp"] = (i + 1) % NTMP
            return i

        def mm(pi, out, lhsT, rhs, start, stop, reads):
            P.emit("pe", lambda h: h.matmul(out, lhsT=lhsT, rhs=rhs, start=start, stop=stop),
                   reads=reads, writes=[r_ps[pi]])

        def act(out, in_, func, reads, writes, bias=0.0, scale=1.0):
            P.emit("act", lambda h: h.activation(out=out, in_=in_, func=func, bias=bias, scale=scale),
                   reads=reads, writes=writes)

        def tt(out, in0, in1, op, reads, writes, eng="dve"):
            P.emit(eng, lambda h: h.tensor_tensor(out=out, in0=in0, in1=in1, op=op), reads=reads, writes=writes)

        def ts(out, in0, s1, s2, op0, op1, reads, writes, eng="dve"):
            if s2 is None:
                P.emit(eng, lambda h: h.tensor_scalar(out=out, in0=in0, scalar1=s1, scalar2=None, op0=op0),
                       reads=reads, writes=writes)
            else:
                P.emit(eng, lambda h: h.tensor_scalar(out=out, in0=in0, scalar1=s1, scalar2=s2, op0=op0, op1=op1),
                       reads=reads, writes=writes)

        def stt(out, in0, scalar, in1, op0, op1, reads, writes, eng="dve"):
            P.emit(eng, lambda h: h.scalar_tensor_tensor(out=out, in0=in0, scalar=scalar, in1=in1, op0=op0, op1=op1),
                   reads=reads, writes=writes)

        def cp(out, in_, reads, writes, eng="dve"):
            if eng == "act":
                P.emit(eng, lambda h: h.copy(out=out, in_=in_), reads=reads, writes=writes)
            else:
                P.emit(eng, lambda h: h.tensor_copy(out=out, in_=in_), reads=reads, writes=writes)

        def recip(out, in_, reads, writes):
            P.emit("dve", lambda h: h.reciprocal(out=out, in_=in_), reads=reads, writes=writes)

        def memset(ap, val, writes, eng="pool"):
            P.emit(eng, lambda h: h.memset(ap, val), writes=writes)

        def load_slot(src_ap, view_fn):
            i = next_slot()
            dst = view_fn(slots[i])
            P.emit("pool", lambda h: h.dma_start(out=dst, in_=src_ap), writes=[r_slot[i]], dma=r_slot[i])
            return i, dst

        def pv(name, idx=0, n=1):
            o_, w_ = PV[name]
            return pvec[:, o_ + idx:o_ + idx + n]

        TILES = [(0, 0, 512), (1, 512, 512), (2, 1024, 512), (3, 1536, 512), (4, 0, NCX)]

        def xview(t):
            ti, t0, n = TILES[t]
            if ti < 4:
                return xT[:, :, t0:t0 + n]
            return cT[:, :, 0:n]

        P.emit("sp", lambda h: h.dma_start(out=pvec[:], in_=pvec_d), writes=[r_pvec], dma=r_pvec)
        P.emit("sp", lambda h: h.dma_start(out=cvec[:], in_=cvec_d.rearrange("(c p) j -> p c j", p=128)),
               writes=[r_cvec], dma=r_cvec)
        for t in range(4):
            t0 = TILES[t][1]
            P.emit("sp", lambda h, t0=t0: h.dma_start(out=xT[:, :, t0:t0 + 512],
                                                    in_=xT_d.rearrange("(c p) t -> p c t", p=128)[:, :, t0:t0 + 512]),
                   writes=[r_x[t]], dma=r_x[t])
        P.emit("sp", lambda h: h.dma_start(out=cT[:], in_=cT_d.rearrange("(c p) t -> p c t", p=128)),
               writes=[r_x[4]], dma=r_x[4])
        P.emit("pool", lambda h: h.dma_start(out=rot_bf[:], in_=rot_d), writes=[r_const], dma=r_const)
        P.emit("pool", lambda h: h.dma_start(out=sguw[:], in_=sguw_d.rearrange("l g q p -> q l g p")),
               writes=[r_const], dma=r_const)
        memset(ones_bf[:], 1.0, [r_const], eng="dve")
        memset(mv8[:], 1.0, [r_vt], eng="dve")
        act(scb[:], cvec[:], AF.Silu, [r_cvec], [r_small])

        def ada_block(l, j):
            rm = r_small if l == 0 else r_mod1
            pi = ps_any()
            si, wv = load_slot(w_ada_d[l].rearrange("(c p) f -> p c f", p=128)[:, :, j * 512:(j + 1) * 512],
                               lambda s: s[:, 0:4096].rearrange("p (c f) -> p c f", c=DC))
            for m in range(4):
                for c in range(DC):
                    mm(pi, ps[pi][:, 2 * m:2 * m + 2], wv[:, c, m * 128:(m + 1) * 128], scb[:, c, :],
                       c == 0, c == DC - 1, [r_slot[si], r_small])
            bo = PV["b_ada"][0] + l * 72 + j * 4
            for col in range(2):
                tt(modT[:, l, j * 4:(j + 1) * 4, col], ps[pi][:, 0:8].rearrange("p (j c) -> p j c", c=2)[:, :, col],
                   pvec[:, bo:bo + 4], ALU.add, [r_ps[pi], r_pvec], [rm])

        def ada_finish_k(l, k, part="all"):
            rm = r_small if l == 0 else r_mod1
            if part != "gh":
                for col in range(2):
                    go = PV["norm_g"][0] + (l * 3 + k) * 8
                    stt(gsT[:, l, k, :, col], modT[:, l, (3 * k + 1) * 8:(3 * k + 2) * 8, col], 1.0,
                        pvec[:, go:go + 8], ALU.add, ALU.mult, [rm, r_pvec], [rm])
            if part == "gs":
                return
            if k != 1:
                which = 0 if k == 0 else 1
                for col in range(2):
                    ts(ghT[:, l, which, :, col], modT[:, l, (3 * k + 2) * 8:(3 * k + 3) * 8, col], 0.5, None,
                       ALU.mult, None, [rm], [rm])
            else:
                lo = PV["lam_p"][0] + l * 256
                ls = lscr[:, l * 68:(l + 1) * 68] if False else lscr
                tt(lscr[:, 0:64], pvec[:, lo:lo + 64], pvec[:, lo + 64:lo + 128], ALU.mult, [r_pvec], [rm])
                tt(lscr[:, 64:128], pvec[:, lo + 128:lo + 192], pvec[:, lo + 192:lo + 256], ALU.mult, [r_pvec], [rm])
                P.emit("dve", lambda h: h.reduce_sum(out=lscr[:, 128:130], in_=lscr[:, 0:128].rearrange("p (a b) -> p a b", a=2),
                                                     axis=mybir.AxisListType.X), reads=[rm], writes=[rm])
                act(lscr[:, 130:132], lscr[:, 128:130], AF.Exp, [rm], [rm])
                lam_init = 0.8 - 0.6 * float(np.exp(-0.3 * l))
                stt(lamT[:, l, 0:1], lscr[:, 131:132], -lam_init, lscr[:, 130:131], ALU.add, ALU.subtract,
                    [rm], [rm])
                so = PV["subln_g"][0] + l
                ts(lamT[:, l, 1:2], pvec[:, so:so + 1], 1.0 - lam_init, None, ALU.mult, None, [r_pvec], [rm])
            if l == 1 and k == 2:
                P.fence([r_small, r_mod1])

        ada_todo = [(0, j) for j in range(4, 18)] + [(1, j) for j in range(18)]

        def ada_some(n_):
            for _ in range(n_):
                if ada_todo:
                    l_, j_ = ada_todo.pop(0)
                    ada_block(l_, j_)
                    if (l_, j_) == (0, 5):
                        ada_finish_k(0, 0, "gh")
                    elif j_ % 6 == 5:
                        ada_finish_k(l_, j_ // 6)

        for j in range(4):
            ada_block(0, j)
        ada_finish_k(0, 0, "gs")
        cp(fgT[:], pv("final_g", 0, 8), [r_pvec], [r_small])

        def modnorm(t, gs_ap, shift_ap, hoff=0, rh=None, mixer=False):
            ti, t0, n = TILES[t]
            rh = rh or r_hTc[0]
            xv = xview(t)
            sqv = hT[:, :, 768:1280] if mixer else sq8
            sqr = r_hTc[2] if mixer else [r_sq8] * DC
            for c in range(DC):
                if c < 5:
                    act(sqv[:, c, 0:n], xv[:, c, :], AF.Square, [r_x[t]], [sqr[c]])
                else:
                    tt(sqv[:, c, 0:n], xv[:, c, :], xv[:, c, :], ALU.mult, [r_x[t]], [sqr[c]])
            pi = ps_any()
            for c in range(DC):
                mm(pi, ps[pi][:, 0:n], ones_bf[:], sqv[:, c, 0:n], c == 0, c == DC - 1, [sqr[c], r_const])
            ri = ntmp()
            act(tmps[ri][:, 0:n], ps[pi][:, 0:n], AF.Sqrt, [r_ps[pi]], [r_tmp[ri]], bias=EPS, scale=1.0 / D)
            recip(tmps[ri][:, 0:n], tmps[ri][:, 0:n], [r_tmp[ri]], [r_tmp[ri]])
            wis = [ntmp(), ntmp(), ntmp()]
            for c in range(DC):
                wi = wis[c % 3]
                stt(tmps[wi][:, 0:n], xv[:, c, :], gs_ap[:, c:c + 1], tmps[ri][:, 0:n], ALU.mult, ALU.mult,
                    [r_x[t], r_small, r_tmp[ri]], [r_tmp[wi]])
                act(hT[:, c, hoff:hoff + n], tmps[wi][:, 0:n], AF.Identity, [r_tmp[wi], r_small], [rh[c]],
                    bias=shift_ap[:, c:c + 1], scale=1.0)

        def ffn_subs(tlist):
            subs = []
            ho = 0
            for si_, t in enumerate(tlist):
                ti, t0, n = TILES[t]
                subs.append((t, ho, n, 0 if ti < 4 else 1, r_hTc[si_]))
                ho += n
            return subs

        def ffn_prenorm(tlist, l, which, count):
            k = 0 if which == 0 else 2
            for (t, ho, n, col, rh) in ffn_subs(tlist)[:count]:
                modnorm(t, gsT[:, l, k, :, col], modT[:, l, (3 * k) * 8:(3 * k + 1) * 8, col], hoff=ho, rh=rh, mixer=True)

        def ffn(tlist, l, which, prenormed=0):
            k = 0 if which == 0 else 2
            w1 = ffw[f"ffn{which + 1}_w1"][l].rearrange("(c p) f -> p c f", p=128)
            w3 = ffw[f"ffn{which + 1}_w3"][l].rearrange("(c p) f -> p c f", p=128)
            w2 = ffw[f"ffn{which + 1}_w2"][l].rearrange("(c p) d -> p c d", p=128)
            subs = ffn_subs(tlist)

            def norm_sub(sub):
                t, ho, n, col, rh = sub
                if any(sub is s_ for s_ in subs[:prenormed]):
                    return
                modnorm(t, gsT[:, l, k, :, col], modT[:, l, (3 * k) * 8:(3 * k + 1) * 8, col], hoff=ho, rh=rh)

            for fh in range(2):
                fbase = fh * 11
                for fb in range(3):
                    nf = 4 if fb < 2 else 3
                    wcols = nf * 128
                    c0 = (fbase + fb * 4) * 128
                    s1, v1 = load_slot(w1[:, :, c0:c0 + wcols],
                                       lambda s, wcols=wcols: s[:, 0:DC * wcols].rearrange("p (c f) -> p c f", c=DC))
                    s3, v3 = load_slot(w3[:, :, c0:c0 + wcols],
                                       lambda s, wcols=wcols: s[:, 0:DC * wcols].rearrange("p (c f) -> p c f", c=DC))
                    first_blk = fh == 0 and fb == 0
                    order = ([(m, sb_) for sb_ in subs for m in range(nf)] if first_blk
                             else [(m, sb_) for m in range(nf) for sb_ in subs])
                    if first_blk:
                        norm_sub(subs[0])
                        if len(subs) > 1:
                            norm_sub(subs[1])
                    for m, sb_ in order:
                        fc = fb * 4 + m
                        if first_blk and len(subs) > 2 and sb_ is subs[1] and m == 0:
                            norm_sub(subs[2])
                        for (t, ho, n, col, rh) in (sb_,):
                            p1, p3 = ps_any(), ps_any()
                            for c in range(DC):
                                mm(p1, ps[p1][:, 0:n], v1[:, c, m * 128:(m + 1) * 128], hT[:, c, ho:ho + n], c == 0,
                                   c == DC - 1, [r_slot[s1], rh[c]])
                            for c in range(DC):
                                mm(p3, ps[p3][:, 0:n], v3[:, c, m * 128:(m + 1) * 128], hT[:, c, ho:ho + n], c == 0,
                                   c == DC - 1, [r_slot[s3], rh[c]])
                            wi = ntmp()
                            act(tmps[wi][:, 0:n], ps[p1][:, 0:n], AF.Silu, [r_ps[p1]], [r_tmp[wi]])
                            tt(gT[:, fc, ho:ho + n], tmps[wi][:, 0:n], ps[p3][:, 0:n], ALU.mult, [r_tmp[wi], r_ps[p3]],
                               [r_gT[fc]])
                    if l == 0:
                        ada_some(1)
                for dc in range(DC):
                    s2, v2 = load_slot(w2[:, fbase:fbase + 11, dc * 128:(dc + 1) * 128],
                                       lambda s: s[:, 0:11 * 128].rearrange("p (c d) -> p c d", c=11))
                    for (t, ho, n, col, rh) in subs:
                        xv = xview(t)
                        pi = ps_any()
                        for fc in range(11):
                            mm(pi, ps[pi][:, 0:n], v2[:, fc, :], gT[:, fc, ho:ho + n], fc == 0, fc == 10,
                               [r_slot[s2], r_gT[fc]])
                        stt(xv[:, dc, :], ps[pi][:, 0:n], ghT[:, l, which, dc:dc + 1, col], xv[:, dc, :], ALU.mult, ALU.add,
                            [r_ps[pi], r_small, r_x[t]], [r_x[t]])
                    if l == 0 and dc % 2 == 1:
                        ada_some(1)

        w_in_v = [w_in_d[l].rearrange("(c p) f -> p c f", p=128) for l in range(L)]

        def v512(s):
            return s[:, 0:4096].rearrange("p (c f) -> p c f", c=DC)

        def v256(s):
            return s[:, 0:2048].rearrange("p (c f) -> p c f", c=DC)

        def proj(si, wv, m, n, reads_extra=()):
            pi = ps_any()
            for c in range(DC):
                mm(pi, ps[pi][:, 0:n], wv[:, c, m * 128:(m + 1) * 128], hT[:, c, 0:n], c == 0, c == DC - 1,
                   [r_slot[si], r_hTc[0][c]])
            return pi

        def rope_to(dst_fn, pi, n, dst_res):
            a = ntmp()
            cp(sqs[:, 0:n], ps[pi][:, 0:n], [r_ps[pi]], [r_sqs])
            p2 = ps_any()
            mm(p2, ps[p2][:, 0:n], rot_bf[:], sqs[:, 0:n], True, True, [r_sqs, r_const])
            tt(tmps[a][:, 0:n], ps[pi][:, 0:n], ropec[:, 0:n], ALU.mult, [r_ps[pi], r_rope], [r_tmp[a]])
            b = ntmp()
            tt(tmps[b][:, 0:n], ps[p2][:, 0:n], ropes[:, 0:n], ALU.mult, [r_ps[p2], r_rope], [r_tmp[b]])
            return a, b

        def mixer_a(t, l):
            ti, t0, n = TILES[t]
            latent = ti < 4
            col = 0 if latent else 1
            modnorm(t, gsT[:, l, 1, :, col], modT[:, l, 24:32, col], mixer=True)
            if latent:
                P.emit("sp", None, writes=[r_rope], dma=r_rope, multi=[
                    lambda h: h.dma_start(out=ropec[:], in_=cos_d[:, t0:t0 + 512]),
                    lambda h: h.dma_start(out=ropes[:], in_=sin_d[:, t0:t0 + 512])])
            si, wv = load_slot(w_in_v[l][:, :, K_OFF:K_OFF + 512], v512)
            for hd in range(4):
                pi = proj(si, wv, hd, n)
                if latent:
                    a, b = rope_to(None, pi, n, r_kst)
                    tt(kst[:, hd, 0:n], tmps[a][:, 0:n], tmps[b][:, 0:n], ALU.add, [r_tmp[a], r_tmp[b]], [r_kst])
                else:
                    cp(kst[:, hd, 0:n], ps[pi][:, 0:n], [r_ps[pi]], [r_kst])
            if latent:
                dstK = K_loc[l].rearrange("p (h w) -> p h w", h=4)[:, :, t0:t0 + n]
                P.emit("sp", lambda h: h.dma_start(out=dstK, in_=kst[:, :, 0:n]), reads=[r_kst], writes=[r_Kloc[l]], dma=r_Kloc[l])
            else:
                dstK = Kc_loc[l].rearrange("p (h w) -> p h w", h=4)
                P.emit("sp", lambda h: h.dma_start(out=dstK, in_=kst[:, :, 0:n]), reads=[r_kst], writes=[r_Kc[l]], dma=r_Kc[l])
            si, wv = load_slot(w_in_v[l][:, :, V_OFF:V_OFF + 512], v512)
            for tc in range(n // 128):
                pi = ps_any()
                for c in range(DC):
                    mm(pi, ps[pi][:, :], hT[:, c, tc * 128:(tc + 1) * 128], wv[:, c, :], c == 0, c == DC - 1,
                       [r_slot[si], r_hTc[0][c]])
                cp(vst[:, tc, :], ps[pi][:, :], [r_ps[pi]], [r_vst], eng="act" if tc % 2 else "dve")
            if latent:
                dstV = V_loc[l][t0:t0 + n, :].rearrange("(k p) e -> p k e", p=128)
                P.emit("sp", lambda h: h.dma_start(out=dstV, in_=vst[:, 0:n // 128, :]), reads=[r_vst], writes=[r_Vloc[l]], dma=r_Vloc[l])
            else:
                dstV = Vc_loc[l].rearrange("(k p) e -> p k e", p=128)
                P.emit("sp", lambda h: h.dma_start(out=dstV, in_=vst[:, 0:n // 128, :]), reads=[r_vst], writes=[r_Vc[l]], dma=r_Vc[l])
            if (not latent) and l == L - 1:
                return
            si, wv = load_slot(w_in_v[l][:, :, 256:768], v512)
            for k2 in range(2):
                pc = proj(si, wv, k2, n)
                px = proj(si, wv, 2 + k2, n)
                a = ntmp()
                act(tmps[a][:, 0:n], ps[pc][:, 0:n], AF.Identity, [r_ps[pc]], [r_tmp[a]])
                tt(cvst[:, k2, 0:n], tmps[a][:, 0:n], ps[px][:, 0:n], ALU.mult, [r_tmp[a], r_ps[px]], [r_cvst])
            si, wv = load_slot(w_in_v[l][:, :, D_OFF:D_OFF + 512], v512)
            for k2 in range(2):
                pz = proj(si, wv, k2, n)
                pg = proj(si, wv, 2 + k2, n)
                a = ntmp()
                act(tmps[a][:, 0:n], ps[pg][:, 0:n], AF.Sigmoid, [r_ps[pg]], [r_tmp[a]])
                tt(cvst[:, 2 + k2, 0:n], tmps[a][:, 0:n], ps[pz][:, 0:n], ALU.mult, [r_tmp[a], r_ps[pz]], [r_cvst])
            if latent:
                dst = CV_loc[l].rearrange("p (c w) -> p c w", c=4)[:, :, t0:t0 + n]
                P.emit("sp", lambda h: h.dma_start(out=dst, in_=cvst[:, :, 0:n]), reads=[r_cvst], writes=[r_CV[l]], dma=r_CV[l])
                hv = H_loc[l].rearrange("p (c w) -> p c w", c=4)
                if ti == 0:
                    P.emit("sp", lambda h: h.dma_start(out=hv[:, :, 0:16], in_=cvst[:, :, 0:16]),
                           reads=[r_cvst], writes=[r_Hloc[l]], dma=r_Hloc[l])
                if ti == 3:
                    P.emit("sp", lambda h: h.dma_start(out=hv[:, :, 16:32], in_=cvst[:, :, 496:512]),
                           reads=[r_cvst], writes=[r_Hloc[l]], dma=r_Hloc[l])
            else:
                dst = CVc_loc[l].rearrange("p (c w) -> p c w", c=4)
                P.emit("sp", lambda h: h.dma_start(out=dst, in_=cvst[:, :, 0:n]), reads=[r_cvst], writes=[r_CVc[l]], dma=r_CVc[l])

        def exchange(l):
            grp = [[0, 1], [2, 3], [4, 5], [6, 7]]
            P.emit("pool", lambda h: h.collective_compute("AllGather", ALU.bypass, replica_groups=grp,
                                                          ins=[K_loc[l]], outs=[K_all[l]]),
                   reads=[r_Kloc[l]], writes=[r_Kall[l]], dma=r_Kall[l], inc=1)
            P.emit("pool", lambda h: h.collective_compute("AllGather", ALU.bypass, replica_groups=grp,
                                                          ins=[H_loc[l]], outs=[H_all[l]]),
                   reads=[r_Hloc[l]], writes=[r_Hall[l]], dma=r_Hall[l], inc=1)
            P.emit("pool", lambda h: h.collective_compute("AllGather", ALU.bypass, replica_groups=grp,
                                                          ins=[V_loc[l]], outs=[V_all[l]]),
                   reads=[r_Vloc[l]], writes=[r_Vall[l]], dma=r_Vall[l], inc=1)

        def attention(t, l, hooks=None):
            hooks = dict(hooks or {})
            ti, t0, n = TILES[t]
            latent = ti < 4
            kts = list(range(NKT)) if latent else [32, 33]
            pending = []
            active = []
            for hd in range(4):
                kall = K_all[l].rearrange("q (h w) -> q h w", h=4)
                if latent:
                    for part in range(2):
                        P.emit("sp", lambda h, hd=hd, part=part: h.dma_start(
                            out=kT[:, part * 2048:(part + 1) * 2048], in_=kall[part * 128:(part + 1) * 128, hd, 0:2048]),
                            reads=[r_Kall[l]], writes=[r_kTp[part]], dma=r_kTp[part])
                        P.emit("sp", lambda h, hd=hd, part=part: h.dma_start(
                            out=vT[:, part * 16:(part + 1) * 16, :],
                            in_=V_all[l][part * 2048:(part + 1) * 2048, hd * 128:(hd + 1) * 128].rearrange("(k p) e -> p k e", p=128)),
                            reads=[r_Vall[l]], writes=[r_vTp[part]], dma=r_vTp[part])
                P.emit("sp", lambda h, hd=hd: h.dma_start(out=kT[:, 4096:4352],
                                                        in_=Kc_loc[l].rearrange("p (h w) -> p h w", h=4)[:, hd, :]),
                       reads=[r_Kc[l]], writes=[r_kTp[2]], dma=r_kTp[2])
                P.emit("sp", lambda h, hd=hd: h.dma_start(
                    out=vT[:, 32:34, :], in_=Vc_loc[l][:, hd * 128:(hd + 1) * 128].rearrange("(k p) e -> p k e", p=128)),
                    reads=[r_Vc[l]], writes=[r_vTp[2]], dma=r_vTp[2])

                def pv_step(ki):
                    kt = kts[ki]
                    first, last = ki == 0, ki == len(kts) - 1
                    for m in range(2):
                        pj = (ki % 2) * 2 + m
                        mm(4 + 2 * m, ps[4 + 2 * m][:, 0:n], vT[:, kt, :], pT[pj][:, 0:n], first, last, [r_vTp[min(kt // 16, 2)], r_pT[pj]])
                        mm(5 + 2 * m, ps[5 + 2 * m][:, 0:n], ones_bf[:], pT[pj][:, 0:n], first, last, [r_const, r_pT[pj]])

                for ki, kt in enumerate(kts):
                    pb = pairA()
                    for m in range(2):
                        qsrc = q0T if m == 0 else q1T
                        mm(pb + m, ps[pb + m][:, 0:n], kT[:, kt * 128:(kt + 1) * 128], qsrc[:, hd, 0:n], True, True,
                           [r_kTp[min(kt // 16, 2)], r_q])
                    pj0 = (ki % 2) * 2
                    if n == 512:
                        P.emit("act", lambda h, pb=pb, pj0=pj0: h.activation(
                            out=pT2[pj0 // 2], in_=ps_all[:, pb * 512:(pb + 2) * 512], func=AF.Exp, bias=0.0, scale=0.125),
                            reads=[r_ps[pb], r_ps[pb + 1]], writes=[r_pT[pj0], r_pT[pj0 + 1]])
                    else:
                        for m in range(2):
                            act(pT[pj0 + m][:, 0:n], ps[pb + m][:, 0:n], AF.Exp, [r_ps[pb + m]], [r_pT[pj0 + m]], scale=0.125)
                    if ki > 0:
                        pv_step(ki - 1)
                    if ki == min(6, len(kts) - 1) and pending:
                        pending.pop(0)()
                    fn_ = hooks.pop((hd, ki), None)
                    if fn_ is not None:
                        active.append(fn_())
                    for g_ in list(active):
                        try:
                            next(g_)
                        except StopIteration:
                            active.remove(g_)
                pv_step(len(kts) - 1)
                ev = [ntmp2(), ntmp2()]
                if n == 512:
                    P.emit("dve", lambda h, ev=ev: h.tensor_copy(out=tmp_all[:, ev[0] * 512:(ev[0] + 2) * 512],
                                                                in_=ps_all[:, 4 * 512:6 * 512]),
                           reads=[r_ps[4], r_ps[5]], writes=[r_tmp[ev[0]], r_tmp[ev[0] + 1]])
                    P.emit("act", lambda h, ev=ev: h.copy(out=tmp_all[:, ev[1] * 512:(ev[1] + 2) * 512],
                                                         in_=ps_all[:, 6 * 512:8 * 512]),
                           reads=[r_ps[6], r_ps[7]], writes=[r_tmp[ev[1]], r_tmp[ev[1] + 1]])
                else:
                    for j_ in range(2):
                        cp(tmps[ev[0] + j_][:, 0:n], ps[4 + j_][:, 0:n], [r_ps[4 + j_]], [r_tmp[ev[0] + j_]])
                        cp(tmps[ev[1] + j_][:, 0:n], ps[6 + j_][:, 0:n], [r_ps[6 + j_]], [r_tmp[ev[1] + j_]], eng="act")
                o0, z0, o1, z1 = ev[0], ev[0] + 1, ev[1], ev[1] + 1
                recip(tmps[z0][:, 0:n], tmps[z0][:, 0:n], [r_tmp[z0]], [r_tmp[z0]])
                tt(tmps[o0][:, 0:n], tmps[o0][:, 0:n], tmps[z0][:, 0:n], ALU.mult, [r_tmp[o0], r_tmp[z0]], [r_tmp[o0]])
                recip(tmps[z1][:, 0:n], tmps[z1][:, 0:n], [r_tmp[z1]], [r_tmp[z1]])
                tt(tmps[o1][:, 0:n], tmps[o1][:, 0:n], tmps[z1][:, 0:n], ALU.mult, [r_tmp[o1], r_tmp[z1]], [r_tmp[o1]])
                stt(tmps[o0][:, 0:n], tmps[o1][:, 0:n], lamT[:, l, 0:1], tmps[o0][:, 0:n], ALU.mult, ALU.add,
                    [r_tmp[o1], r_tmp[o0], r_small], [r_tmp[o0]])

                def finish(hd=hd, o0=o0, rz2=z0):
                    act(sqs[:, 0:n], tmps[o0][:, 0:n], AF.Square, [r_tmp[o0]], [r_sqs])
                    pi = pairA()
                    mm(pi, ps[pi][:, 0:n], ones_bf[:], sqs[:, 0:n], True, True, [r_sqs, r_const])
                    act(tmps[rz2][:, 0:n], ps[pi][:, 0:n], AF.Sqrt, [r_ps[pi]], [r_tmp[rz2]], bias=EPS, scale=1.0 / 128)
                    recip(tmps[rz2][:, 0:n], tmps[rz2][:, 0:n], [r_tmp[rz2]], [r_tmp[rz2]])
                    stt(ocT[:, hd, 0:n], tmps[o0][:, 0:n], lamT[:, l, 1:2], tmps[rz2][:, 0:n], ALU.mult, ALU.mult,
                        [r_tmp[o0], r_tmp[rz2], r_small], [r_oc])

                pending.append(finish)
                if l == 0 and latent:
                    ada_some(2 if (t == 0 and hd < 2) else 1)
            if hooks or active:
                while pending:
                    pending.pop(0)()
                for g_ in active:
                    for _ in g_:
                        pass
                for key_ in sorted(hooks):
                    for _ in hooks[key_]():
                        pass
            return pending

        def load_window(t, l):
            ti, t0, n = TILES[t]
            memset(win[:], 0.0, [r_win], eng="pool")
            if ti == 4:
                src = CVc_loc[l].rearrange("p (c w) -> p c w", c=4)
                P.emit("sp", lambda h: h.dma_start(out=win[:, :, 16:16 + n], in_=src), reads=[r_CVc[l]], writes=[r_win], dma=r_win)
                return
            src = CV_loc[l].rearrange("p (c w) -> p c w", c=4)
            lo = max(t0 - 16, 0)
            hi = min(t0 + n + 16, NT)
            fns = [lambda h: h.dma_start(out=win[:, :, 16 + (lo - t0):16 + (hi - t0)], in_=src[:, :, lo:hi])]
            hall = H_all[l].rearrange("q (c w) -> q c w", c=4)
            if ti == 0:
                fns.append(lambda h: h.dma_start(out=win[:, :, 0:16], in_=hall[0:128, :, 16:32]))
            if ti == 3:
                fns.append(lambda h: h.dma_start(out=win[:, :, 16 + n:32 + n], in_=hall[128:256, :, 0:16]))
            P.emit("sp", None, reads=[r_CV[l], r_Hall[l]], writes=[r_win], dma=r_win, multi=fns)
            if ti == 0:
                ts(win[:, :, 0:16], win[:, :, 0:16], pv("mleft"), None, ALU.mult, None, [r_win, r_pvec], [r_win])
            if ti == 3:
                ts(win[:, :, 16 + n:32 + n], win[:, :, 16 + n:32 + n], pv("mright"), None, ALU.mult, None,
                   [r_win, r_pvec], [r_win])

        def mixer_b(t, l, pre=None, skip_norm=False):
            ti, t0, n = TILES[t]
            latent = ti < 4
            col = 0 if latent else 1
            if not skip_norm:
                modnorm(t, gsT[:, l, 1, :, col], modT[:, l, 24:32, col], mixer=True)
            if latent:
                P.emit("sp", None, writes=[r_rope], dma=r_rope, multi=[
                    lambda h: h.dma_start(out=ropec[:], in_=cos_d[:, t0:t0 + 512]),
                    lambda h: h.dma_start(out=ropes[:], in_=sin_d[:, t0:t0 + 512])])
            si, wv = load_slot(w_in_v[l][:, :, Q_OFF:Q_OFF + 512], v512)
            memset(q0T[:, :, :], 0.0, [r_q], eng="pool")
            memset(q1T[:, :, :], 0.0, [r_q], eng="pool")
            for hd in range(4):
                pi = proj(si, wv, hd, n)
                if latent:
                    a, b = rope_to(None, pi, n, r_q)
                    tt(q0T[0:64, hd, 0:n], tmps[a][0:64, 0:n], tmps[b][0:64, 0:n], ALU.add, [r_tmp[a], r_tmp[b]], [r_q])
                    tt(q1T[64:128, hd, 0:n], tmps[a][64:128, 0:n], tmps[b][64:128, 0:n], ALU.add, [r_tmp[a], r_tmp[b]], [r_q])
                else:
                    cp(q0T[0:64, hd, 0:n], ps[pi][0:64, 0:n], [r_ps[pi]], [r_q])
                    cp(q1T[64:128, hd, 0:n], ps[pi][64:128, 0:n], [r_ps[pi]], [r_q])
            if t == 0 and l == 0:
                for hd in range(4):
                    tap(hd, q0T[:, hd, :], r_q)
                    tap(4 + hd, q1T[:, hd, :], r_q)
                tap(32, hT[:, 0, :], r_hTc[0][0])
            load_window(t, l)
            si, wv = load_slot(w_in_v[l][:, :, 0:256], v256)
            cao = PV["conv_a_w"][0] + l * 6
            for k2 in range(2):
                pb = proj(si, wv, k2, n)
                a = ntmp()
                ts(tmps[a][:, 0:n], win[:, k2, 15:15 + n], pvec[:, cao + k2:cao + k2 + 1], None, ALU.mult, None,
                   [r_win, r_pvec], [r_tmp[a]])
                for j in (1, 2):
                    stt(tmps[a][:, 0:n], win[:, k2, 15 + j:15 + j + n], pvec[:, cao + 2 * j + k2:cao + 2 * j + k2 + 1],
                        tmps[a][:, 0:n], ALU.mult, ALU.add, [r_win, r_pvec, r_tmp[a]], [r_tmp[a]])
                tt(aT[:, k2, 0:n], tmps[a][:, 0:n], ps[pb][:, 0:n], ALU.mult, [r_tmp[a], r_ps[pb]], [r_a])
            cdo = PV["conf_dw"][0] + l * 62
            for k2 in range(2):
                eng = "dve"
                a = ntmp()
                dbo = PV["conf_db"][0] + l * 2 + k2
                ts(tmps[a][:, 0:n], win[:, 2 + k2, 1:1 + n], pvec[:, cdo + k2:cdo + k2 + 1], pvec[:, dbo:dbo + 1],
                   ALU.mult, ALU.add, [r_win, r_pvec], [r_tmp[a]], eng=eng)
                for j in range(1, 31):
                    stt(tmps[a][:, 0:n], win[:, 2 + k2, 1 + j:1 + j + n], pvec[:, cdo + 2 * j + k2:cdo + 2 * j + k2 + 1],
                        tmps[a][:, 0:n], ALU.mult, ALU.add, [r_win, r_pvec, r_tmp[a]], [r_tmp[a]], eng=eng)
                cp(dT[:, k2, 0:n], tmps[a][:, 0:n], [r_tmp[a]], [r_d])
            P.fence([r_sq8] + r_vnm + r_csq)
            ntc = n // 128
            nb = (ntc + 1) // 2
            vnm4 = [merged[:, tc, :].rearrange("p (g c) -> p g c", g=4) for tc in range(4)]
            csq = [merged[:, 4, :], merged[:, 5, :]]
            hk = {}

            def gelu_steps(src, w_, out_ap, out_res, src_res, g):
                tt(tmps[g][:, 0:w_], src, src, ALU.mult, [src_res], [r_tmp[g]])
                ts(tmps[g][:, 0:w_], tmps[g][:, 0:w_], 0.044715, 1.0, ALU.mult, ALU.add, [r_tmp[g]], [r_tmp[g]])
                tt(tmps[g][:, 0:w_], tmps[g][:, 0:w_], src, ALU.mult, [r_tmp[g], src_res], [r_tmp[g]])
                yield
                act(tmps[g][:, 0:w_], tmps[g][:, 0:w_], AF.Sigmoid, [r_tmp[g]], [r_tmp[g]], scale=1.5957691216057308)
                yield
                tt(out_ap, tmps[g][:, 0:w_], src, ALU.mult, [r_tmp[g], src_res], [out_res])

            def sgu_proj():
                si, wv = load_slot(w_in_v[l][:, :, B_OFF:B_OFF + 512], v512)
                pu = pairA()
                for k2 in range(2):
                    for c in range(DC):
                        mm(pu + k2, ps[pu + k2][:, 0:n], wv[:, c, k2 * 128:(k2 + 1) * 128], hT[:, c, 0:n], c == 0, c == DC - 1,
                           [r_slot[si], r_hTc[0][c]])
                pv_ = pairA()
                for bk in range(nb):
                    ntcb = min(2, ntc - 2 * bk)
                    for tcl in range(ntcb):
                        tc = 2 * bk + tcl
                        for c in range(DC):
                            mm(pv_ + bk, ps[pv_ + bk][:, tcl * 256:(tcl + 1) * 256], hT[:, c, tc * 128:(tc + 1) * 128],
                               wv[:, c, 256:512], c == 0, c == DC - 1, [r_slot[si], r_hTc[0][c]])
                memset(merged[:, 0:4, :], 0.0, r_vnm, eng="pool")
                cu = [ntmp(), ntmp()]
                for k2 in range(2):
                    cp(tmps[cu[k2]][:, 0:n], ps[pu + k2][:, 0:n], [r_ps[pu + k2]], [r_tmp[cu[k2]]])
                vtb = []
                for bk in range(nb):
                    ntcb = min(2, ntc - 2 * bk)
                    vb = ntmp()
                    vtb.append(vb)
                    cp(tmps[vb][:, 0:ntcb * 256], ps[pv_ + bk][:, 0:ntcb * 256], [r_ps[pv_ + bk]], [r_tmp[vb]])
                g = ntmp()
                sqs_ = [ntmp(), ntmp()]
                yield
                for k2 in range(2):
                    yield from gelu_steps(tmps[cu[k2]][:, 0:n], n, bT[:, k2, 0:n], r_b, r_tmp[cu[k2]], g)
                for bk in range(nb):
                    ntcb = min(2, ntc - 2 * bk)
                    vb = vtb[bk]
                    yield from gelu_steps(tmps[vb][:, 0:ntcb * 256], ntcb * 256, tmps[vb][:, 0:ntcb * 256], r_tmp[vb],
                                          r_tmp[vb], g)
                    P.emit("dve", lambda h, vb=vb, bk=bk, ntcb=ntcb: h.reduce_sum(
                        out=mv8[:, 2 * bk:2 * bk + ntcb], in_=tmps[vb][:, 0:ntcb * 256].rearrange("p (a b) -> p a b", a=ntcb),
                        axis=mybir.AxisListType.X), reads=[r_tmp[vb]], writes=[r_vt])
                ts(mv8[:, 0:4], mv8[:, 0:4], -1.0 / 256, None, ALU.mult, None, [r_vt], [r_vt])
                for bk in range(nb):
                    vb = vtb[bk]
                    ntcb = min(2, ntc - 2 * bk)
                    for tcl in range(ntcb):
                        tc = 2 * bk + tcl
                        ts(tmps[vb][:, tcl * 256:(tcl + 1) * 256], tmps[vb][:, tcl * 256:(tcl + 1) * 256], mv8[:, tc:tc + 1],
                           None, ALU.add, None, [r_tmp[vb], r_vt], [r_tmp[vb]])
                    sq_ = sqs_[bk]
                    tt(tmps[sq_][:, 0:ntcb * 256], tmps[vb][:, 0:ntcb * 256], tmps[vb][:, 0:ntcb * 256], ALU.mult,
                       [r_tmp[vb]], [r_tmp[sq_]])
                    P.emit("dve", lambda h, sq_=sq_, bk=bk, ntcb=ntcb: h.reduce_sum(
                        out=mv8[:, 4 + 2 * bk:4 + 2 * bk + ntcb],
                        in_=tmps[sq_][:, 0:ntcb * 256].rearrange("p (a b) -> p a b", a=ntcb),
                        axis=mybir.AxisListType.X), reads=[r_tmp[sq_]], writes=[r_vt])
                yield
                yield
                act(mv8[:, 4:8], mv8[:, 4:8], AF.Sqrt, [r_vt], [r_vt], bias=EPS, scale=1.0 / 256)
                yield
                recip(mv8[:, 4:8], mv8[:, 4:8], [r_vt], [r_vt])
                for tc in range(ntc):
                    vb = vtb[tc // 2]
                    tcl = tc % 2
                    for g_ in range(4):
                        gg = g_ % 2
                        ts(vnm4[tc][:, g_, gg * 64:(gg + 1) * 64], tmps[vb][:, tcl * 256 + g_ * 64:tcl * 256 + (g_ + 1) * 64],
                           mv8[:, 4 + tc:5 + tc], None, ALU.mult, None, [r_tmp[vb], r_vt], [r_vnm[tc]])

            def sgu_spatial():
                pS = pairA()
                for tc in range(ntc):
                    for k2 in range(2):
                        for gg in range(2):
                            g_ = 2 * k2 + gg
                            mm(pS + k2, ps[pS + k2][:, tc * 128:(tc + 1) * 128], vnm4[tc][:, g_, :], sguw[:, l, g_, :],
                               gg == 0, gg == 1, [r_vnm[tc], r_const])
                for k2 in range(2):
                    a = ntmp()
                    tt(tmps[a][:, 0:n], ps[pS + k2][:, 0:n], bias_rep[:, k2, 0:n], ALU.add, [r_ps[pS + k2], r_small],
                       [r_tmp[a]])
                    tt(bT[:, k2, 0:n], tmps[a][:, 0:n], bT[:, k2, 0:n], ALU.mult, [r_tmp[a], r_b], [r_b])
                yield

            def conf_ln():
                hacc = [ntmp(), ntmp()]
                rz = ntmp()
                pm = pairA()
                for k2 in range(2):
                    mm(pm, ps[pm][:, 0:n], ones_bf[:], dT[:, k2, 0:n], k2 == 0, k2 == 1, [r_d, r_const])
                for k2 in range(2):
                    a = hacc[k2]
                    stt(tmps[a][:, 0:n], ps[pm][:, 0:n], -1.0 / 256, dT[:, k2, 0:n], ALU.mult, ALU.add,
                        [r_ps[pm], r_d], [r_tmp[a]])
                    tt(csq[k2][:, 0:n], tmps[a][:, 0:n], tmps[a][:, 0:n], ALU.mult, [r_tmp[a]], [r_csq[k2]])
                yield
                yield
                yield
                pvv = pairA()
                for k2 in range(2):
                    mm(pvv, ps[pvv][:, 0:n], ones_bf[:], csq[k2][:, 0:n], k2 == 0, k2 == 1, [r_csq[k2], r_const])
                yield
                act(tmps[rz][:, 0:n], ps[pvv][:, 0:n], AF.Sqrt, [r_ps[pvv]], [r_tmp[rz]], bias=EPS, scale=1.0 / 256)
                recip(tmps[rz][:, 0:n], tmps[rz][:, 0:n], [r_tmp[rz]], [r_tmp[rz]])
                for k2 in range(2):
                    a = hacc[k2]
                    go = PV["conf_ln_g"][0] + l * 2 + k2
                    stt(tmps[a][:, 0:n], tmps[a][:, 0:n], pvec[:, go:go + 1], tmps[rz][:, 0:n], ALU.mult, ALU.mult,
                        [r_tmp[a], r_tmp[rz], r_pvec], [r_tmp[a]])
                yield
                yield
                for k2 in range(2):
                    a = hacc[k2]
                    bo = PV["conf_ln_b"][0] + l * 2 + k2
                    act(dT[:, k2, 0:n], tmps[a][:, 0:n], AF.Silu, [r_tmp[a], r_pvec], [r_d], bias=pvec[:, bo:bo + 1],
                        scale=1.0)

            hk[(1, 12)] = conf_ln
            hk[(2, 7)] = sgu_proj
            hk[(3, 8)] = sgu_spatial
            late = attention(t, l, hk)
            if t == 0 and l == 0:
                for hd in range(4):
                    tap(8 + hd, ocT[:, hd, :], r_oc)
            P.fence([r_sq8] + r_vnm + r_csq)
            if t == 0 and l == 0:
                for k2 in range(2):
                    tap(12 + k2, aT[:, k2, :], r_a)
                    tap(14 + k2, bT[:, k2, :], r_b)
                    tap(16 + k2, dT[:, k2, :], r_d)
                for c4 in range(4):
                    tap(28 + c4, win[:, c4, 0:512], r_win)
            wa = w_a_d[l].rearrange("(c p) d -> p c d", p=128)
            wb = w_b_d[l].rearrange("(c p) d -> p c d", p=128)
            wc = w_c_d[l].rearrange("(c p) d -> p c d", p=128)
            wd = w_d_d[l].rearrange("(c p) d -> p c d", p=128)
            gview = w_in_v[l][:, :, G_OFF:G_OFF + 4096].rearrange("p c (i d) -> p c i d", i=4)
            branches = [(aT, r_a, 2), (bT, r_b, 2), (ocT, r_oc, 4), (dT, r_d, 2)]
            for dc in range(DC):
                i = next_slot()
                gdst = slots[i][:, 0:4096].rearrange("p (c i f) -> p c i f", c=DC, i=4)
                P.emit("pool", None, writes=[r_slot[i]], dma=r_slot[i], multi=[
                    (lambda h, dc=dc, gdst=gdst, bi=bi: h.dma_start(out=gdst[:, :, bi, :],
                                                                   in_=gview[:, :, bi, dc * 128:(dc + 1) * 128]))
                    for bi in range(4)])
                j = next_slot()
                bdst = slots[j][:, 0:1280].rearrange("p (c f) -> p c f", c=10)
                P.emit("pool", None, writes=[r_slot[j]], dma=r_slot[j], multi=[
                    lambda h, dc=dc, bdst=bdst: h.dma_start(out=bdst[:, 0:2, :], in_=wa[:, :, dc * 128:(dc + 1) * 128]),
                    lambda h, dc=dc, bdst=bdst: h.dma_start(out=bdst[:, 2:4, :], in_=wb[:, :, dc * 128:(dc + 1) * 128]),
                    lambda h, dc=dc, bdst=bdst: h.dma_start(out=bdst[:, 4:8, :], in_=wc[:, :, dc * 128:(dc + 1) * 128]),
                    lambda h, dc=dc, bdst=bdst: h.dma_start(out=bdst[:, 8:10, :], in_=wd[:, :, dc * 128:(dc + 1) * 128])])
                koff = [0, 2, 4, 8]
                acc = ntmp()
                for bi, (bt_, br_, nk) in enumerate(branches):
                    if bi == 2:
                        while late:
                            late.pop(0)()
                    pg = psA()
                    for c in range(DC):
                        mm(pg, ps[pg][:, 0:n], gdst[:, c, bi, :], hT[:, c, 0:n], c == 0, c == DC - 1, [r_slot[i], r_hTc[0][c]])
                    act(sgs[bi][:, 0:n], ps[pg][:, 0:n], AF.Sigmoid, [r_ps[pg]], [r_sgs[bi]])
                    py = psB()
                    for kc in range(nk):
                        mm(py, ps[py][:, 0:n], bdst[:, koff[bi] + kc, :], bt_[:, kc, 0:n], kc == 0, kc == nk - 1,
                           [r_slot[j], br_])
                    if bi == 0:
                        tt(tmps[acc][:, 0:n], ps[py][:, 0:n], sgs[bi][:, 0:n], ALU.mult, [r_ps[py], r_sgs[bi]], [r_tmp[acc]])
                    else:
                        b2 = ntmp()
                        tt(tmps[b2][:, 0:n], ps[py][:, 0:n], sgs[bi][:, 0:n], ALU.mult, [r_ps[py], r_sgs[bi]], [r_tmp[b2]])
                        if bi < 3:
                            tt(tmps[acc][:, 0:n], tmps[acc][:, 0:n], tmps[b2][:, 0:n], ALU.add, [r_tmp[acc], r_tmp[b2]],
                               [r_tmp[acc]])
                        else:
                            tt(merged[:, dc, 0:n], tmps[acc][:, 0:n], tmps[b2][:, 0:n], ALU.add, [r_tmp[acc], r_tmp[b2]],
                               [r_sq8])
            if t == 0 and l == 0:
                for dc in range(DC):
                    tap(18 + dc, merged[:, dc, :], r_sq8)
            if pre is not None:
                P.fence(r_hT_all)
                pre()
            xv = xview(t)
            wo = w_o_d[l].rearrange("(c p) d -> p c d", p=128)
            for half in range(2):
                si, wv = load_slot(wo[:, :, half * 512:(half + 1) * 512], v512)
                for m in range(4):
                    dc = half * 4 + m
                    pi = psB()
                    for c in range(DC):
                        mm(pi, ps[pi][:, 0:n], wv[:, c, m * 128:(m + 1) * 128], merged[:, c, 0:n], c == 0, c == DC - 1,
                           [r_slot[si], r_sq8])
                    stt(xv[:, dc, :], ps[pi][:, 0:n], modT[:, l, 40 + dc:41 + dc, col], xv[:, dc, :], ALU.mult, ALU.add,
                        [r_ps[pi], r_small, r_x[t]], [r_x[t]])

        def final_norm(t):
            ti, t0, n = TILES[t]
            xv = xview(t)
            for c in range(DC):
                act(sq8[:, c, 0:n], xv[:, c, :], AF.Square, [r_x[t]], [r_sq8])
            pi = ps_any()
            for c in range(DC):
                mm(pi, ps[pi][:, 0:n], ones_bf[:], sq8[:, c, 0:n], c == 0, c == DC - 1, [r_sq8, r_const])
            ri = ntmp()
            act(tmps[ri][:, 0:n], ps[pi][:, 0:n], AF.Sqrt, [r_ps[pi]], [r_tmp[ri]], bias=EPS, scale=1.0 / D)
            recip(tmps[ri][:, 0:n], tmps[ri][:, 0:n], [r_tmp[ri]], [r_tmp[ri]])
            for c in range(DC):
                stt(xv[:, c, :], xv[:, c, :], fgT[:, c:c + 1], tmps[ri][:, 0:n], ALU.mult, ALU.mult,
                    [r_x[t], r_small, r_tmp[ri]], [r_x[t]])
            P.emit("sp", lambda h, t0=t0: h.dma_start(out=yT_d.rearrange("(c p) t -> p c t", p=128)[:, :, t0:t0 + 512],
                                                    in_=xT[:, :, t0:t0 + 512]),
                   reads=[r_x[t]], writes=[r_out], dma=r_out)

        stage = {"n": 0}
        big_res = r_kTp + r_vTp + [r_win, r_q, r_oc, r_a, r_b, r_d, r_kst, r_vst, r_cvst] + r_pT + r_gT

        def go():
            stage["n"] += 1
            P.fence(big_res)
            P.fence(r_hT_all)
            return stage["n"] <= DEBUG_STOP

        for l in range(L):
            last = l == L - 1
            so = PV["sgu_b"][0] + l * 256
            for k2 in range(2):
                for r4 in range(4):
                    cp(bias_rep[:, k2, r4 * 128:(r4 + 1) * 128], pvec[:, so + k2 * 128:so + (k2 + 1) * 128],
                       [r_pvec], [r_small])
            for grp_ in ([4, 0, 1], [2, 3]):
                if go():
                    ffn(grp_, l, 0)
                for t in grp_:
                    if t != 4 and go():
                        mixer_a(t, l)
            if go():
                exchange(l)
            if go():
                mixer_a(4, l)
            for grp_ in (([0, 1], [2, 3]) if last else ([4, 0, 1], [2, 3])):
                for i_, t in enumerate(grp_):
                    if go():
                        if i_ + 1 < len(grp_):
                            t2 = grp_[i_ + 1]
                            col2 = 0 if t2 < 4 else 1
                            pre_ = (lambda t2=t2, col2=col2: modnorm(t2, gsT[:, l, 1, :, col2], modT[:, l, 24:32, col2],
                                                                      mixer=True))
                        else:
                            pre_ = (lambda grp_=grp_: ffn_prenorm(grp_, l, 1, len(grp_) - 1))
                        mixer_b(t, l, pre=pre_, skip_norm=(i_ > 0))
                if go():
                    ffn(grp_, l, 1, prenormed=len(grp_) - 1)
                if last:
                    for t in grp_:
                        final_norm(t)
        while ada_todo:
            ada_some(1)

        if DEBUG_STOP < 10 ** 8:
            for t in range(4):
                final_norm(t)
        P.emit("sp", lambda h: h.nop(), reads=[r_out, r_dbg])
        P.finalize()
    return nc


def _rope_tables(half):
    nf = 16
    t = np.arange(NT, dtype=np.float32) + np.float32(half * NT)
    row = np.floor(t / 64).astype(np.float32)
    colp = (t - row * 64).astype(np.float32)
    inv = (np.float32(10000.0) ** (-np.arange(nf, dtype=np.float32) / np.float32(nf))).astype(np.float32)
    cosT = np.zeros((128, NT), np.float32)
    sinT = np.zeros((128, NT), np.float32)
    for p in range(128):
        r = p % 64
        axis, hf, f = r // 32, (r % 32) // 16, r % 16
        pos = row if axis == 0 else colp
        ang = (pos * inv[f]).astype(np.float32)
        cosT[p] = np.cos(ang)
        sinT[p] = np.sin(ang) * (-1.0 if hf == 0 else 1.0)
    return cosT, sinT


def _rot_matrix():
    R = np.zeros((128, 128), np.float32)
    for p in range(128):
        hf = (p % 32) // 16
        partner = p + 16 if hf == 0 else p - 16
        R[partner, p] = 1.0
    return R


def _pack_pvec(inp, half):
    pv = np.zeros((128, NPV), np.float32)

    def put(name, arr):
        o, w = PV[name]
        arr = np.asarray(arr, np.float32)
        assert arr.shape == (128, w), (name, arr.shape, w)
        pv[:, o:o + w] = arr

    def fm(v):
        v = np.asarray(v, np.float32)
        lead = v.shape[:-1]
        n = v.shape[-1] // 128
        a = v.reshape(lead + (n, 128))
        a = np.moveaxis(a, -1, 0)
        return a.reshape(128, -1)

    put("b_ada", fm(inp["b_ada"]))
    put("norm_g", fm(inp["norm_g"]))
    put("final_g", fm(inp["final_g"]))
    put("conv_a_w", fm(inp["conv_a_w"]))
    put("conf_dw", fm(inp["conf_dw"]))
    put("conf_db", fm(inp["conf_db"]))
    put("conf_ln_g", fm(inp["conf_ln_g"]))
    put("conf_ln_b", fm(inp["conf_ln_b"]))
    put("subln_g", fm(inp["subln_g"]))
    put("lam_p", np.broadcast_to(np.asarray(inp["lam_p"], np.float32).reshape(1, L * 256), (128, L * 256)))
    sb_ = np.asarray(inp["sgu_b"], np.float32)
    rep = np.zeros((128, L, 2, 128), np.float32)
    for l in range(L):
        for g in range(4):
            rep[(g % 2) * 64:(g % 2) * 64 + 64, l, g // 2, :] = sb_[l, g][None, :]
    put("sgu_b", rep.reshape(128, -1))
    put("mleft", np.full((128, 1), 1.0 if half == 1 else 0.0, np.float32))
    put("mright", np.full((128, 1), 1.0 if half == 0 else 0.0, np.float32))
    return pv


_NC_CACHE = {}


def kernel(**inp):
    if "nc" not in _NC_CACHE:
        _NC_CACHE["nc"] = build_program()
    nc = _NC_CACHE["nc"]
    x = np.asarray(inp["x"], np.float32)
    ctx = np.asarray(inp["ctx"], np.float32)
    c = np.asarray(inp["c"], np.float32)
    c_ctx = np.asarray(inp["c_ctx"], np.float32)
    rot = _rot_matrix()
    sgu_wT = np.ascontiguousarray(np.swapaxes(np.asarray(inp["sgu_w"], np.float32), 2, 3))
    shared = {k: np.ascontiguousarray(np.asarray(inp[k], np.float32)) for k in
              ("w_ada", "ffn1_w1", "ffn1_w3", "ffn1_w2", "ffn2_w1", "ffn2_w3", "ffn2_w2", "w_in",
               "w_a_out", "w_b_out", "w_c_out", "w_d_out", "w_o")}
    tabs = [_rope_tables(0), _rope_tables(1)]
    in_maps = []
    for core in range(8):
        b, half = core // 2, core % 2
        m = dict(shared)
        m["xT"] = np.ascontiguousarray(x[b, half * NT:(half + 1) * NT, :].T)
        m["cT"] = np.ascontiguousarray(ctx[b].T)
        m["cvec"] = np.ascontiguousarray(np.stack([c[b], c_ctx], axis=1))
        m["pvec"] = _pack_pvec(inp, half)
        m["rot"] = rot
        m["cosT"], m["sinT"] = tabs[half]
        m["sgu_wT"] = sgu_wT
        in_maps.append(m)
    res = run_bass_kernel_spmd(nc, in_maps, core_ids=list(range(8)))
    if DEBUG_TAPS:
        _NC_CACHE["dbg"] = [np.asarray(res.results[core]["dbg"]) for core in range(8)]
    out = np.empty((4, 2 * NT, D), np.float32)
    for core in range(8):
        b, half = core // 2, core % 2
        out[b, half * NT:(half + 1) * NT, :] = np.asarray(res.results[core]["yT"]).T
    return out
```
